# Optimizing a Trainium2 kernel written in Bass

```python
import math
import jax, jax.numpy as jnp
from jax import lax
import numpy as np

D_MODEL = 1024
BATCH = 4
SEQ = 8192
DEPTH = 1

HEAD_DIM = 64
D_MIX = D_MODEL
SWA_WIDTH = D_MIX // 2
MOBA_WIDTH = D_MIX - SWA_WIDTH
SWA_Q_HEADS = SWA_WIDTH // HEAD_DIM
SWA_KV_HEADS = 2
SWA_KV_WIDTH = SWA_KV_HEADS * HEAD_DIM
MOBA_HEADS = MOBA_WIDTH // HEAD_DIM
N_HEADS_TOTAL = SWA_Q_HEADS + MOBA_HEADS
WINDOW = 128
SWA_BLOCK = 128
MOBA_BLOCK = 256
MOBA_TOPK = 3
MOBA_QCHUNK = 32
NUM_BUCKETS = 32
MAX_DISTANCE = 1024
PLE_DIM = 256
RMS_EPS = 1e-6
IN_SPLITS = (SWA_WIDTH, SWA_KV_WIDTH, SWA_KV_WIDTH, SWA_WIDTH,
             MOBA_WIDTH, MOBA_WIDTH, MOBA_WIDTH, MOBA_WIDTH)
IN_COLS = sum(IN_SPLITS)

kernel_name = "hybrid_swa_sink_moba_t5bias_ple"


def rms_norm(x, g):
    xf = x.astype(jnp.float32)
    r = lax.rsqrt(jnp.mean(xf * xf, axis=-1, keepdims=True) + RMS_EPS)
    return (xf * r).astype(x.dtype) * g


def rel_bucket(dist):
    n = jnp.maximum(dist, 0)
    max_exact = NUM_BUCKETS // 2
    nf = jnp.maximum(n, 1).astype(jnp.float32)
    large = max_exact + (jnp.log(nf / max_exact) / math.log(MAX_DISTANCE / max_exact)
                         * (NUM_BUCKETS - max_exact)).astype(jnp.int32)
    large = jnp.minimum(large, NUM_BUCKETS - 1)
    return jnp.where(n < max_exact, n, large)


def swa_attention(q, k, v, sinks, bias_table):
    B, S, Hq, Dh = q.shape
    Hkv = k.shape[2]
    G = Hq // Hkv
    blk = SWA_BLOCK
    nb = S // blk
    qb = q.reshape(B, nb, blk, Hkv, G, Dh)
    pad = ((0, 0), (blk, 0), (0, 0), (0, 0))
    kb = jnp.pad(k, pad).reshape(B, nb + 1, blk, Hkv, Dh)
    vb = jnp.pad(v, pad).reshape(B, nb + 1, blk, Hkv, Dh)
    kband = jnp.concatenate([kb[:, :-1], kb[:, 1:]], axis=2)
    vband = jnp.concatenate([vb[:, :-1], vb[:, 1:]], axis=2)
    logits = jnp.einsum('bnqhgd,bnkhd->bnhgqk', qb, kband).astype(jnp.float32) * (Dh ** -0.5)
    qi = jnp.arange(blk, dtype=jnp.int32)[:, None]
    kj = jnp.arange(2 * blk, dtype=jnp.int32)[None, :]
    dist = qi + blk - kj
    bias = bias_table[rel_bucket(dist)].astype(jnp.float32)
    bias = bias.reshape(blk, 2 * blk, Hkv, G).transpose(2, 3, 0, 1)
    kpos = jnp.arange(nb, dtype=jnp.int32)[:, None] * blk - blk + kj
    valid = ((dist >= 0) & (dist < WINDOW))[None] & (kpos >= 0)[:, None, :]
    logits = jnp.where(valid[None, :, None, None], logits + bias, -jnp.inf)
    sink = sinks.astype(jnp.float32).reshape(Hkv, G)[None, None, :, :, None, None]
    m = jnp.maximum(jnp.max(logits, axis=-1, keepdims=True), sink)
    e = jnp.exp(logits - m)
    probs = e / (jnp.sum(e, axis=-1, keepdims=True) + jnp.exp(sink - m))
    out = jnp.einsum('bnhgqk,bnkhd->bnqhgd', probs.astype(v.dtype), vband)
    return out.reshape(B, S, Hq * Dh)


def moba_attention(q, k, v, bias_table):
    B, S, H, Dh = q.shape
    BS = MOBA_BLOCK
    nblk = -(-S // BS)
    Sp = nblk * BS
    pad = ((0, 0), (0, Sp - S), (0, 0), (0, 0))
    q, k, v = jnp.pad(q, pad), jnp.pad(k, pad), jnp.pad(v, pad)
    kblocks = k.reshape(B, nblk, BS, H, Dh)
    vblocks = v.reshape(B, nblk, BS, H, Dh)
    kmean = jnp.mean(kblocks.astype(jnp.float32), axis=2)
    scores = jnp.einsum('bshd,bnhd->bshn', q.astype(jnp.float32), kmean)
    qblk = jnp.arange(Sp, dtype=jnp.int32) // BS
    past = jnp.arange(nblk, dtype=jnp.int32)[None, :] < qblk[:, None]
    scores = jnp.where(past[None, :, None, :], scores, -jnp.inf)
    k_sel = max(1, min(MOBA_TOPK, nblk))
    _, idx = lax.top_k(scores, k_sel)
    valid = idx < qblk[None, :, None, None]

    kbh = kblocks.transpose(0, 3, 1, 2, 4)
    vbh = vblocks.transpose(0, 3, 1, 2, 4)
    QC = MOBA_QCHUNK
    nch = Sp // QC
    q_c = q.reshape(B, nch, QC, H, Dh).transpose(1, 0, 2, 3, 4)
    idx_c = idx.reshape(B, nch, QC, H, k_sel).transpose(1, 0, 2, 3, 4)
    valid_c = valid.reshape(B, nch, QC, H, k_sel).transpose(1, 0, 2, 3, 4)
    bi = jnp.arange(B)[:, None, None, None]
    hi = jnp.arange(H)[None, None, :, None]
    hi5 = jnp.arange(H)[None, None, :, None, None]
    key_off = jnp.arange(BS, dtype=jnp.int32)
    scale = Dh ** -0.5

    def chunk_fn(args):
        qc, idxc, validc, c = args
        ksel = kbh[bi, hi, idxc]
        vsel = vbh[bi, hi, idxc]
        qpos = c * QC + jnp.arange(QC, dtype=jnp.int32)
        logit_sel = jnp.einsum('bqhd,bqhskd->bqhsk', qc, ksel).astype(jnp.float32) * scale
        kpos_sel = idxc[..., None] * BS + key_off
        dist_sel = qpos[None, :, None, None, None] - kpos_sel
        bias_sel = bias_table[rel_bucket(dist_sel), hi5].astype(jnp.float32)
        logit_sel = jnp.where(validc[..., None], logit_sel + bias_sel, -jnp.inf)
        own = (c * QC) // BS
        kown = lax.dynamic_slice_in_dim(k, own * BS, BS, axis=1)
        vown = lax.dynamic_slice_in_dim(v, own * BS, BS, axis=1)
        logit_own = jnp.einsum('bqhd,bkhd->bqhk', qc, kown).astype(jnp.float32) * scale
        dist_own = qpos[:, None] - (own * BS + key_off)[None, :]
        bias_own = bias_table[rel_bucket(dist_own)].astype(jnp.float32).transpose(0, 2, 1)
        logit_own = jnp.where((dist_own >= 0)[None, :, None, :], logit_own + bias_own[None], -jnp.inf)
        logits = jnp.concatenate([logit_sel.reshape(B, QC, H, k_sel * BS), logit_own], axis=-1)
        probs = jax.nn.softmax(logits, axis=-1).astype(v.dtype)
        p_sel = probs[..., :k_sel * BS].reshape(B, QC, H, k_sel, BS)
        p_own = probs[..., k_sel * BS:]
        return (jnp.einsum('bqhsk,bqhskd->bqhd', p_sel, vsel)
                + jnp.einsum('bqhk,bkhd->bqhd', p_own, vown))

    out = lax.map(chunk_fn, (q_c, idx_c, valid_c, jnp.arange(nch, dtype=jnp.int32)))
    out = out.transpose(1, 0, 2, 3, 4).reshape(B, Sp, H * Dh)
    return out[:, :S]


def hybrid_layer(x, p_i, norm_g, w_in, sinks, rel_bias, w_out, ple_g, w_ple_gate, w_ple_proj):
    B, S, _ = x.shape
    h = rms_norm(x, norm_g)
    proj = h @ w_in
    offs = np.cumsum(IN_SPLITS)[:-1].tolist()
    a_q, a_k, a_v, a_g, b_q, b_k, b_v, b_g = jnp.split(proj, offs, axis=-1)
    a_out = swa_attention(a_q.reshape(B, S, SWA_Q_HEADS, HEAD_DIM),
                          a_k.reshape(B, S, SWA_KV_HEADS, HEAD_DIM),
                          a_v.reshape(B, S, SWA_KV_HEADS, HEAD_DIM),
                          sinks, rel_bias[:, :SWA_Q_HEADS])
    b_out = moba_attention(b_q.reshape(B, S, MOBA_HEADS, HEAD_DIM),
                           b_k.reshape(B, S, MOBA_HEADS, HEAD_DIM),
                           b_v.reshape(B, S, MOBA_HEADS, HEAD_DIM),
                           rel_bias[:, SWA_Q_HEADS:])
    mixed = jnp.concatenate([a_out * jax.nn.silu(a_g), b_out * jax.nn.silu(b_g)], axis=-1)
    x = x + mixed @ w_out
    gate = jax.nn.sigmoid(rms_norm(x, ple_g) @ w_ple_gate)
    return x + (p_i @ w_ple_proj) * gate


def setup_inputs(seed: int = 0) -> dict:
    key = jax.random.key(seed)
    ks = jax.random.split(key, 12)
    f32 = jnp.float32
    return {
        "x": jax.random.normal(ks[0], (BATCH, SEQ, D_MODEL), f32),
        "p": jax.random.normal(ks[1], (DEPTH, BATCH, SEQ, PLE_DIM), f32),
        "norm_in": 1.0 + 0.02 * jax.random.normal(ks[2], (DEPTH, D_MODEL), f32),
        "w_in": jax.random.normal(ks[3], (DEPTH, D_MODEL, IN_COLS), f32) * D_MODEL ** -0.5,
        "sinks": 0.5 * jax.random.normal(ks[4], (DEPTH, SWA_Q_HEADS), f32),
        "rel_bias": 0.5 * jax.random.normal(ks[5], (NUM_BUCKETS, N_HEADS_TOTAL), f32),
        "w_out": jax.random.normal(ks[6], (DEPTH, D_MIX, D_MODEL), f32) * D_MIX ** -0.5,
        "ple_norm": 1.0 + 0.02 * jax.random.normal(ks[7], (DEPTH, D_MODEL), f32),
        "w_ple_gate": jax.random.normal(ks[8], (DEPTH, D_MODEL, D_MODEL), f32) * D_MODEL ** -0.5,
        "w_ple_proj": jax.random.normal(ks[9], (DEPTH, PLE_DIM, D_MODEL), f32) * (0.5 * PLE_DIM ** -0.5),
        "final_norm": 1.0 + 0.02 * jax.random.normal(ks[10], (D_MODEL,), f32),
    }


def reference(x, p, norm_in, w_in, sinks, rel_bias, w_out, ple_norm, w_ple_gate, w_ple_proj, final_norm):
    for i in range(DEPTH):
        x = hybrid_layer(x, p[i], norm_in[i], w_in[i], sinks[i], rel_bias, w_out[i],
                         ple_norm[i], w_ple_gate[i], w_ple_proj[i])
    return rms_norm(x, final_norm)
```

```python
import numpy as np
import ml_dtypes
from contextlib import ExitStack
import concourse.bass as bass
import concourse.mybir as mybir
from concourse.bass_utils import run_bass_kernel_spmd

F32 = mybir.dt.float32
BF16 = mybir.dt.bfloat16
AF = mybir.ActivationFunctionType
ALU = mybir.AluOpType
AX = mybir.AxisListType

S = 8192
D = 1024
NPAIR = 16
OWN = 4096
INC = 3328
NEG = -30000.0
NEGBIG = -1.0e30
DEBUG = False
STOP_AFTER = 99


def build_nc():
    nc = bass.Bass("TRN2", target_bir_lowering=False)
    dbg_kind = "ExternalOutput" if DEBUG else "Internal"

    def din(name, shape, dt=F32):
        return nc.dram_tensor(name, list(shape), dt, kind="ExternalInput").ap()

    def dscr(name, shape, dt=BF16):
        return nc.dram_tensor(name, list(shape), dt, kind=dbg_kind).ap()

    xv = din("xv", [S, D])
    pown = din("pown", [OWN, 256])
    w_in = din("w_in", [D, INC])
    w_out = din("w_out", [D, D])
    w_gate = din("w_gate", [D, D])
    w_ple = din("w_ple", [256, D])
    gin = din("gin", [128, 8])
    gple = din("gple", [128, 8])
    gfin = din("gfin", [128, D])
    sinkb = din("sinkb", [128, 8])
    cfar_d = din("cfar", [128, 8])
    bswa_d = din("bswa", [128, 4, 8, 128])
    bmoba_d = din("bmoba", [8, 128, 10, 2, 256])
    pm_d = din("pm", [128, 32, 32])
    oz_d = din("oz", [128, 32, 32])
    onehot_d = din("onehot", [32, S], BF16)
    out_d = nc.dram_tensor("out", [OWN, D], F32, kind="ExternalOutput").ap()

    AKT = dscr("AKT", [128, S])
    AV = dscr("AV", [S, 128])
    BKT = dscr("BKT", [512, S])
    BV = dscr("BV", [S, 512])
    AQT = dscr("AQT", [512, OWN])
    BQT = dscr("BQT", [512, OWN])
    GT = dscr("GT", [1024, OWN])
    KMT = dscr("KMT", [512, 32])
    MT = dscr("MT", [1024, OWN])
    WOd = nc.dram_tensor("WOd", [D, D], BF16, kind="Internal").ap()
    WGd = nc.dram_tensor("WGd", [D, D], BF16, kind="Internal").ap()
    WPd = nc.dram_tensor("WPd", [256, D], BF16, kind="Internal").ap()

    with ExitStack() as es:
        def sb(name, shape, dt):
            return es.enter_context(nc.sbuf_tensor(name, list(shape), dt))

        sem_log = []

        def sem(name):
            h = nc.alloc_semaphore(name=name)
            sem_log.append(h)
            return h

        def sems_since(mark, keep=()):
            out = [h for h in sem_log[mark:] if all(h is not k for k in keep)]
            return out

        identb = sb("identb", [128, 128], BF16)
        identf = sb("identf", [128, 128], F32)
        ones_f = sb("ones_f", [128, 64], F32)
        gin_s = sb("gin_s", [128, 8], F32)
        gple_s = sb("gple_s", [128, 8], F32)
        s_wp = sem("s_wp")
        mhalf = sb("mhalf", [128, 1], F32)

        with ExitStack() as es1:
            W1 = es1.enter_context(nc.sbuf_tensor("W1", [128, 8, INC], BF16))

            with ExitStack() as es0:
                wst = [es0.enter_context(nc.sbuf_tensor(f"wst{i}", [128, INC], F32)) for i in range(2)]
                s_wl = [sem("s_wl0"), sem("s_wl1")]
                s_wcA = sem("s_wcA")
                s_wcD = sem("s_wcD")
                s_misc = sem("s_misc")
                s_g0 = sem("s_g0")
                chunks = []
                for kc in range(8):
                    chunks.append((w_in[kc * 128:(kc + 1) * 128, :], INC, W1[:, kc, :], gin_s[:, kc:kc + 1]))
                NCH = len(chunks)
                with nc.Block() as block:
                    @block.sync
                    def _(sync):
                        sync.dma_start(out=gin_s[:], in_=gin[:, :]).then_inc(s_misc, 16)
                        sync.dma_start(out=gple_s[:], in_=gple[:, :]).then_inc(s_misc, 16)
                        for n, (src, wd, dst, sc) in enumerate(chunks):
                            if n >= 2:
                                k = n // 2
                                if n % 2 == 0:
                                    sync.wait_ge(s_wcD, k)
                                else:
                                    sync.wait_ge(s_wcA, k)
                            sync.dma_start(out=wst[n % 2][:, 0:wd], in_=src).then_inc(s_wl[n % 2], 16)
                        sync.wait_ge(s_wl[0], 16 * (NCH // 2))
                        sync.wait_ge(s_wl[1], 16 * (NCH // 2))
                        sync.wait_ge(s_misc, 32)

                    @block.vector
                    def _(v):
                        v.wait_ge(s_misc, 32)
                        for n, (src, wd, dst, sc) in enumerate(chunks):
                            if n % 2 != 0:
                                continue
                            v.wait_ge(s_wl[0], 16 * (n // 2 + 1))
                            if sc is None:
                                v.tensor_copy(out=dst, in_=wst[0][:, 0:wd]).then_inc(s_wcD, 1)
                            else:
                                v.tensor_scalar(out=dst, in0=wst[0][:, 0:wd], scalar1=sc, scalar2=None,
                                                op0=ALU.mult).then_inc(s_wcD, 1)

                    @block.scalar
                    def _(a):
                        a.wait_ge(s_misc, 32)
                        for n, (src, wd, dst, sc) in enumerate(chunks):
                            if n % 2 != 1:
                                continue
                            a.wait_ge(s_wl[1], 16 * (n // 2 + 1))
                            if sc is None:
                                a.activation(out=dst, in_=wst[1][:, 0:wd], func=AF.Copy).then_inc(s_wcA, 1)
                            else:
                                a.activation(out=dst, in_=wst[1][:, 0:wd], func=AF.Copy, scale=sc).then_inc(s_wcA, 1)

                    @block.gpsimd
                    def _(g):
                        g.memset(identf[:], 0.0).then_inc(s_g0, 1)
                        g.wait_ge(s_g0, 1)
                        g.affine_select(out=identf[:], in_=identf[:], pattern=[[-1, 128]],
                                        compare_op=ALU.not_equal, fill=1.0, base=0, channel_multiplier=1).then_inc(s_g0, 1)
                        g.memset(ones_f[:], 1.0)
                        g.memset(mhalf[:], -0.5)
                        g.wait_ge(s_g0, 2)
                        g.tensor_copy(out=identb[:], in_=identf[:])

            free_in_p1 = [s_wl[0], s_wl[1], s_wcA, s_wcD, s_misc, s_g0]
            p1_mark = len(sem_log)
            phase1(nc, es1, locals())
            free_in_p2 = sems_since(p1_mark)

        with ExitStack() as es23:
            def sb23(name, shape, dt):
                return es23.enter_context(nc.sbuf_tensor(name, list(shape), dt))
            pre = dict(
                KP0=sb23("KP0", [96, S], BF16), VP0=sb23("VP0", [128, 64, 65], BF16), QP0=sb23("QP0", [96, OWN], BF16),
                BM0=sb23("BM0", [128, 10, 2, 256], BF16), KM0=sb23("KM0", [64, 32], BF16),
                pm_s=sb23("pm_s", [128, 32, 32], F32), oz_s=sb23("oz_s", [128, 32, 32], F32),
                cfar_s=sb23("cfar_s", [128, 8], F32), NM=sb23("NM", [128, 32, 96], BF16),
                ones_b=sb23("ones_b", [128, 64], BF16),
                s_c=sem("p3_c"), s_hl0=sem("p3_hl0"), s_bml0=sem("p3_bml0"),
            )
            if STOP_AFTER >= 2:
                phase2(nc, locals())
            with ExitStack() as es34:
                def sb34(name, shape, dt):
                    return es34.enter_context(nc.sbuf_tensor(name, list(shape), dt))
                pre4 = dict(WO=sb34("WO", [128, 8, D], BF16), WG=sb34("WG", [128, 8, D], BF16),
                            WP=sb34("WP", [128, 2, D], BF16), GF=sb34("GF", [128, D], F32),
                            s_w4=sem("p4_w"), s_w4g=sem("p4_wg"), s_w4f=sem("p4_wf"))
                if STOP_AFTER >= 3:
                    phase3(nc, locals())
                if STOP_AFTER >= 4:
                    phase4(nc, locals())
    return nc


def phase1(nc, es1, env):
    xv = env["xv"]; W1 = env["W1"]; identb = env["identb"]
    AKT = env["AKT"]; AV = env["AV"]; BKT = env["BKT"]; BV = env["BV"]
    AQT = env["AQT"]; BQT = env["BQT"]; GT = env["GT"]; KMT = env["KMT"]

    def sb(name, shape, dt):
        return es1.enter_context(nc.sbuf_tensor(name, list(shape), dt))

    def ps(name, shape, dt):
        return es1.enter_context(nc.psum_tensor(name, list(shape), dt))

    sem = env["sem"]

    xin = [sb(f"xin{i}", [128, 4, D], F32) for i in range(2)]
    xs = [sb(f"xs{i}", [128, 4, D], BF16) for i in range(2)]
    xT = [sb(f"xT{i}", [128, 8, 512], BF16) for i in range(2)]
    junk = sb("junk", [128, D], BF16)
    ssq = [sb(f"ssq{i}", [128, 4], F32) for i in range(2)]
    sq = [sb(f"sq{i}", [128, 4], F32) for i in range(2)]
    rr = [sb(f"rr{i}", [128, 4], F32) for i in range(2)]
    NS = 3
    stgA = [sb(f"stgA{i}", [128, 512], BF16) for i in range(NS)]
    stgD = [sb(f"stgD{i}", [128, 512], BF16) for i in range(NS)]
    ksum = sb("ksum", [128, 4, 32], F32)
    kmb = sb("kmb", [128, 4, 32], BF16)
    NB = 5
    PB = [ps(f"PB{i}", [128, 512], F32) for i in range(NB)]
    TP = [ps(f"TP{i}", [128, 8, 128], BF16) for i in range(2)]

    s_xl = [sem("p1_xl0"), sem("p1_xl1")]; s_sqd = sem("p1_sqd"); s_rrd = sem("p1_rrd"); s_xsd = sem("p1_xsd")
    s_trd = sem("p1_trd"); s_xtc = sem("p1_xtc"); s_ped = sem("p1_ped")
    s_evA = sem("p1_evA"); s_evD = sem("p1_evD"); s_odA = [sem(f"p1_odA{i}") for i in range(3)]; s_odD = [sem(f"p1_odD{i}") for i in range(3)]
    s_km = sem("p1_km"); s_kmo = sem("p1_kmo"); s_acc = sem("p1_acc")
    s_wp = env["s_wp"]; s_wgl = [sem("p1_wgl0"), sem("p1_wgl1")]; s_wgo = [sem("p1_wgo0"), sem("p1_wgo1")]; s_wgc = sem("p1_wgc")
    w_out = env["w_out"]; w_gate = env["w_gate"]; w_ple = env["w_ple"]; gple_s = env["gple_s"]
    WOd = env["WOd"]; WGd = env["WGd"]; WPd = env["WPd"]
    wgst = [sb(f"wgst{i}", [128, D], F32) for i in range(2)]
    wgo = [sb(f"wgo{i}", [128, D], BF16) for i in range(2)]

    jobs = []
    cnt = {"A": 0, "D": 0}

    def add(j, **kw):
        e = kw["eng"]
        kw["k"] = cnt[e]; cnt[e] += 1
        kw["pair"] = j; kw["n"] = len(jobs)
        jobs.append(kw)

    for j in range(NPAIR):
        t0 = 512 * j
        o0 = 256 * j
        add(j, typ="fm", sec="kv", eng="D", N=512, tok=0, wc=512, dst=AKT[:, t0:t0 + 512], op="copy", ks=None)
        for c in range(4):
            add(j, typ="fm", sec="kv", eng="D", N=512, tok=0, wc=1792 + 128 * c,
                dst=BKT[128 * c:128 * (c + 1), t0:t0 + 512], op="copy", ks=c)
        for tt in range(4):
            add(j, typ="tm", sec="kv", eng="D", N=512, tt=tt, wc=2304,
                dst=BV[t0 + 128 * tt:t0 + 128 * (tt + 1), :], op="copy", ks=None)
            add(j, typ="tm", sec="kv", eng="D", N=128, tt=tt, wc=640,
                dst=AV[t0 + 128 * tt:t0 + 128 * (tt + 1), :], op="copy", ks=None)
        for c in range(4):
            add(j, typ="fm", sec="qg", eng="D", N=256, tok=0, wc=128 * c,
                dst=AQT[128 * c:128 * (c + 1), o0:o0 + 256], op="qscale", ks=None)
            add(j, typ="fm", sec="qg", eng="A", N=256, tok=0, wc=768 + 128 * c,
                dst=GT[128 * c:128 * (c + 1), o0:o0 + 256], op="silu", ks=None)
        for c in range(4):
            add(j, typ="fm", sec="qg", eng="D", N=256, tok=0, wc=1280 + 128 * c,
                dst=BQT[128 * c:128 * (c + 1), o0:o0 + 256], op="qscale", ks=None)
            add(j, typ="fm", sec="qg", eng="A", N=256, tok=0, wc=2816 + 128 * c,
                dst=GT[512 + 128 * c:512 + 128 * (c + 1), o0:o0 + 256], op="silu", ks=None)
    NJ = len(jobs)
    JPP = NJ // NPAIR
    evsem = {"A": s_evA, "D": s_evD}
    odsem = {"A": s_odA, "D": s_odD}
    stg = {"A": stgA, "D": stgD}

    with nc.Block() as block:
        @block.gpsimd
        def _(g):
            nc.clear_and_free_semaphores(env["free_in_p1"])
            for j in range(NPAIR):
                if j >= 2:
                    g.wait_ge(s_xsd, j - 1)
                g.dma_start(out=xin[j % 2][:], in_=xv[512 * j:512 * (j + 1), :].rearrange("(tt p) d -> p tt d", p=128)
                            ).then_inc(s_xl[j % 2], 16)
                if j == 1:
                    g.dma_start(out=WOd[:, :], in_=w_out[:, :], max_dma_last_dim=4096).then_inc(s_wp, 16)
                    g.dma_start(out=WPd[:, :], in_=w_ple[:, :], max_dma_last_dim=4096).then_inc(s_wp, 16)
                if 2 <= j < 10:
                    kc = j - 2
                    g.dma_start(out=wgst[kc % 2][:], in_=w_gate[kc * 128:(kc + 1) * 128, :]).then_inc(s_wgl[kc % 2], 16)
                    g.wait_ge(s_wgl[kc % 2], 16 * (kc // 2 + 1))
                    if kc >= 2:
                        g.wait_ge(s_wgo[kc % 2], 16 * (kc // 2))
                    g.tensor_scalar(out=wgo[kc % 2][:], in0=wgst[kc % 2][:], scalar1=gple_s[:, kc:kc + 1], scalar2=0.0,
                                    op0=ALU.mult, op1=ALU.add).then_inc(s_wgc, 1)
                    g.wait_ge(s_wgc, kc + 1)
                    g.dma_start(out=WGd[kc * 128:(kc + 1) * 128, :], in_=wgo[kc % 2][:]).then_inc(s_wgo[kc % 2], 16)
            g.wait_ge(s_wgo[0], 16 * 4)
            g.wait_ge(s_wgo[1], 16 * 4)
            g.wait_ge(s_wp, 32)

        @block.tensor
        def _(t):
            def transposes(j):
                for tt in range(4):
                    n_t = 4 * j + tt
                    if tt == 0:
                        t.wait_ge(s_xsd, j + 1)
                    if n_t >= 2:
                        t.wait_ge(s_xtc, n_t - 1)
                    for kc in range(8):
                        ins = t.transpose(out=TP[n_t % 2][:, kc, :], in_=xs[j % 2][:, tt, kc * 128:(kc + 1) * 128],
                                          identity=identb[:])
                    ins.then_inc(s_trd, 1)

            def run_job(jb):
                n = jb["n"]; j = jb["pair"]
                if n >= NB:
                    pj = jobs[n - NB]
                    t.wait_ge(evsem[pj["eng"]], pj["k"] + 1)
                bank = PB[n % NB]
                for kc in range(8):
                    if jb["typ"] == "fm":
                        ins = t.matmul(bank[:, 0:jb["N"]], W1[:, kc, jb["wc"]:jb["wc"] + 128],
                                       xT[j % 2][:, kc, jb["tok"]:jb["tok"] + jb["N"]],
                                       start=(kc == 0), stop=(kc == 7))
                    else:
                        tt = jb["tt"]
                        ins = t.matmul(bank[:, 0:jb["N"]], xT[j % 2][:, kc, tt * 128:(tt + 1) * 128],
                                       W1[:, kc, jb["wc"]:jb["wc"] + jb["N"]],
                                       start=(kc == 0), stop=(kc == 7))
                ins.then_inc(s_ped, 1)

            transposes(0)
            for j in range(NPAIR):
                pj = [jb for jb in jobs if jb["pair"] == j]
                t.wait_ge(s_xtc, 4 * (j + 1))
                for jb in pj:
                    if jb["sec"] == "kv":
                        run_job(jb)
                qg = [jb for jb in pj if jb["sec"] == "qg"]
                for jb in qg[:8]:
                    run_job(jb)
                if j + 1 < NPAIR:
                    transposes(j + 1)
                for jb in qg[8:]:
                    run_job(jb)

        def evac(eng, e, jb):
            n = jb["n"]; k = jb["k"]; N = jb["N"]
            eng.wait_ge(s_ped, n + 1)
            if k >= NS:
                eng.wait_ge(odsem[e][k % NS], 16 * (k // NS))
            bank = PB[n % NB]
            dst = stg[e][k % NS][:, 0:N]
            if jb["op"] == "silu":
                eng.activation(out=dst, in_=bank[:, 0:N], func=AF.Silu).then_inc(evsem[e], 1)
            elif jb["op"] == "qscale":
                eng.tensor_scalar(out=dst, in0=bank[:, 0:N], scalar1=0.125, scalar2=None,
                                  op0=ALU.mult).then_inc(evsem[e], 1)
            else:
                if jb["ks"] is not None:
                    c = jb["ks"]; j = jb["pair"]
                    eng.tensor_reduce(out=ksum[:, c, 2 * j:2 * j + 2],
                                      in_=bank[:, 0:512].rearrange("p (b k) -> p b k", k=256),
                                      axis=AX.X, op=ALU.add)
                eng.tensor_copy(out=dst, in_=bank[:, 0:N]).then_inc(evsem[e], 1)

        @block.vector
        def _(v):
            def recip(j):
                v.wait_ge(s_sqd, j + 1)
                v.reciprocal(out=rr[j % 2][:], in_=sq[j % 2][:]).then_inc(s_rrd, 1)

            def xtcopies(j):
                for tt in range(4):
                    n_t = 4 * j + tt
                    v.wait_ge(s_trd, n_t + 1)
                    if tt == 0 and j >= 2:
                        v.wait_ge(s_ped, JPP * (j - 1))
                    v.tensor_copy(out=xT[j % 2][:, :, tt * 128:(tt + 1) * 128], in_=TP[n_t % 2][:, :, :]
                                  ).then_inc(s_xtc, 1)

            recip(0)
            xtcopies(0)
            for j in range(NPAIR):
                pj = [jb for jb in jobs if jb["pair"] == j and jb["eng"] == "D"]
                if j + 1 < NPAIR:
                    recip(j + 1)
                for jb in pj:
                    if jb["sec"] == "kv":
                        evac(v, "D", jb)
                qgd = [jb for jb in pj if jb["sec"] == "qg"]
                for jb in qgd[:4]:
                    evac(v, "D", jb)
                if j + 1 < NPAIR:
                    xtcopies(j + 1)
                for jb in qgd[4:]:
                    evac(v, "D", jb)
            v.tensor_scalar(out=kmb[:], in0=ksum[:], scalar1=1.0 / 256.0, scalar2=None,
                            op0=ALU.mult).then_inc(s_km, 1)

        @block.scalar
        def _(a):
            def norm(j):
                a.wait_ge(s_xl[j % 2], 16 * (j // 2 + 1))
                for tt in range(4):
                    ins = a.activation(out=junk[:], in_=xin[j % 2][:, tt, :], func=AF.Square,
                                       accum_out=ssq[j % 2][:, tt:tt + 1])
                ins.then_inc(s_acc, 1)
                a.wait_ge(s_acc, j + 1)
                a.activation(out=sq[j % 2][:], in_=ssq[j % 2][:], func=AF.Sqrt, scale=1.0 / D, bias=1e-6
                             ).then_inc(s_sqd, 1)
                a.wait_ge(s_rrd, j + 1)
                if j >= 2:
                    a.wait_ge(s_trd, 4 * (j - 1))
                for tt in range(4):
                    ins = a.activation(out=xs[j % 2][:, tt, :], in_=xin[j % 2][:, tt, :], func=AF.Copy,
                                       scale=rr[j % 2][:, tt:tt + 1])
                ins.then_inc(s_xsd, 1)

            norm(0)
            for j in range(NPAIR):
                if j + 1 < NPAIR:
                    norm(j + 1)
                for jb in jobs:
                    if jb["pair"] == j and jb["eng"] == "A":
                        evac(a, "A", jb)

        @block.sync
        def _(sync):
            for jb in jobs:
                e = jb["eng"]; k = jb["k"]
                sync.wait_ge(evsem[e], k + 1)
                sync.dma_start(out=jb["dst"], in_=stg[e][k % NS][:, 0:jb["N"]]).then_inc(odsem[e][k % NS], 16)
            sync.wait_ge(s_km, 1)
            sync.dma_start(out=KMT.rearrange("(c p) n -> p c n", p=128), in_=kmb[:]).then_inc(s_kmo, 16)
            for e_ in ("A", "D"):
                for sl_ in range(NS):
                    sync.wait_ge(odsem[e_][sl_], 16 * len(range(sl_, cnt[e_], NS)))
            sync.wait_ge(s_kmo, 16)


def _mk(nc, est):
    def sb(name, shape, dt):
        return est.enter_context(nc.sbuf_tensor(name, list(shape), dt))

    def ps(name, shape, dt=F32):
        return est.enter_context(nc.psum_tensor(name, list(shape), dt))
    return sb, ps


def phase2(nc, env):
    sem = env["sem"]; ones_f = env["ones_f"]
    AKT = env["AKT"]; AV = env["AV"]; AQT = env["AQT"]; GT = env["GT"]; MT = env["MT"]
    bswa_d = env["bswa_d"]; sinkb = env["sinkb"]
    pre = env["pre"]; BKT = env["BKT"]; BV = env["BV"]; BQT = env["BQT"]; KMT = env["KMT"]
    bmoba_d = env["bmoba_d"]; pm_d = env["pm_d"]; oz_d = env["oz_d"]; onehot_d = env["onehot_d"]; cfar_d = env["cfar_d"]
    with ExitStack() as e2:
        sb, ps = _mk(nc, e2)
        AQs = sb("AQs", [128, 4, OWN], BF16)
        AKs = sb("AKs", [128, S], BF16)
        AVs = sb("AVs", [128, 64, 2, 65], BF16)
        bsw = sb("bsw", [128, 4, 8, 128], F32)
        esk = sb("esk", [128, 8], F32)
        esk_rep = sb("esk_rep", [1, 8, 128], F32)
        esk_hi = sb("esk_hi", [1, 8, 128], BF16)
        esk_lo = sb("esk_lo", [1, 8, 128], BF16)
        sel65 = sb("sel65", [1, 65], BF16)
        bcs = [sb(f"s_bcs{i}", [64, 512], F32) for i in range(4)]
        lnr = sb("s_lnr", [128, 512], F32)
        tmp = [sb(f"s_tmp{i}", [128, 4, 128], F32) for i in range(2)]
        PTb = [sb(f"s_pt{i}", [128, 512], BF16) for i in range(3)]
        rrow = [sb(f"s_rrow{i}", [128, 512], F32) for i in range(4)]
        t1b = [sb(f"s_t1{i}", [64, 512], F32) for i in range(4)]
        mst = [sb(f"s_mst{i}", [64, 4, 128], BF16) for i in range(4)]
        gsw = [sb(f"s_g{i}", [64, 4, 128], BF16) for i in range(4)]
        SP = [ps(f"s_SP{i}", [128, 512]) for i in range(4)]
        OP = [ps(f"s_OP{i}", [128, 512]) for i in range(2)]
        s_ld = sem("p2_ld"); s_ldc = [sem(f"p2_ldc{i}") for i in range(4)]; s_qk = sem("p2_qk"); s_add = sem("p2_add"); s_exp = sem("p2_exp"); s_pv = sem("p2_pv")
        s_rr = sem("p2_rr"); s_t1 = sem("p2_t1"); s_bc = sem("p2_bc"); s_mx = sem("p2_mx"); s_mo = [sem(f"p2_mo{i}") for i in range(4)]
        s_gl = [sem(f"p2_gl{i}") for i in range(4)]; s_es = sem("p2_es"); s_ms = sem("p2_ms"); s_g2 = sem("p2_g2"); s_er = sem("p2_er"); s_bcs = sem("p2_bcs")

        NU = 64
        units = []
        q = 0
        for m in range(NU):
            T, kvh = m // 2, m % 2
            i, u = T // 2, T % 2
            if u == 0:
                cl = ([(4 * (i - 1) + 3, 2)] if i > 0 else []) + [(4 * i + 3, 3), (4 * i, 0)]
            else:
                cl = [(4 * i, 1), (4 * i + 1, 0)]
            cands = []
            for (kt, kind) in cl:
                cands.append(dict(q=q, kt=kt, kind=kind))
                q += 1
            units.append(dict(m=m, T=T, kvh=kvh, cands=cands))
        NLD = 2
        NLDC = 2 + 1 + 2

        with nc.Block() as block:
            @block.sync
            def _(sync):
                sync.dma_start(out=esk[:], in_=sinkb[:, :]).then_inc(s_ld, 16)
                sync.dma_start(out=bsw[:], in_=bswa_d[:, :, :, :]).then_inc(s_ld, 16)
                for g in range(4):
                    for kvh in range(2):
                        sync.dma_start(out=AQs[kvh * 64:(kvh + 1) * 64, :, 1024 * g:1024 * (g + 1)],
                                       in_=AQT[kvh * 256:(kvh + 1) * 256, 1024 * g:1024 * (g + 1)].rearrange("(hh d) t -> d hh t", d=64)
                                       ).then_inc(s_ldc[g], 16)
                    sync.dma_start(out=AKs[:, 2048 * g:2048 * (g + 1)], in_=AKT[:, 2048 * g:2048 * (g + 1)]).then_inc(s_ldc[g], 16)
                    for k in range(2):
                        sync.dma_start(out=AVs[:, 16 * g:16 * (g + 1), k, 0:64],
                                       in_=AV[2048 * g:2048 * (g + 1), k * 64:(k + 1) * 64].rearrange("(t p) d -> p t d", p=128)
                                       ).then_inc(s_ldc[g], 16)
                sync.dma_start(out=pre["KP0"][64:96, :], in_=onehot_d[:, :]).then_inc(pre["s_c"], 16)
                sync.dma_start(out=pre["pm_s"][:], in_=pm_d[:, :, :]).then_inc(pre["s_c"], 16)
                sync.dma_start(out=pre["oz_s"][:], in_=oz_d[:, :, :]).then_inc(pre["s_c"], 16)
                sync.dma_start(out=pre["cfar_s"][:], in_=cfar_d[:, :]).then_inc(pre["s_c"], 16)
                sync.dma_start(out=pre["KP0"][0:64, :], in_=BKT[0:64, :]).then_inc(pre["s_hl0"], 16)
                sync.dma_start(out=pre["QP0"][0:64, :], in_=BQT[0:64, :]).then_inc(pre["s_hl0"], 16)
                for g in range(4):
                    sync.dma_start(out=pre["VP0"][:, 16 * g:16 * (g + 1), 0:64],
                                   in_=BV[2048 * g:2048 * (g + 1), 0:64].rearrange("(t p) d -> p t d", p=128)
                                   ).then_inc(pre["s_hl0"], 16)
                sync.dma_start(out=pre["KM0"][:], in_=KMT[0:64, :]).then_inc(pre["s_hl0"], 16)
                for un in units:
                    m, T, kvh = un["m"], un["T"], un["kvh"]
                    sync.wait_ge(s_mx, m + 1)
                    sync.dma_start(out=MT[kvh * 256:(kvh + 1) * 256, T * 128:(T + 1) * 128].rearrange("(hh d) t -> d hh t", d=64),
                                   in_=mst[m % 4][:]).then_inc(s_mo[m % 4], 16)
                for k_ in range(4):
                    sync.wait_ge(s_mo[k_], 16 * (NU // 4))
                sync.wait_ge(pre["s_c"], 16 * 4)
                sync.wait_ge(pre["s_hl0"], 16 * 7)

            @block.gpsimd
            def _(g):
                nc.clear_and_free_semaphores(env["free_in_p2"])
                g.memset(AVs[:, :, :, 64:65], 1.0)
                g.memset(sel65[:], 0.0).then_inc(s_g2, 1)
                g.wait_ge(s_g2, 1)
                g.memset(sel65[0:1, 64:65], 1.0)
                g.memset(rrow[0][:], 1.0)
                g.memset(rrow[2][:], 1.0)
                g.memset(rrow[3][:], 1.0)
                g.memset(rrow[1][:], 1.0).then_inc(s_ms, 1)
                g.memset(pre["VP0"][:, :, 64:65], 1.0)
                g.memset(pre["NM"][:], 0.0)
                g.memset(pre["ones_b"][:], 1.0)
                g.dma_start(out=pre["BM0"][:], in_=bmoba_d[0], max_dma_last_dim=4096).then_inc(pre["s_bml0"], 16)

                def gload(un):
                    m, T, kvh = un["m"], un["T"], un["kvh"]
                    if m >= 4:
                        g.wait_ge(s_t1, m - 3)
                    g.dma_start(out=gsw[m % 4][:],
                                in_=GT[kvh * 256:(kvh + 1) * 256, T * 128:(T + 1) * 128].rearrange("(hh d) t -> d hh t", d=64)
                                ).then_inc(s_gl[m % 4], 16)

                for k_ in range(3):
                    gload(units[k_])
                for un in units:
                    m = un["m"]
                    if m + 3 < NU:
                        gload(units[m + 3])
                    g.wait_ge(s_bcs, m + 1)
                    g.wait_ge(s_t1, m + 1)
                    if m >= 4:
                        g.wait_ge(s_mo[m % 4], 16 * (m // 4))
                    g.tensor_tensor(out=mst[m % 4][:].rearrange("p h q -> p (h q)"), in0=t1b[m % 4][:],
                                    in1=bcs[m % 4][:], op=ALU.mult).then_inc(s_mx, 1)

            @block.tensor
            def _(t):
                def QK(un):
                    T, kvh = un["T"], un["kvh"]
                    if un["m"] % 16 == 0:
                        t.wait_ge(s_ldc[un["m"] // 16], 16 * NLDC)
                    for cd in un["cands"]:
                        qq = cd["q"]
                        if qq >= 4:
                            t.wait_ge(s_add, qq - 3)
                        t.matmul(SP[qq % 4][:, :], AKs[kvh * 64:(kvh + 1) * 64, cd["kt"] * 128:(cd["kt"] + 1) * 128],
                                 AQs[kvh * 64:(kvh + 1) * 64, :, T * 128:(T + 1) * 128], start=True, stop=True
                                 ).then_inc(s_qk, 1)

                def PV(un):
                    m, kvh = un["m"], un["kvh"]
                    if m >= 2:
                        t.wait_ge(s_t1, m - 1)
                    first = True
                    for cd in un["cands"]:
                        qq = cd["q"]
                        t.wait_ge(s_exp, qq + 1)
                        ins = t.matmul(OP[m % 2][0:65, :], AVs[:, cd["kt"], kvh, :], PTb[qq % 3][:, :],
                                       start=first, stop=False)
                        first = False
                        if cd is un["cands"][-1]:
                            ins = t.matmul(OP[m % 2][0:65, :], sel65[0:1, 0:65], esk_hi[0:1, kvh * 4:(kvh + 1) * 4, :],
                                           start=False, stop=True)
                        ins.then_inc(s_pv, 1)

                t.wait_ge(s_ld, 16 * NLD)
                t.wait_ge(s_ms, 1)
                t.wait_ge(s_er, 1)
                QK(units[0]); QK(units[1])
                for m in range(NU):
                    PV(units[m])
                    if m + 2 < NU:
                        QK(units[m + 2])

            @block.vector
            def _(v):
                def adds(un):
                    kvh = un["kvh"]
                    for cd in un["cands"]:
                        qq = cd["q"]
                        v.wait_ge(s_qk, qq + 1)
                        if qq >= 2:
                            v.wait_ge(s_exp, qq - 1)
                        v.tensor_tensor(out=tmp[qq % 2][:], in0=SP[qq % 4][:, :].rearrange("p (h q) -> p h q", q=128),
                                        in1=bsw[:, cd["kind"], kvh * 4:(kvh + 1) * 4, :], op=ALU.add).then_inc(s_add, 1)

                def post(un):
                    m, kvh = un["m"], un["kvh"]
                    lastq = un["cands"][-1]["q"]
                    v.wait_ge(s_pv, lastq + 1)
                    v.wait_ge(s_gl[m % 4], 16 * (m // 4 + 1))
                    if m >= 4:
                        v.wait_ge(s_mx, m - 3)
                    v.tensor_tensor(out=t1b[m % 4][:], in0=OP[m % 2][0:64, :],
                                    in1=gsw[m % 4][:].rearrange("p h q -> p (h q)"), op=ALU.mult).then_inc(s_t1, 1)

                def fin(un):
                    m = un["m"]
                    v.wait_ge(s_rr, m + 1)
                    if m >= 4:
                        v.wait_ge(s_mx, m - 3)
                    v.stream_shuffle(out=bcs[m % 4][0:32, :], in_=rrow[m % 4][64:96, :], mask=[0] * 32)
                    v.stream_shuffle(out=bcs[m % 4][32:64, :], in_=rrow[m % 4][64:96, :], mask=[0] * 32
                                     ).then_inc(s_bcs, 1)

                v.wait_ge(s_ld, 16 * NLD)
                v.wait_ge(s_es, 1)
                v.tensor_copy(out=esk_rep[0:1, :, :], in_=esk[0:1, :].unsqueeze(2).broadcast_to([1, 8, 128]))
                v.tensor_copy(out=esk_hi[0:1, :, :], in_=esk_rep[0:1, :, :])
                v.tensor_tensor(out=esk_lo[0:1, :, :], in0=esk_rep[0:1, :, :], in1=esk_hi[0:1, :, :], op=ALU.subtract
                                ).then_inc(s_er, 1)
                adds(units[0]); adds(units[1])
                for m in range(NU):
                    post(units[m])
                    if m + 2 < NU:
                        adds(units[m + 2])
                    if m >= 1:
                        fin(units[m - 1])
                fin(units[NU - 1])

            @block.scalar
            def _(a):
                a.wait_ge(s_ld, 16 * NLD)
                a.activation(out=esk[:], in_=esk[:], func=AF.Exp).then_inc(s_es, 1)
                def recip_act(un):
                    m = un["m"]
                    a.wait_ge(s_pv, un["cands"][-1]["q"] + 1)
                    if m >= 4:
                        a.wait_ge(s_bcs, m - 3)
                    a.activation(out=lnr[64:65, :], in_=OP[m % 2][64:65, :], func=AF.Ln)
                    a.activation(out=rrow[m % 4][64:65, :], in_=lnr[64:65, :], func=AF.Exp, scale=-1.0).then_inc(s_rr, 1)

                for un in units:
                    for cd in un["cands"]:
                        qq = cd["q"]
                        a.wait_ge(s_add, qq + 1)
                        if qq >= 3:
                            a.wait_ge(s_pv, qq - 2)
                        a.activation(out=PTb[qq % 3][:], in_=tmp[qq % 2][:].rearrange("p h q -> p (h q)"), func=AF.Exp
                                     ).then_inc(s_exp, 1)
                    if un["m"] >= 1:
                        recip_act(units[un["m"] - 1])
                recip_act(units[NU - 1])


def phase3(nc, env):
    sem = env["sem"]; identb = env["identb"]
    BKT = env["BKT"]; BV = env["BV"]; BQT = env["BQT"]; GT = env["GT"]; MT = env["MT"]; KMT = env["KMT"]
    bmoba_d = env["bmoba_d"]; pm_d = env["pm_d"]; oz_d = env["oz_d"]; onehot_d = env["onehot_d"]; cfar_d = env["cfar_d"]
    with ExitStack() as e3:
        sb, ps = _mk(nc, e3)
        pre = env["pre"]
        KP = [pre["KP0"], sb("KP1", [96, S], BF16)]
        VP = [pre["VP0"], sb("VP1", [128, 64, 65], BF16)]
        QP = [pre["QP0"], sb("QP1", [96, OWN], BF16)]
        BM = [pre["BM0"], sb("BM1", [128, 10, 2, 256], BF16)]
        KM = [pre["KM0"], sb("KM1", [64, 32], BF16)]
        pm_s = pre["pm_s"]; oz_s = pre["oz_s"]; cfar_s = pre["cfar_s"]
        smb = sb("smb", [128, 32, 32], F32)
        m8 = sb("m8", [128, 32, 8], F32)
        NM = pre["NM"]; ones_b = pre["ones_b"]
        PTg = [sb(f"PTg{i}", [128, 1024], BF16) for i in range(3)]
        obs = [sb(f"obs{i}", [65, 512], F32) for i in range(2)]
        rrow = [sb(f"rrowm{i}", [96, 512], F32) for i in range(2)]
        bcm = [sb(f"bcm{i}", [64, 512], F32) for i in range(2)]
        t1m = [sb(f"t1m{i}", [64, 512], F32) for i in range(2)]
        mstm = [sb(f"mstm{i}", [64, 512], BF16) for i in range(2)]
        gmm = [sb(f"gmm{i}", [64, 512], BF16) for i in range(2)]
        NSB = 2
        SBg = [ps(f"m_SB{i}", [128, 1024]) for i in range(NSB)]
        SELb = [ps(f"m_SEL{i}", [128, 512]) for i in range(2)]
        OB = ps("m_OB", [128, 512])
        TPn = [SELb[0][:, :].bitcast(BF16), SELb[1][:, :].bitcast(BF16)]

        s_c = pre["s_c"]; s_hl = [pre["s_hl0"], sem("p3_hl1")]; s_bml = [pre["s_bml0"], sem("p3_bml1")]
        s_sc = sem("p3_sc"); s_m8 = sem("p3_m8"); s_nm = sem("p3_nm"); s_bmf = sem("p3_bmf")
        s_selt = sem("p3_selt"); s_selcp = sem("p3_selcp"); s_qk = sem("p3_qk")
        s_exp = sem("p3_exp"); s_pv = sem("p3_pv"); s_obc = sem("p3_obc"); s_rr = sem("p3_rr"); s_bcs = sem("p3_bcs")
        s_t1 = sem("p3_t1"); s_bc = sem("p3_bc"); s_mx = sem("p3_mx"); s_mo = [sem("p3_mo0"), sem("p3_mo1")]
        s_gl = [sem("p3_gl0"), sem("p3_gl1")]; s_ms = sem("p3_ms"); s_gq = sem("p3_gq")

        heads = []
        gidx = 0
        kidx = 0
        near_g = []
        for h in range(8):
            jl = []
            pairs = []
            for spi, sp in enumerate([7, 0, 6, 1, 5, 2, 4, 3]):
                gp = 8 * h + spi
                i0 = 2 * sp
                pr = dict(gp=gp, sp=sp, h=h, first=gidx)
                nkb = 4 * sp + 4
                for vb in range(nkb):
                    e0 = 2 * i0 + 1 - vb
                    jb = dict(g=gidx, h=h, sp=sp, gp=gp, vb=vb, e0=e0, near=(e0 <= 5),
                              firstjob=(vb == 0), lastjob=(vb == nkb - 1))
                    if jb["near"]:
                        jb["k"] = kidx; kidx += 1; near_g.append(gidx)
                    jl.append(jb)
                    gidx += 1
                pr["last"] = gidx - 1
                pairs.append(pr)
            heads.append(dict(h=h, jobs=jl, pairs=pairs))
        NHL = 7

        with nc.Block() as block:
            @block.sync
            def _(sync):
                sync.dma_start(out=KP[1][64:96, :], in_=onehot_d[:, :]).then_inc(s_c, 16)

                def loads(h):
                    hb = h % 2
                    sync.dma_start(out=KP[hb][0:64, :], in_=BKT[h * 64:(h + 1) * 64, :]).then_inc(s_hl[hb], 16)
                    sync.dma_start(out=QP[hb][0:64, :], in_=BQT[h * 64:(h + 1) * 64, :]).then_inc(s_hl[hb], 16)
                    for g in range(4):
                        sync.dma_start(out=VP[hb][:, 16 * g:16 * (g + 1), 0:64],
                                       in_=BV[2048 * g:2048 * (g + 1), h * 64:(h + 1) * 64].rearrange("(t p) d -> p t d", p=128)
                                       ).then_inc(s_hl[hb], 16)
                    sync.dma_start(out=KM[hb][:], in_=KMT[h * 64:(h + 1) * 64, :]).then_inc(s_hl[hb], 16)

                loads(1)
                p4 = env["pre4"]
                sync.dma_start(out=p4["WO"][:], in_=env["WOd"].rearrange("(c p) n -> p c n", p=128)).then_inc(p4["s_w4"], 16)
                sync.dma_start(out=p4["WG"][:], in_=env["WGd"].rearrange("(c p) n -> p c n", p=128)).then_inc(p4["s_w4g"], 16)
                sync.dma_start(out=p4["WP"][:], in_=env["WPd"].rearrange("(c p) n -> p c n", p=128)).then_inc(p4["s_w4g"], 16)
                sync.dma_start(out=p4["GF"][:], in_=env["gfin"][:, :]).then_inc(p4["s_w4f"], 16)
                for hd in heads:
                    h = hd["h"]
                    for pr in hd["pairs"]:
                        gp = pr["gp"]
                        sync.wait_ge(s_mx, gp + 1)
                        sync.dma_start(out=MT[512 + h * 64:512 + (h + 1) * 64, pr["sp"] * 512:(pr["sp"] + 1) * 512],
                                       in_=mstm[gp % 2][:]).then_inc(s_mo[gp % 2], 16)
                    if h + 2 < 8:
                        loads(h + 2)
                sync.wait_ge(s_mo[0], 16 * 32)
                sync.wait_ge(s_mo[1], 16 * 32)
                sync.wait_ge(p4["s_w4"], 16)
                sync.wait_ge(p4["s_w4g"], 32)
                sync.wait_ge(p4["s_w4f"], 16)

            @block.gpsimd
            def _(g):
                gq = [0]

                def bmload(h):
                    g.dma_start(out=BM[h % 2][:], in_=bmoba_d[h], max_dma_last_dim=4096).then_inc(s_bml[h % 2], 16)

                g.memset(rrow[0][:], 1.0)
                g.memset(rrow[1][:], 1.0)
                g.memset(VP[1][:, :, 64:65], 1.0).then_inc(s_ms, 1)
                bmload(1)
                for hd in heads:
                    h = hd["h"]
                    for pr in hd["pairs"]:
                        gp = pr["gp"]
                        if gp >= 2:
                            g.wait_ge(s_t1, gp - 1)
                        g.dma_start(out=gmm[gp % 2][:],
                                    in_=GT[512 + h * 64:512 + (h + 1) * 64, pr["sp"] * 512:(pr["sp"] + 1) * 512]
                                    ).then_inc(s_gl[gp % 2], 16)
                        g.wait_ge(s_obc, gp + 1)
                        g.wait_ge(s_gl[gp % 2], 16 * (gp // 2 + 1))
                        g.tensor_tensor(out=t1m[gp % 2][:], in0=obs[gp % 2][0:64, :], in1=gmm[gp % 2][:], op=ALU.mult
                                        ).then_inc(s_t1, 1)
                        g.wait_ge(s_bcs, gp + 1)
                        g.wait_ge(s_t1, gp + 1)
                        if gp >= 2:
                            g.wait_ge(s_mo[gp % 2], 16 * (gp // 2))
                        g.tensor_tensor(out=mstm[gp % 2][:], in0=t1m[gp % 2][:], in1=bcm[gp % 2][:], op=ALU.mult
                                        ).then_inc(s_mx, 1)
                    if h + 2 < 8:
                        g.wait_ge(s_qk, hd["jobs"][-1]["g"] + 1)
                        bmload(h + 2)

            @block.tensor
            def _(t):
                t.wait_ge(s_c, 16 * 5)
                t.wait_ge(s_ms, 1)

                def QK(jb):
                    g_, hb, sp, vb = jb["g"], jb["h"] % 2, jb["sp"], jb["vb"]
                    e0 = jb["e0"]
                    for ks in range(2):
                        kt = 2 * vb + ks
                        ins = t.matmul(SBg[g_ % NSB][:, ks * 512:(ks + 1) * 512], KP[hb][0:96, kt * 128:(kt + 1) * 128],
                                       QP[hb][0:96, sp * 512:(sp + 1) * 512], start=True, stop=not jb["near"])
                        if ks == 0 and g_ >= NSB:
                            ins._wait_ge(s_exp, g_ - NSB + 1)
                        if jb["near"]:
                            if e0 < 0:
                                ins = t.matmul(SBg[g_ % NSB][:, ks * 512 + 256:(ks + 1) * 512], identb[:, :],
                                               BM[hb][:, e0 + 4, ks, :], start=False, stop=True)
                            elif e0 <= 3:
                                ins = t.matmul(SBg[g_ % NSB][:, ks * 512:(ks + 1) * 512], identb[:, :],
                                               BM[hb][:, e0 + 2:e0 + 5:2, ks, :], start=False, stop=True)
                            else:
                                ins = t.matmul(SBg[g_ % NSB][:, ks * 512:ks * 512 + 256], identb[:, :],
                                               BM[hb][:, e0 + 2, ks, :], start=False, stop=True)
                    ins.then_inc(s_qk, 1)

                def PV(jb):
                    g_, hb, vb, gp = jb["g"], jb["h"] % 2, jb["vb"], jb["gp"]
                    if jb["firstjob"] and gp >= 1:
                        t.wait_ge(s_obc, gp)
                    for ks in range(2):
                        kt = 2 * vb + ks
                        ins = t.matmul(OB[0:65, :], VP[hb][:, kt, :], PTg[g_ % 3][:, ks * 512:(ks + 1) * 512],
                                       start=(jb["firstjob"] and ks == 0), stop=(jb["lastjob"] and ks == 1))
                        if ks == 0:
                            ins._wait_ge(s_exp, g_ + 1)
                    ins.then_inc(s_pv, 1)

                def sel_scores(h, half):
                    hb = h % 2
                    if half == 0:
                        t.wait_ge(s_hl[hb], 16 * NHL * (h // 2 + 1))
                        if h >= 1:
                            t.wait_ge(s_selcp, 8 * h)
                    else:
                        t.wait_ge(s_m8, 2 * h + 1)
                    for T in range(16 * half, 16 * half + 16):
                        ins = t.matmul(SELb[half][:, (T % 16) * 32:(T % 16 + 1) * 32], QP[hb][0:64, T * 128:(T + 1) * 128],
                                       KM[hb][0:64, :], start=True, stop=True)
                    ins.then_inc(s_sc, 1)

                def sel_tr(h, gq_):
                    if gq_ == 0:
                        t.wait_ge(s_nm, h + 1)
                        t.wait_ge(s_m8, 2 * h + 2)
                    if gq_ >= 2:
                        t.wait_ge(s_selcp, 8 * h + gq_ - 1)
                    for tq in range(4):
                        T = 4 * gq_ + tq
                        ins = t.transpose(out=TPn[gq_ % 2][0:96, tq * 128:(tq + 1) * 128], in_=NM[:, T, :],
                                          identity=identb[:])
                    ins.then_inc(s_selt, 1)

                sel_scores(0, 0)
                sel_scores(0, 1)
                for gq_ in range(8):
                    sel_tr(0, gq_)
                for hd in heads:
                    h = hd["h"]; hb = h % 2
                    t.wait_ge(s_selcp, 8 * (h + 1))
                    t.wait_ge(s_bmf, h + 1)
                    jl = hd["jobs"]
                    QK(jl[0])
                    if NSB >= 3:
                        QK(jl[1])
                    for idx, jb in enumerate(jl):
                        if NSB >= 3:
                            PV(jb)
                            if idx + 2 < len(jl):
                                QK(jl[idx + 2])
                        else:
                            if idx + 1 < len(jl):
                                QK(jl[idx + 1])
                            PV(jb)
                        if h + 1 < 8:
                            if idx == 36:
                                sel_scores(h + 1, 0)
                            if idx == 46:
                                sel_scores(h + 1, 1)
                            if idx >= 64 and (idx - 64) % 8 == 0 and (idx - 64) // 8 < 8:
                                sel_tr(h + 1, (idx - 64) // 8)

            @block.vector
            def _(v):
                v.wait_ge(s_c, 16 * 5)

                def post(pr):
                    gp = pr["gp"]
                    v.wait_ge(s_pv, pr["last"] + 1)
                    if gp >= 2:
                        v.wait_ge(s_t1, gp - 1)
                    v.tensor_copy(out=obs[gp % 2][:], in_=OB[0:65, :]).then_inc(s_obc, 1)
                    for c in range(4):
                        rpend.append((gp, c))

                rpend = []

                def rchunk():
                    if not rpend:
                        return
                    gp, c = rpend.pop(0)
                    if c == 0:
                        v.wait_ge(s_obc, gp + 1)
                        if gp >= 2:
                            v.wait_ge(s_bcs, gp - 1)
                    ins = v.reciprocal(out=rrow[gp % 2][64:65, c * 128:(c + 1) * 128],
                                       in_=obs[gp % 2][64:65, c * 128:(c + 1) * 128])
                    if c == 3:
                        ins.then_inc(s_rr, 1)
                        v.wait_ge(s_rr, gp + 1)
                        if gp >= 2:
                            v.wait_ge(s_mx, gp - 1)
                        v.stream_shuffle(out=bcm[gp % 2][0:32, :], in_=rrow[gp % 2][64:96, :], mask=[0] * 32)
                        v.stream_shuffle(out=bcm[gp % 2][32:64, :], in_=rrow[gp % 2][64:96, :], mask=[0] * 32
                                         ).then_inc(s_bcs, 1)

                def bmfold(h):
                    hb = h % 2
                    v.wait_ge(s_bml[hb], 16 * (h // 2 + 1))
                    v.tensor_scalar(out=BM[hb][:], in0=BM[hb][:], scalar1=cfar_s[:, h:h + 1], scalar2=None,
                                    op0=ALU.subtract).then_inc(s_bmf, 1)

                def sel_dve(h):
                    for half in range(2):
                        v.wait_ge(s_sc, 2 * h + half + 1)
                        v.tensor_tensor(out=smb[:, 16 * half:16 * (half + 1), :],
                                        in0=SELb[half][:, :].rearrange("p (t n) -> p t n", n=32),
                                        in1=pm_s[:, 16 * half:16 * (half + 1), :], op=ALU.add)
                        for T in range(16 * half, 16 * half + 16):
                            ins = v.max(out=m8[:, T, :], in_=smb[:, T, :])
                        ins.then_inc(s_m8, 1)
                    v.wait_ge(s_m8, 2 * h + 2)
                    v.tensor_tensor(out=smb[:], in0=smb[:], in1=m8[:, :, 2:3].broadcast_to([128, 32, 32]), op=ALU.is_ge)
                    v.tensor_scalar(out=smb[:], in0=smb[:], scalar1=-1.0, scalar2=-NEG, op0=ALU.add, op1=ALU.mult)
                    v.tensor_tensor(out=smb[:], in0=smb[:], in1=pm_s[:], op=ALU.add)
                    v.tensor_tensor(out=smb[:], in0=smb[:], in1=oz_s[:], op=ALU.mult)
                    v.tensor_scalar(out=NM[:, :, 64:96], in0=smb[:], scalar1=cfar_s[:, h:h + 1], scalar2=None,
                                    op0=ALU.add).then_inc(s_nm, 1)

                def sel_cp(h, gq_):
                    hb = h % 2
                    v.wait_ge(s_selt, 8 * h + gq_ + 1)
                    v.tensor_copy(out=QP[hb][64:96, gq_ * 512:(gq_ + 1) * 512], in_=TPn[gq_ % 2][64:96, 0:512]
                                  ).then_inc(s_selcp, 1)

                bmfold(0)
                sel_dve(0)
                for gq_ in range(8):
                    sel_cp(0, gq_)
                for hd in heads:
                    h = hd["h"]; hb = h % 2
                    for pi, pr in enumerate(hd["pairs"]):
                        post(pr)
                        while rpend:
                            rchunk()
                        if h + 1 < 8:
                            if pi == 1:
                                sel_dve(h + 1)
                            if pi == 3:
                                for gq_ in range(0, 4):
                                    sel_cp(h + 1, gq_)
                            if pi == 4:
                                for gq_ in range(4, 6):
                                    sel_cp(h + 1, gq_)
                            if pi == 5:
                                for gq_ in range(6, 8):
                                    sel_cp(h + 1, gq_)
                                bmfold(h + 1)

            @block.scalar
            def _(a):
                a.wait_ge(s_c, 16 * 5)
                for hd in heads:
                    h = hd["h"]; hb = h % 2
                    for jb in hd["jobs"]:
                        g_ = jb["g"]
                        if g_ >= 3:
                            a.wait_ge(s_pv, g_ - 2)
                        a.wait_ge(s_qk, g_ + 1)
                        a.activation(out=PTg[g_ % 3][:], in_=SBg[g_ % NSB][:, :], func=AF.Exp).then_inc(s_exp, 1)


def phase4(nc, env):
    sem = env["sem"]; identb = env["identb"]
    WOd = env["WOd"]; WGd = env["WGd"]; WPd = env["WPd"]; gfin = env["gfin"]
    xv = env["xv"]; pown = env["pown"]; MT = env["MT"]; out_d = env["out_d"]
    NT = 32
    with ExitStack() as e4:
        sb, ps = _mk(nc, e4)
        p4 = env["pre4"]
        WO = p4["WO"]; WG = p4["WG"]; WP = p4["WP"]; GF = p4["GF"]
        s_xs1d = sem("p4_xs1d")
        xo = [sb(f"f_xo{i}", [128, D], F32) for i in range(2)]
        pt = [sb(f"f_pt{i}", [128, 256], F32) for i in range(2)]
        mt = [sb(f"f_mt{i}", [128, 8, 128], BF16) for i in range(2)]
        x1 = [sb(f"f_x1{i}", [128, D], F32) for i in range(2)]
        xs1 = sb("f_xs1", [128, D], BF16)
        pb = sb("f_pb", [128, 256], BF16)
        xs1T = sb("f_xs1T", [128, 8, 128], BF16)
        pT = sb("f_pT", [128, 2, 128], BF16)
        sg = sb("f_sg", [128, D], F32)
        tq = sb("f_tq", [128, D], F32)
        x2 = sb("f_x2", [128, D], F32)
        ob = [sb(f"f_ob{i}", [128, D], F32) for i in range(2)]
        junk = sb("f_junk", [128, D], BF16)
        ssq1 = sb("f_ssq1", [128, 1], F32); sq1 = sb("f_sq1", [128, 1], F32); r1 = sb("f_r1", [128, 1], F32)
        ssq2 = sb("f_ssq2", [128, 1], F32); sq2 = sb("f_sq2", [128, 1], F32); r2 = sb("f_r2", [128, 1], F32)
        Y = [ps(f"f_Y{i}", [128, 512]) for i in range(2)]
        G = [ps(f"f_G{i}", [128, 512]) for i in range(2)]
        PP = [ps(f"f_PP{i}", [128, 512]) for i in range(2)]
        TPx = ps("f_TPx", [128, 8, 128], BF16)
        TPp = ps("f_TPp", [128, 2, 128], BF16)
        s_ld = [sem("p4_ld0"), sem("p4_ld1")]; s_A = sem("p4_A"); s_x1 = sem("p4_x1"); s_acc1 = sem("p4_acc1")
        s_xs1 = sem("p4_xs1"); s_tr = sem("p4_tr"); s_cp = sem("p4_cp"); s_G = sem("p4_G")
        s_PP = sem("p4_PP"); s_sg = sem("p4_sg"); s_t = sem("p4_t"); s_x2 = sem("p4_x2"); s_acc2 = sem("p4_acc2")
        s_o = sem("p4_o"); s_od = [sem("p4_od0"), sem("p4_od1")]

        s_p1 = sem("p4_p1"); s_p2 = sem("p4_p2"); s_gq = sem("p4_gq")
        mhalf = env["mhalf"]
        ms1 = sb("f_ms1", [128, 1], F32); ms2 = sb("f_ms2", [128, 1], F32)

        with nc.Block() as block:
            @block.gpsimd
            def _(g):
                def loads(T):
                    if T >= 2:
                        g.wait_ge(s_x1, T - 1)
                        g.wait_ge(s_xs1, T - 1)
                        g.wait_ge(s_A, T - 1)
                    i, u = T // 2, T % 2
                    r0 = 512 * i + 128 * u
                    g.dma_start(out=xo[T % 2][:], in_=xv[r0:r0 + 128, :]).then_inc(s_ld[T % 2], 16)
                    g.dma_start(out=pt[T % 2][:], in_=pown[T * 128:(T + 1) * 128, :]).then_inc(s_ld[T % 2], 16)
                    g.dma_start(out=mt[T % 2][:], in_=MT[:, T * 128:(T + 1) * 128].rearrange("(c p) t -> p c t", p=128)
                                ).then_inc(s_ld[T % 2], 16)

                gq = [0]

                def pow1(T):
                    g.wait_ge(s_acc1, T + 1)
                    g.tensor_scalar(out=ms1[:], in0=ssq1[:], scalar1=1.0 / D, scalar2=1e-6, op0=ALU.mult, op1=ALU.add
                                    ).then_inc(s_gq, 1)
                    gq[0] += 1
                    g.wait_ge(s_gq, gq[0])
                    g.tensor_tensor(out=r1[:], in0=ms1[:], in1=mhalf[:], op=ALU.pow).then_inc(s_p1, 1)

                def pow2(T):
                    g.wait_ge(s_acc2, T + 1)
                    g.tensor_scalar(out=ms2[:], in0=ssq2[:], scalar1=1.0 / D, scalar2=1e-6, op0=ALU.mult, op1=ALU.add
                                    ).then_inc(s_gq, 1)
                    gq[0] += 1
                    g.wait_ge(s_gq, gq[0])
                    g.tensor_tensor(out=r2[:], in0=ms2[:], in1=mhalf[:], op=ALU.pow).then_inc(s_p2, 1)

                loads(0); loads(1)
                pow1(0)
                for T in range(NT):
                    if T + 2 < NT:
                        loads(T + 2)
                    if T >= 1:
                        pow2(T - 1)
                    if T + 1 < NT:
                        pow1(T + 1)
                pow2(NT - 1)

            @block.tensor
            def _(t):
                def A(T):
                    t.wait_ge(s_ld[T % 2], 48 * (T // 2 + 1))
                    if T >= 1:
                        t.wait_ge(s_x1, T)
                    for half in range(2):
                        for c in range(8):
                            ins = t.matmul(Y[half][:, :], mt[T % 2][:, c, :], WO[:, c, half * 512:(half + 1) * 512],
                                           start=(c == 0), stop=(c == 7))
                    ins.then_inc(s_A, 1)

                def TR(T):
                    t.wait_ge(s_xs1, T + 1)
                    t.wait_ge(s_xs1d, T + 1)
                    if T >= 1:
                        t.wait_ge(s_cp, T)
                    for c in range(8):
                        t.transpose(out=TPx[:, c, :], in_=xs1[:, c * 128:(c + 1) * 128], identity=identb[:])
                    for c in range(2):
                        ins = t.transpose(out=TPp[:, c, :], in_=pb[:, c * 128:(c + 1) * 128], identity=identb[:])
                    ins.then_inc(s_tr, 1)

                def GM(T):
                    t.wait_ge(s_cp, T + 1)
                    if T >= 1:
                        t.wait_ge(s_t, T)
                    for half in range(2):
                        for c in range(8):
                            ins = t.matmul(G[half][:, :], xs1T[:, c, :], WG[:, c, half * 512:(half + 1) * 512],
                                           start=(c == 0), stop=(c == 7))
                    ins.then_inc(s_G, 1)
                    for half in range(2):
                        for c in range(2):
                            ins = t.matmul(PP[half][:, :], pT[:, c, :], WP[:, c, half * 512:(half + 1) * 512],
                                           start=(c == 0), stop=(c == 1))
                    ins.then_inc(s_PP, 1)

                A(0)
                for T in range(NT):
                    if T + 1 < NT:
                        A(T + 1)
                    TR(T)
                    GM(T)

            @block.vector
            def _(v):
                def x1f(T):
                    v.wait_ge(s_A, T + 1)
                    v.wait_ge(s_ld[T % 2], 48 * (T // 2 + 1))
                    for half in range(2):
                        ins = v.tensor_tensor(out=x1[T % 2][:, half * 512:(half + 1) * 512], in0=Y[half][:, :],
                                              in1=xo[T % 2][:, half * 512:(half + 1) * 512], op=ALU.add)
                    ins.then_inc(s_x1, 1)

                def xs1half(T):
                    v.wait_ge(s_p1, T + 1)
                    if T >= 1:
                        v.wait_ge(s_tr, T)
                    v.tensor_scalar(out=xs1[:, 512:1024], in0=x1[T % 2][:, 512:1024], scalar1=r1[:, 0:1], scalar2=None,
                                    op0=ALU.mult).then_inc(s_xs1d, 1)

                def copies(T):
                    v.wait_ge(s_tr, T + 1)
                    v.tensor_copy(out=xs1T[:], in_=TPx[:, :, :])
                    v.tensor_copy(out=pT[:], in_=TPp[:, :, :]).then_inc(s_cp, 1)

                def tail(T):
                    v.wait_ge(s_PP, T + 1)
                    v.wait_ge(s_sg, T + 1)
                    for half in range(2):
                        ins = v.tensor_tensor(out=tq[:, half * 512:(half + 1) * 512], in0=PP[half][:, :],
                                              in1=sg[:, half * 512:(half + 1) * 512], op=ALU.mult)
                    ins.then_inc(s_t, 1)
                    v.tensor_tensor(out=x2[:], in0=tq[:], in1=x1[T % 2][:], op=ALU.add).then_inc(s_x2, 1)

                def outf(T):
                    v.wait_ge(s_p2, T + 1)
                    if T >= 2:
                        v.wait_ge(s_od[T % 2], 16 * (T // 2))
                    v.scalar_tensor_tensor(out=ob[T % 2][:], in0=x2[:], scalar=r2[:, 0:1], in1=GF[:],
                                           op0=ALU.mult, op1=ALU.mult).then_inc(s_o, 1)

                x1f(0)
                xs1half(0)
                for T in range(NT):
                    if T >= 1:
                        tail(T - 1)
                    if T + 1 < NT:
                        x1f(T + 1)
                    copies(T)
                    if T >= 1:
                        outf(T - 1)
                    if T + 1 < NT:
                        xs1half(T + 1)
                tail(NT - 1)
                outf(NT - 1)

            @block.scalar
            def _(a):
                def n1(T):
                    a.wait_ge(s_x1, T + 1)
                    a.activation(out=junk[:], in_=x1[T % 2][:], func=AF.Square, accum_out=ssq1[:, 0:1]).then_inc(s_acc1, 1)
                    if T >= 1:
                        a.wait_ge(s_tr, T)
                    a.activation(out=pb[:], in_=pt[T % 2][:], func=AF.Copy)
                    a.wait_ge(s_p1, T + 1)
                    a.activation(out=xs1[:, 0:512], in_=x1[T % 2][:, 0:512], func=AF.Copy, scale=r1[:, 0:1]
                                 ).then_inc(s_xs1, 1)

                def sig(T):
                    a.wait_ge(s_G, T + 1)
                    if T >= 1:
                        a.wait_ge(s_t, T)
                    for half in range(2):
                        ins = a.activation(out=sg[:, half * 512:(half + 1) * 512], in_=G[half][:, :], func=AF.Sigmoid)
                    ins.then_inc(s_sg, 1)

                def n2(T):
                    a.wait_ge(s_x2, T + 1)
                    a.activation(out=junk[:], in_=x2[:], func=AF.Square, accum_out=ssq2[:, 0:1]).then_inc(s_acc2, 1)

                n1(0)
                for T in range(NT):
                    if T >= 1:
                        sig(T - 1)
                        n2(T - 1)
                    if T + 1 < NT:
                        n1(T + 1)
                sig(NT - 1)
                n2(NT - 1)

            @block.sync
            def _(sync):
                for T in range(NT):
                    sync.wait_ge(s_o, T + 1)
                    sync.dma_start(out=out_d[T * 128:(T + 1) * 128, :], in_=ob[T % 2][:]).then_inc(s_od[T % 2], 16)
                sync.wait_ge(s_od[0], 16 * (NT // 2))
                sync.wait_ge(s_od[1], 16 * (NT // 2))


def _rel_bucket(dist):
    n = np.maximum(dist, 0)
    max_exact = 16
    nf = np.maximum(n, 1).astype(np.float32)
    val = (np.log(nf / np.float32(max_exact)) / np.float32(np.log(1024.0 / 16.0))) * np.float32(16)
    large = max_exact + val.astype(np.int32)
    large = np.minimum(large, 31)
    return np.where(n < max_exact, n, large).astype(np.int64)


def _core_tables(r, rel_bias):
    tab = np.asarray(rel_bias, np.float32)
    k = np.arange(128)[:, None]
    q = np.arange(128)[None, :]
    d0 = q - k
    diag = np.where((d0 >= 0)[:, None, :], tab[_rel_bucket(d0)][:, :, :8].transpose(0, 2, 1), np.float32(NEG))
    d1 = 128 + q - k
    prev = np.where((d1 < 128)[:, None, :], tab[_rel_bucket(d1)][:, :, :8].transpose(0, 2, 1), np.float32(NEG))
    masked = np.full_like(prev, NEG)
    bswa = np.stack([diag, prev, prev if r == 0 else masked, masked if r == 0 else prev], axis=1)
    bm = np.full((8, 128, 10, 2, 256), NEG, np.float32)
    qq = np.arange(256)[None, :]
    for e in range(8):
        delta = (e + 2 * r - 1) if e % 2 == 0 else (e - 1)
        for ks in range(2):
            dist = delta * 256 + qq - (ks * 128 + k)
            vals = tab[_rel_bucket(dist)][:, :, 8:]
            vals = np.where((dist >= 0)[:, :, None], vals, np.float32(NEG))
            bm[:, :, e + 2, ks, :] = vals.transpose(2, 0, 1)
    cfar = np.broadcast_to(tab[31, 8:][None, :], (128, 8))
    pm = np.full((32, 32), NEGBIG, np.float32)
    oz = np.ones((32, 32), np.float32)
    for T in range(32):
        i = T // 2
        for vb in range(32):
            past = (vb <= 2 * i - 1) or (r == 1 and vb == 2 * i + 1)
            if past:
                pm[T, vb] = 0.0
        oz[T, 2 * i] = 0.0
    pm = np.broadcast_to(pm[None], (128, 32, 32))
    oz = np.broadcast_to(oz[None], (128, 32, 32))
    return dict(bswa=np.ascontiguousarray(bswa, np.float32), bmoba=np.ascontiguousarray(bm),
                cfar=np.ascontiguousarray(cfar, np.float32), pm=np.ascontiguousarray(pm),
                oz=np.ascontiguousarray(oz))


def _virt_perm(r):
    return np.array([2 * (v // 2) + ((v % 2) ^ r) for v in range(32)])


def make_in_maps(x, p, norm_in, w_in, sinks, rel_bias, w_out, ple_norm, w_ple_gate, w_ple_proj, final_norm):
    f = lambda a: np.ascontiguousarray(np.asarray(a, dtype=np.float32))
    x = f(x); p = f(p); w_in = f(w_in)[0]; w_out = f(w_out)[0]; w_gate = f(w_ple_gate)[0]; w_ple = f(w_ple_proj)[0]
    norm_in = f(norm_in)[0]; ple_norm = f(ple_norm)[0]; final_norm = f(final_norm); sinks = f(sinks)[0]
    onehot = np.zeros((32, S), ml_dtypes.bfloat16)
    for n in range(32):
        onehot[n, n * 256:(n + 1) * 256] = 1.0
    shared = dict(
        w_in=w_in, w_out=w_out, w_gate=w_gate, w_ple=w_ple,
        gin=np.ascontiguousarray(norm_in.reshape(8, 128).T), gple=np.ascontiguousarray(ple_norm.reshape(8, 128).T),
        gfin=np.ascontiguousarray(np.broadcast_to(final_norm[None, :], (128, D))),
        sinkb=np.ascontiguousarray(np.broadcast_to(sinks[None, :], (128, 8))),
        onehot=onehot,
    )
    tabs = [_core_tables(r, rel_bias) for r in range(2)]
    maps = []
    for c in range(8):
        b, r = c // 2, c % 2
        perm = _virt_perm(r)
        xb = x[b].reshape(32, 256, D)
        xvv = np.ascontiguousarray(xb[perm].reshape(S, D))
        own = np.array([2 * i + r for i in range(16)])
        pw = np.ascontiguousarray(p[0, b].reshape(32, 256, 256)[own].reshape(OWN, 256))
        m = dict(shared)
        m.update(tabs[r])
        m["xv"] = xvv
        m["pown"] = pw
        maps.append(m)
    return maps


_NC_CACHE = {}


def kernel(x, p, norm_in, w_in, sinks, rel_bias, w_out, ple_norm, w_ple_gate, w_ple_proj, final_norm):
    maps = make_in_maps(x, p, norm_in, w_in, sinks, rel_bias, w_out, ple_norm, w_ple_gate, w_ple_proj, final_norm)
    if "nc" not in _NC_CACHE:
        _NC_CACHE["nc"] = build_nc()
    nc = _NC_CACHE["nc"]
    res = run_bass_kernel_spmd(nc, maps, core_ids=list(range(8)))
    if DEBUG:
        return res
    out = np.empty((4, 32, 256, D), np.float32)
    for c in range(8):
        b, r = c // 2, c % 2
        o = np.asarray(res.results[c]["out"], np.float32).reshape(16, 256, D)
        for i in range(16):
            out[b, 2 * i + r] = o[i]
    return out.reshape(4, S, D)
```

```python
import numpy as np
import ml_dtypes
from contextlib import ExitStack
import concourse.bass as bass
import concourse.mybir as mybir
from concourse.bass_utils import run_bass_kernel_spmd

F32 = mybir.dt.float32
BF16 = mybir.dt.bfloat16
AF = mybir.ActivationFunctionType
ALU = mybir.AluOpType
AX = mybir.AxisListType

S = 8192
D = 1024
NPAIR = 16
OWN = 4096
INC = 3328
NEG = -30000.0
NEGBIG = -1.0e30
DEBUG = False
STOP_AFTER = 99


def build_nc():
    nc = bass.Bass("TRN2", target_bir_lowering=False)
    dbg_kind = "ExternalOutput" if DEBUG else "Internal"

    def din(name, shape, dt=F32):
        return nc.dram_tensor(name, list(shape), dt, kind="ExternalInput").ap()

    def dscr(name, shape, dt=BF16):
        return nc.dram_tensor(name, list(shape), dt, kind=dbg_kind).ap()

    xv = din("xv", [S, D])
    pown = din("pown", [OWN, 256])
    w_in = din("w_in", [D, INC])
    w_out = din("w_out", [D, D])
    w_gate = din("w_gate", [D, D])
    w_ple = din("w_ple", [256, D])
    gin = din("gin", [128, 8])
    gple = din("gple", [128, 8])
    gfin = din("gfin", [128, D])
    sinkb = din("sinkb", [128, 8])
    cfar_d = din("cfar", [128, 8])
    bswa_d = din("bswa", [128, 4, 8, 128])
    bmoba_d = din("bmoba", [8, 128, 10, 2, 256])
    pm_d = din("pm", [128, 32, 32])
    oz_d = din("oz", [128, 32, 32])
    onehot_d = din("onehot", [32, S], BF16)
    out_d = nc.dram_tensor("out", [OWN, D], F32, kind="ExternalOutput").ap()

    AKT = dscr("AKT", [128, S])
    AV = dscr("AV", [S, 128])
    BKT = dscr("BKT", [512, S])
    BV = dscr("BV", [S, 512])
    AQT = dscr("AQT", [512, OWN])
    BQT = dscr("BQT", [512, OWN])
    GT = dscr("GT", [1024, OWN])
    KMT = dscr("KMT", [512, 32])
    MT = dscr("MT", [1024, OWN])
    WOd = nc.dram_tensor("WOd", [D, D], BF16, kind="Internal").ap()
    WGd = nc.dram_tensor("WGd", [D, D], BF16, kind="Internal").ap()
    WPd = nc.dram_tensor("WPd", [256, D], BF16, kind="Internal").ap()

    with ExitStack() as es:
        def sb(name, shape, dt):
            return es.enter_context(nc.sbuf_tensor(name, list(shape), dt))

        sem_log = []

        def sem(name):
            h = nc.alloc_semaphore(name=name)
            sem_log.append(h)
            return h

        def sems_since(mark, keep=()):
            out = [h for h in sem_log[mark:] if all(h is not k for k in keep)]
            return out

        identb = sb("identb", [128, 128], BF16)
        identf = sb("identf", [128, 128], F32)
        ones_f = sb("ones_f", [128, 64], F32)
        gin_s = sb("gin_s", [128, 8], F32)
        gple_s = sb("gple_s", [128, 8], F32)
        s_wp = sem("s_wp")
        mhalf = sb("mhalf", [128, 1], F32)

        with ExitStack() as es1:
            W1 = es1.enter_context(nc.sbuf_tensor("W1", [128, 8, INC], BF16))

            with ExitStack() as es0:
                wst = [es0.enter_context(nc.sbuf_tensor(f"wst{i}", [128, INC], F32)) for i in range(2)]
                s_wl = [sem("s_wl0"), sem("s_wl1")]
                s_wcA = sem("s_wcA")
                s_wcD = sem("s_wcD")
                s_misc = sem("s_misc")
                s_g0 = sem("s_g0")
                chunks = []
                for kc in range(8):
                    chunks.append((w_in[kc * 128:(kc + 1) * 128, :], INC, W1[:, kc, :], gin_s[:, kc:kc + 1]))
                NCH = len(chunks)
                with nc.Block() as block:
                    @block.sync
                    def _(sync):
                        sync.dma_start(out=gin_s[:], in_=gin[:, :]).then_inc(s_misc, 16)
                        sync.dma_start(out=gple_s[:], in_=gple[:, :]).then_inc(s_misc, 16)
                        for n, (src, wd, dst, sc) in enumerate(chunks):
                            if n >= 2:
                                k = n // 2
                                if n % 2 == 0:
                                    sync.wait_ge(s_wcD, k)
                                else:
                                    sync.wait_ge(s_wcA, k)
                            sync.dma_start(out=wst[n % 2][:, 0:wd], in_=src).then_inc(s_wl[n % 2], 16)
                        sync.wait_ge(s_wl[0], 16 * (NCH // 2))
                        sync.wait_ge(s_wl[1], 16 * (NCH // 2))
                        sync.wait_ge(s_misc, 32)

                    @block.vector
                    def _(v):
                        v.wait_ge(s_misc, 32)
                        for n, (src, wd, dst, sc) in enumerate(chunks):
                            if n % 2 != 0:
                                continue
                            v.wait_ge(s_wl[0], 16 * (n // 2 + 1))
                            if sc is None:
                                v.tensor_copy(out=dst, in_=wst[0][:, 0:wd]).then_inc(s_wcD, 1)
                            else:
                                v.tensor_scalar(out=dst, in0=wst[0][:, 0:wd], scalar1=sc, scalar2=None,
                                                op0=ALU.mult).then_inc(s_wcD, 1)

                    @block.scalar
                    def _(a):
                        a.wait_ge(s_misc, 32)
                        for n, (src, wd, dst, sc) in enumerate(chunks):
                            if n % 2 != 1:
                                continue
                            a.wait_ge(s_wl[1], 16 * (n // 2 + 1))
                            if sc is None:
                                a.activation(out=dst, in_=wst[1][:, 0:wd], func=AF.Copy).then_inc(s_wcA, 1)
                            else:
                                a.activation(out=dst, in_=wst[1][:, 0:wd], func=AF.Copy, scale=sc).then_inc(s_wcA, 1)

                    @block.gpsimd
                    def _(g):
                        g.memset(identf[:], 0.0).then_inc(s_g0, 1)
                        g.wait_ge(s_g0, 1)
                        g.affine_select(out=identf[:], in_=identf[:], pattern=[[-1, 128]],
                                        compare_op=ALU.not_equal, fill=1.0, base=0, channel_multiplier=1).then_inc(s_g0, 1)
                        g.memset(ones_f[:], 1.0)
                        g.memset(mhalf[:], -0.5)
                        g.wait_ge(s_g0, 2)
                        g.tensor_copy(out=identb[:], in_=identf[:])

            free_in_p1 = [s_wl[0], s_wl[1], s_wcA, s_wcD, s_misc, s_g0]
            p1_mark = len(sem_log)
            phase1(nc, es1, locals())
            free_in_p2 = sems_since(p1_mark)

        with ExitStack() as es23:
            def sb23(name, shape, dt):
                return es23.enter_context(nc.sbuf_tensor(name, list(shape), dt))
            pre = dict(
                KP0=sb23("KP0", [96, S], BF16), VP0=sb23("VP0", [128, 64, 65], BF16), QP0=sb23("QP0", [96, OWN], BF16),
                BM0=sb23("BM0", [128, 10, 2, 256], BF16), KM0=sb23("KM0", [64, 32], BF16),
                pm_s=sb23("pm_s", [128, 32, 32], F32), oz_s=sb23("oz_s", [128, 32, 32], F32),
                cfar_s=sb23("cfar_s", [128, 8], F32), NM=sb23("NM", [128, 32, 96], BF16),
                ones_b=sb23("ones_b", [128, 64], BF16),
                s_c=sem("p3_c"), s_hl0=sem("p3_hl0"), s_bml0=sem("p3_bml0"),
            )
            if STOP_AFTER >= 2:
                phase2(nc, locals())
            with ExitStack() as es34:
                def sb34(name, shape, dt):
                    return es34.enter_context(nc.sbuf_tensor(name, list(shape), dt))
                pre4 = dict(WO=sb34("WO", [128, 8, D], BF16), WG=sb34("WG", [128, 8, D], BF16),
                            WP=sb34("WP", [128, 2, D], BF16), GF=sb34("GF", [128, D], F32),
                            s_w4=sem("p4_w"), s_w4g=sem("p4_wg"), s_w4f=sem("p4_wf"))
                if STOP_AFTER >= 3:
                    phase3(nc, locals())
                if STOP_AFTER >= 4:
                    phase4(nc, locals())
    return nc


def phase1(nc, es1, env):
    xv = env["xv"]; W1 = env["W1"]; identb = env["identb"]
    AKT = env["AKT"]; AV = env["AV"]; BKT = env["BKT"]; BV = env["BV"]
    AQT = env["AQT"]; BQT = env["BQT"]; GT = env["GT"]; KMT = env["KMT"]

    def sb(name, shape, dt):
        return es1.enter_context(nc.sbuf_tensor(name, list(shape), dt))

    def ps(name, shape, dt):
        return es1.enter_context(nc.psum_tensor(name, list(shape), dt))

    sem = env["sem"]

    xin = [sb(f"xin{i}", [128, 4, D], F32) for i in range(2)]
    xs = [sb(f"xs{i}", [128, 4, D], BF16) for i in range(2)]
    xT = [sb(f"xT{i}", [128, 8, 512], BF16) for i in range(2)]
    junk = sb("junk", [128, D], BF16)
    ssq = [sb(f"ssq{i}", [128, 4], F32) for i in range(2)]
    sq = [sb(f"sq{i}", [128, 4], F32) for i in range(2)]
    rr = [sb(f"rr{i}", [128, 4], F32) for i in range(2)]
    NS = 3
    stgA = [sb(f"stgA{i}", [128, 512], BF16) for i in range(NS)]
    stgD = [sb(f"stgD{i}", [128, 512], BF16) for i in range(NS)]
    ksum = sb("ksum", [128, 4, 32], F32)
    kmb = sb("kmb", [128, 4, 32], BF16)
    NB = 5
    PB = [ps(f"PB{i}", [128, 512], F32) for i in range(NB)]
    TP = [ps(f"TP{i}", [128, 8, 128], BF16) for i in range(2)]

    s_xl = [sem("p1_xl0"), sem("p1_xl1")]; s_sqd = sem("p1_sqd"); s_rrd = sem("p1_rrd"); s_xsd = sem("p1_xsd")
    s_trd = sem("p1_trd"); s_xtc = sem("p1_xtc"); s_ped = sem("p1_ped")
    s_evA = sem("p1_evA"); s_evD = sem("p1_evD"); s_odA = [sem(f"p1_odA{i}") for i in range(3)]; s_odD = [sem(f"p1_odD{i}") for i in range(3)]
    s_km = sem("p1_km"); s_kmo = sem("p1_kmo"); s_acc = sem("p1_acc")
    s_wp = env["s_wp"]; s_wgl = [sem("p1_wgl0"), sem("p1_wgl1")]; s_wgo = [sem("p1_wgo0"), sem("p1_wgo1")]; s_wgc = sem("p1_wgc")
    w_out = env["w_out"]; w_gate = env["w_gate"]; w_ple = env["w_ple"]; gple_s = env["gple_s"]
    WOd = env["WOd"]; WGd = env["WGd"]; WPd = env["WPd"]
    wgst = [sb(f"wgst{i}", [128, D], F32) for i in range(2)]
    wgo = [sb(f"wgo{i}", [128, D], BF16) for i in range(2)]

    jobs = []
    cnt = {"A": 0, "D": 0}

    def add(j, **kw):
        e = kw["eng"]
        kw["k"] = cnt[e]; cnt[e] += 1
        kw["pair"] = j; kw["n"] = len(jobs)
        jobs.append(kw)

    for j in range(NPAIR):
        t0 = 512 * j
        o0 = 256 * j
        add(j, typ="fm", sec="kv", eng="D", N=512, tok=0, wc=512, dst=AKT[:, t0:t0 + 512], op="copy", ks=None)
        for c in range(4):
            add(j, typ="fm", sec="kv", eng="D", N=512, tok=0, wc=1792 + 128 * c,
                dst=BKT[128 * c:128 * (c + 1), t0:t0 + 512], op="copy", ks=c)
        for tt in range(4):
            add(j, typ="tm", sec="kv", eng="D", N=512, tt=tt, wc=2304,
                dst=BV[t0 + 128 * tt:t0 + 128 * (tt + 1), :], op="copy", ks=None)
            add(j, typ="tm", sec="kv", eng="D", N=128, tt=tt, wc=640,
                dst=AV[t0 + 128 * tt:t0 + 128 * (tt + 1), :], op="copy", ks=None)
        for c in range(4):
            add(j, typ="fm", sec="qg", eng="D", N=256, tok=0, wc=128 * c,
                dst=AQT[128 * c:128 * (c + 1), o0:o0 + 256], op="qscale", ks=None)
            add(j, typ="fm", sec="qg", eng="A", N=256, tok=0, wc=768 + 128 * c,
                dst=GT[128 * c:128 * (c + 1), o0:o0 + 256], op="silu", ks=None)
        for c in range(4):
            add(j, typ="fm", sec="qg", eng="D", N=256, tok=0, wc=1280 + 128 * c,
                dst=BQT[128 * c:128 * (c + 1), o0:o0 + 256], op="qscale", ks=None)
            add(j, typ="fm", sec="qg", eng="A", N=256, tok=0, wc=2816 + 128 * c,
                dst=GT[512 + 128 * c:512 + 128 * (c + 1), o0:o0 + 256], op="silu", ks=None)
    NJ = len(jobs)
    JPP = NJ // NPAIR
    evsem = {"A": s_evA, "D": s_evD}
    odsem = {"A": s_odA, "D": s_odD}
    stg = {"A": stgA, "D": stgD}

    with nc.Block() as block:
        @block.gpsimd
        def _(g):
            nc.clear_and_free_semaphores(env["free_in_p1"])
            for j in range(NPAIR):
                if j >= 2:
                    g.wait_ge(s_xsd, j - 1)
                g.dma_start(out=xin[j % 2][:], in_=xv[512 * j:512 * (j + 1), :].rearrange("(tt p) d -> p tt d", p=128)
                            ).then_inc(s_xl[j % 2], 16)
                if j == 1:
                    g.dma_start(out=WOd[:, :], in_=w_out[:, :], max_dma_last_dim=4096).then_inc(s_wp, 16)
                    g.dma_start(out=WPd[:, :], in_=w_ple[:, :], max_dma_last_dim=4096).then_inc(s_wp, 16)
                if 2 <= j < 10:
                    kc = j - 2
                    g.dma_start(out=wgst[kc % 2][:], in_=w_gate[kc * 128:(kc + 1) * 128, :]).then_inc(s_wgl[kc % 2], 16)
                    g.wait_ge(s_wgl[kc % 2], 16 * (kc // 2 + 1))
                    if kc >= 2:
                        g.wait_ge(s_wgo[kc % 2], 16 * (kc // 2))
                    g.tensor_scalar(out=wgo[kc % 2][:], in0=wgst[kc % 2][:], scalar1=gple_s[:, kc:kc + 1], scalar2=0.0,
                                    op0=ALU.mult, op1=ALU.add).then_inc(s_wgc, 1)
                    g.wait_ge(s_wgc, kc + 1)
                    g.dma_start(out=WGd[kc * 128:(kc + 1) * 128, :], in_=wgo[kc % 2][:]).then_inc(s_wgo[kc % 2], 16)
            g.wait_ge(s_wgo[0], 16 * 4)
            g.wait_ge(s_wgo[1], 16 * 4)
            g.wait_ge(s_wp, 32)

        @block.tensor
        def _(t):
            def transposes(j):
                for tt in range(4):
                    n_t = 4 * j + tt
                    if tt == 0:
                        t.wait_ge(s_xsd, j + 1)
                    if n_t >= 2:
                        t.wait_ge(s_xtc, n_t - 1)
                    for kc in range(8):
                        ins = t.transpose(out=TP[n_t % 2][:, kc, :], in_=xs[j % 2][:, tt, kc * 128:(kc + 1) * 128],
                                          identity=identb[:])
                    ins.then_inc(s_trd, 1)

            def run_job(jb):
                n = jb["n"]; j = jb["pair"]
                if n >= NB:
                    pj = jobs[n - NB]
                    t.wait_ge(evsem[pj["eng"]], pj["k"] + 1)
                bank = PB[n % NB]
                for kc in range(8):
                    if jb["typ"] == "fm":
                        ins = t.matmul(bank[:, 0:jb["N"]], W1[:, kc, jb["wc"]:jb["wc"] + 128],
                                       xT[j % 2][:, kc, jb["tok"]:jb["tok"] + jb["N"]],
                                       start=(kc == 0), stop=(kc == 7))
                    else:
                        tt = jb["tt"]
                        ins = t.matmul(bank[:, 0:jb["N"]], xT[j % 2][:, kc, tt * 128:(tt + 1) * 128],
                                       W1[:, kc, jb["wc"]:jb["wc"] + jb["N"]],
                                       start=(kc == 0), stop=(kc == 7))
                ins.then_inc(s_ped, 1)

            transposes(0)
            for j in range(NPAIR):
                pj = [jb for jb in jobs if jb["pair"] == j]
                t.wait_ge(s_xtc, 4 * (j + 1))
                for jb in pj:
                    if jb["sec"] == "kv":
                        run_job(jb)
                qg = [jb for jb in pj if jb["sec"] == "qg"]
                for jb in qg[:8]:
                    run_job(jb)
                if j + 1 < NPAIR:
                    transposes(j + 1)
                for jb in qg[8:]:
                    run_job(jb)

        def evac(eng, e, jb):
            n = jb["n"]; k = jb["k"]; N = jb["N"]
            eng.wait_ge(s_ped, n + 1)
            if k >= NS:
                eng.wait_ge(odsem[e][k % NS], 16 * (k // NS))
            bank = PB[n % NB]
            dst = stg[e][k % NS][:, 0:N]
            if jb["op"] == "silu":
                eng.activation(out=dst, in_=bank[:, 0:N], func=AF.Silu).then_inc(evsem[e], 1)
            elif jb["op"] == "qscale":
                eng.tensor_scalar(out=dst, in0=bank[:, 0:N], scalar1=0.125, scalar2=None,
                                  op0=ALU.mult).then_inc(evsem[e], 1)
            else:
                if jb["ks"] is not None:
                    c = jb["ks"]; j = jb["pair"]
                    eng.tensor_reduce(out=ksum[:, c, 2 * j:2 * j + 2],
                                      in_=bank[:, 0:512].rearrange("p (b k) -> p b k", k=256),
                                      axis=AX.X, op=ALU.add)
                eng.tensor_copy(out=dst, in_=bank[:, 0:N]).then_inc(evsem[e], 1)

        @block.vector
        def _(v):
            def recip(j):
                v.wait_ge(s_sqd, j + 1)
                v.reciprocal(out=rr[j % 2][:], in_=sq[j % 2][:]).then_inc(s_rrd, 1)

            def xtcopies(j):
                for tt in range(4):
                    n_t = 4 * j + tt
                    v.wait_ge(s_trd, n_t + 1)
                    if tt == 0 and j >= 2:
                        v.wait_ge(s_ped, JPP * (j - 1))
                    v.tensor_copy(out=xT[j % 2][:, :, tt * 128:(tt + 1) * 128], in_=TP[n_t % 2][:, :, :]
                                  ).then_inc(s_xtc, 1)

            recip(0)
            xtcopies(0)
            for j in range(NPAIR):
                pj = [jb for jb in jobs if jb["pair"] == j and jb["eng"] == "D"]
                if j + 1 < NPAIR:
                    recip(j + 1)
                for jb in pj:
                    if jb["sec"] == "kv":
                        evac(v, "D", jb)
                qgd = [jb for jb in pj if jb["sec"] == "qg"]
                for jb in qgd[:4]:
                    evac(v, "D", jb)
                if j + 1 < NPAIR:
                    xtcopies(j + 1)
                for jb in qgd[4:]:
                    evac(v, "D", jb)
            v.tensor_scalar(out=kmb[:], in0=ksum[:], scalar1=1.0 / 256.0, scalar2=None,
                            op0=ALU.mult).then_inc(s_km, 1)

        @block.scalar
        def _(a):
            def norm(j):
                a.wait_ge(s_xl[j % 2], 16 * (j // 2 + 1))
                for tt in range(4):
                    ins = a.activation(out=junk[:], in_=xin[j % 2][:, tt, :], func=AF.Square,
                                       accum_out=ssq[j % 2][:, tt:tt + 1])
                ins.then_inc(s_acc, 1)
                a.wait_ge(s_acc, j + 1)
                a.activation(out=sq[j % 2][:], in_=ssq[j % 2][:], func=AF.Sqrt, scale=1.0 / D, bias=1e-6
                             ).then_inc(s_sqd, 1)
                a.wait_ge(s_rrd, j + 1)
                if j >= 2:
                    a.wait_ge(s_trd, 4 * (j - 1))
                for tt in range(4):
                    ins = a.activation(out=xs[j % 2][:, tt, :], in_=xin[j % 2][:, tt, :], func=AF.Copy,
                                       scale=rr[j % 2][:, tt:tt + 1])
                ins.then_inc(s_xsd, 1)

            norm(0)
            for j in range(NPAIR):
                if j + 1 < NPAIR:
                    norm(j + 1)
                for jb in jobs:
                    if jb["pair"] == j and jb["eng"] == "A":
                        evac(a, "A", jb)

        @block.sync
        def _(sync):
            for jb in jobs:
                e = jb["eng"]; k = jb["k"]
                sync.wait_ge(evsem[e], k + 1)
                sync.dma_start(out=jb["dst"], in_=stg[e][k % NS][:, 0:jb["N"]]).then_inc(odsem[e][k % NS], 16)
            sync.wait_ge(s_km, 1)
            sync.dma_start(out=KMT.rearrange("(c p) n -> p c n", p=128), in_=kmb[:]).then_inc(s_kmo, 16)
            for e_ in ("A", "D"):
                for sl_ in range(NS):
                    sync.wait_ge(odsem[e_][sl_], 16 * len(range(sl_, cnt[e_], NS)))
            sync.wait_ge(s_kmo, 16)


def _mk(nc, est):
    def sb(name, shape, dt):
        return est.enter_context(nc.sbuf_tensor(name, list(shape), dt))

    def ps(name, shape, dt=F32):
        return est.enter_context(nc.psum_tensor(name, list(shape), dt))
    return sb, ps


def phase2(nc, env):
    sem = env["sem"]; ones_f = env["ones_f"]
    AKT = env["AKT"]; AV = env["AV"]; AQT = env["AQT"]; GT = env["GT"]; MT = env["MT"]
    bswa_d = env["bswa_d"]; sinkb = env["sinkb"]
    pre = env["pre"]; BKT = env["BKT"]; BV = env["BV"]; BQT = env["BQT"]; KMT = env["KMT"]
    bmoba_d = env["bmoba_d"]; pm_d = env["pm_d"]; oz_d = env["oz_d"]; onehot_d = env["onehot_d"]; cfar_d = env["cfar_d"]
    with ExitStack() as e2:
        sb, ps = _mk(nc, e2)
        AQs = sb("AQs", [128, 4, OWN], BF16)
        AKs = sb("AKs", [128, S], BF16)
        AVs = sb("AVs", [128, 64, 2, 65], BF16)
        bsw = sb("bsw", [128, 4, 8, 128], F32)
        esk = sb("esk", [128, 8], F32)
        esk_rep = sb("esk_rep", [1, 8, 128], F32)
        esk_hi = sb("esk_hi", [1, 8, 128], BF16)
        esk_lo = sb("esk_lo", [1, 8, 128], BF16)
        sel65 = sb("sel65", [1, 65], BF16)
        bcs = [sb(f"s_bcs{i}", [64, 512], F32) for i in range(4)]
        lnr = sb("s_lnr", [128, 512], F32)
        tmp = [sb(f"s_tmp{i}", [128, 4, 128], F32) for i in range(2)]
        PTb = [sb(f"s_pt{i}", [128, 512], BF16) for i in range(3)]
        rrow = [sb(f"s_rrow{i}", [128, 512], F32) for i in range(4)]
        t1b = [sb(f"s_t1{i}", [64, 512], F32) for i in range(4)]
        mst = [sb(f"s_mst{i}", [64, 4, 128], BF16) for i in range(4)]
        gsw = [sb(f"s_g{i}", [64, 4, 128], BF16) for i in range(4)]
        SP = [ps(f"s_SP{i}", [128, 512]) for i in range(4)]
        OP = [ps(f"s_OP{i}", [128, 512]) for i in range(2)]
        s_ld = sem("p2_ld"); s_ldc = [sem(f"p2_ldc{i}") for i in range(4)]; s_qk = sem("p2_qk"); s_add = sem("p2_add"); s_exp = sem("p2_exp"); s_pv = sem("p2_pv")
        s_rr = sem("p2_rr"); s_t1 = sem("p2_t1"); s_bc = sem("p2_bc"); s_mx = sem("p2_mx"); s_mo = [sem(f"p2_mo{i}") for i in range(4)]
        s_gl = [sem(f"p2_gl{i}") for i in range(4)]; s_es = sem("p2_es"); s_ms = sem("p2_ms"); s_g2 = sem("p2_g2"); s_er = sem("p2_er"); s_bcs = sem("p2_bcs")

        NU = 64
        units = []
        q = 0
        for m in range(NU):
            T, kvh = m // 2, m % 2
            i, u = T // 2, T % 2
            if u == 0:
                cl = ([(4 * (i - 1) + 3, 2)] if i > 0 else []) + [(4 * i + 3, 3), (4 * i, 0)]
            else:
                cl = [(4 * i, 1), (4 * i + 1, 0)]
            cands = []
            for (kt, kind) in cl:
                cands.append(dict(q=q, kt=kt, kind=kind))
                q += 1
            units.append(dict(m=m, T=T, kvh=kvh, cands=cands))
        NLD = 2
        NLDC = 2 + 1 + 2

        with nc.Block() as block:
            @block.sync
            def _(sync):
                sync.dma_start(out=esk[:], in_=sinkb[:, :]).then_inc(s_ld, 16)
                sync.dma_start(out=bsw[:], in_=bswa_d[:, :, :, :]).then_inc(s_ld, 16)
                for g in range(4):
                    for kvh in range(2):
                        sync.dma_start(out=AQs[kvh * 64:(kvh + 1) * 64, :, 1024 * g:1024 * (g + 1)],
                                       in_=AQT[kvh * 256:(kvh + 1) * 256, 1024 * g:1024 * (g + 1)].rearrange("(hh d) t -> d hh t", d=64)
                                       ).then_inc(s_ldc[g], 16)
                    sync.dma_start(out=AKs[:, 2048 * g:2048 * (g + 1)], in_=AKT[:, 2048 * g:2048 * (g + 1)]).then_inc(s_ldc[g], 16)
                    for k in range(2):
                        sync.dma_start(out=AVs[:, 16 * g:16 * (g + 1), k, 0:64],
                                       in_=AV[2048 * g:2048 * (g + 1), k * 64:(k + 1) * 64].rearrange("(t p) d -> p t d", p=128)
                                       ).then_inc(s_ldc[g], 16)
                sync.dma_start(out=pre["KP0"][64:96, :], in_=onehot_d[:, :]).then_inc(pre["s_c"], 16)
                sync.dma_start(out=pre["pm_s"][:], in_=pm_d[:, :, :]).then_inc(pre["s_c"], 16)
                sync.dma_start(out=pre["oz_s"][:], in_=oz_d[:, :, :]).then_inc(pre["s_c"], 16)
                sync.dma_start(out=pre["cfar_s"][:], in_=cfar_d[:, :]).then_inc(pre["s_c"], 16)
                sync.dma_start(out=pre["KP0"][0:64, :], in_=BKT[0:64, :]).then_inc(pre["s_hl0"], 16)
                sync.dma_start(out=pre["QP0"][0:64, :], in_=BQT[0:64, :]).then_inc(pre["s_hl0"], 16)
                for g in range(4):
                    sync.dma_start(out=pre["VP0"][:, 16 * g:16 * (g + 1), 0:64],
                                   in_=BV[2048 * g:2048 * (g + 1), 0:64].rearrange("(t p) d -> p t d", p=128)
                                   ).then_inc(pre["s_hl0"], 16)
                sync.dma_start(out=pre["KM0"][:], in_=KMT[0:64, :]).then_inc(pre["s_hl0"], 16)
                for un in units:
                    m, T, kvh = un["m"], un["T"], un["kvh"]
                    sync.wait_ge(s_mx, m + 1)
                    sync.dma_start(out=MT[kvh * 256:(kvh + 1) * 256, T * 128:(T + 1) * 128].rearrange("(hh d) t -> d hh t", d=64),
                                   in_=mst[m % 4][:]).then_inc(s_mo[m % 4], 16)
                for k_ in range(4):
                    sync.wait_ge(s_mo[k_], 16 * (NU // 4))
                sync.wait_ge(pre["s_c"], 16 * 4)
                sync.wait_ge(pre["s_hl0"], 16 * 7)

            @block.gpsimd
            def _(g):
                nc.clear_and_free_semaphores(env["free_in_p2"])
                g.memset(AVs[:, :, :, 64:65], 1.0)
                g.memset(sel65[:], 0.0).then_inc(s_g2, 1)
                g.wait_ge(s_g2, 1)
                g.memset(sel65[0:1, 64:65], 1.0)
                g.memset(rrow[0][:], 1.0)
                g.memset(rrow[2][:], 1.0)
                g.memset(rrow[3][:], 1.0)
                g.memset(rrow[1][:], 1.0).then_inc(s_ms, 1)
                g.memset(pre["VP0"][:, :, 64:65], 1.0)
                g.memset(pre["NM"][:], 0.0)
                g.memset(pre["ones_b"][:], 1.0)
                g.dma_start(out=pre["BM0"][:], in_=bmoba_d[0], max_dma_last_dim=4096).then_inc(pre["s_bml0"], 16)

                def gload(un):
                    m, T, kvh = un["m"], un["T"], un["kvh"]
                    if m >= 4:
                        g.wait_ge(s_t1, m - 3)
                    g.dma_start(out=gsw[m % 4][:],
                                in_=GT[kvh * 256:(kvh + 1) * 256, T * 128:(T + 1) * 128].rearrange("(hh d) t -> d hh t", d=64)
                                ).then_inc(s_gl[m % 4], 16)

                for k_ in range(3):
                    gload(units[k_])
                for un in units:
                    m = un["m"]
                    if m + 3 < NU:
                        gload(units[m + 3])
                    g.wait_ge(s_bcs, m + 1)
                    g.wait_ge(s_t1, m + 1)
                    if m >= 4:
                        g.wait_ge(s_mo[m % 4], 16 * (m // 4))
                    g.tensor_tensor(out=mst[m % 4][:].rearrange("p h q -> p (h q)"), in0=t1b[m % 4][:],
                                    in1=bcs[m % 4][:], op=ALU.mult).then_inc(s_mx, 1)

            @block.tensor
            def _(t):
                def QK(un):
                    T, kvh = un["T"], un["kvh"]
                    if un["m"] % 16 == 0:
                        t.wait_ge(s_ldc[un["m"] // 16], 16 * NLDC)
                    for cd in un["cands"]:
                        qq = cd["q"]
                        if qq >= 4:
                            t.wait_ge(s_add, qq - 3)
                        t.matmul(SP[qq % 4][:, :], AKs[kvh * 64:(kvh + 1) * 64, cd["kt"] * 128:(cd["kt"] + 1) * 128],
                                 AQs[kvh * 64:(kvh + 1) * 64, :, T * 128:(T + 1) * 128], start=True, stop=True
                                 ).then_inc(s_qk, 1)

                def PV(un):
                    m, kvh = un["m"], un["kvh"]
                    if m >= 2:
                        t.wait_ge(s_t1, m - 1)
                    first = True
                    for cd in un["cands"]:
                        qq = cd["q"]
                        t.wait_ge(s_exp, qq + 1)
                        ins = t.matmul(OP[m % 2][0:65, :], AVs[:, cd["kt"], kvh, :], PTb[qq % 3][:, :],
                                       start=first, stop=False)
                        first = False
                        if cd is un["cands"][-1]:
                            ins = t.matmul(OP[m % 2][0:65, :], sel65[0:1, 0:65], esk_hi[0:1, kvh * 4:(kvh + 1) * 4, :],
                                           start=False, stop=True)
                        ins.then_inc(s_pv, 1)

                t.wait_ge(s_ld, 16 * NLD)
                t.wait_ge(s_ms, 1)
                t.wait_ge(s_er, 1)
                QK(units[0]); QK(units[1])
                for m in range(NU):
                    PV(units[m])
                    if m + 2 < NU:
                        QK(units[m + 2])

            @block.vector
            def _(v):
                def adds(un):
                    kvh = un["kvh"]
                    for cd in un["cands"]:
                        qq = cd["q"]
                        v.wait_ge(s_qk, qq + 1)
                        if qq >= 2:
                            v.wait_ge(s_exp, qq - 1)
                        v.tensor_tensor(out=tmp[qq % 2][:], in0=SP[qq % 4][:, :].rearrange("p (h q) -> p h q", q=128),
                                        in1=bsw[:, cd["kind"], kvh * 4:(kvh + 1) * 4, :], op=ALU.add).then_inc(s_add, 1)

                def post(un):
                    m, kvh = un["m"], un["kvh"]
                    lastq = un["cands"][-1]["q"]
                    v.wait_ge(s_pv, lastq + 1)
                    v.wait_ge(s_gl[m % 4], 16 * (m // 4 + 1))
                    if m >= 4:
                        v.wait_ge(s_mx, m - 3)
                    v.tensor_tensor(out=t1b[m % 4][:], in0=OP[m % 2][0:64, :],
                                    in1=gsw[m % 4][:].rearrange("p h q -> p (h q)"), op=ALU.mult).then_inc(s_t1, 1)

                def fin(un):
                    m = un["m"]
                    v.wait_ge(s_rr, m + 1)
                    if m >= 4:
                        v.wait_ge(s_mx, m - 3)
                    v.stream_shuffle(out=bcs[m % 4][0:64, :], in_=rrow[m % 4][64:128, :], mask=[0] * 32
                                     ).then_inc(s_bcs, 1)

                v.wait_ge(s_ld, 16 * NLD)
                v.wait_ge(s_es, 1)
                v.tensor_copy(out=esk_rep[0:1, :, :], in_=esk[0:1, :].unsqueeze(2).broadcast_to([1, 8, 128]))
                v.tensor_copy(out=esk_hi[0:1, :, :], in_=esk_rep[0:1, :, :])
                v.tensor_tensor(out=esk_lo[0:1, :, :], in0=esk_rep[0:1, :, :], in1=esk_hi[0:1, :, :], op=ALU.subtract
                                ).then_inc(s_er, 1)
                adds(units[0]); adds(units[1])
                for m in range(NU):
                    post(units[m])
                    if m + 2 < NU:
                        adds(units[m + 2])
                    if m >= 1:
                        fin(units[m - 1])
                fin(units[NU - 1])

            @block.scalar
            def _(a):
                a.wait_ge(s_ld, 16 * NLD)
                a.activation(out=esk[:], in_=esk[:], func=AF.Exp).then_inc(s_es, 1)
                def recip_act(un):
                    m = un["m"]
                    a.wait_ge(s_pv, un["cands"][-1]["q"] + 1)
                    if m >= 4:
                        a.wait_ge(s_bcs, m - 3)
                    a.activation(out=lnr[64:65, :], in_=OP[m % 2][64:65, :], func=AF.Ln)
                    a.activation(out=rrow[m % 4][64:65, :], in_=lnr[64:65, :], func=AF.Exp, scale=-1.0)
                    a.activation(out=rrow[m % 4][96:97, :], in_=lnr[64:65, :], func=AF.Exp, scale=-1.0).then_inc(s_rr, 1)

                for un in units:
                    for cd in un["cands"]:
                        qq = cd["q"]
                        a.wait_ge(s_add, qq + 1)
                        if qq >= 3:
                            a.wait_ge(s_pv, qq - 2)
                        a.activation(out=PTb[qq % 3][:], in_=tmp[qq % 2][:].rearrange("p h q -> p (h q)"), func=AF.Exp
                                     ).then_inc(s_exp, 1)
                    if un["m"] >= 1:
                        recip_act(units[un["m"] - 1])
                recip_act(units[NU - 1])


def phase3(nc, env):
    sem = env["sem"]; identb = env["identb"]
    BKT = env["BKT"]; BV = env["BV"]; BQT = env["BQT"]; GT = env["GT"]; MT = env["MT"]; KMT = env["KMT"]
    bmoba_d = env["bmoba_d"]; pm_d = env["pm_d"]; oz_d = env["oz_d"]; onehot_d = env["onehot_d"]; cfar_d = env["cfar_d"]
    with ExitStack() as e3:
        sb, ps = _mk(nc, e3)
        pre = env["pre"]
        KP = [pre["KP0"], sb("KP1", [96, S], BF16)]
        VP = [pre["VP0"], sb("VP1", [128, 64, 65], BF16)]
        QP = [pre["QP0"], sb("QP1", [96, OWN], BF16)]
        BM = [pre["BM0"], sb("BM1", [128, 10, 2, 256], BF16)]
        KM = [pre["KM0"], sb("KM1", [64, 32], BF16)]
        pm_s = pre["pm_s"]; oz_s = pre["oz_s"]; cfar_s = pre["cfar_s"]
        smb = sb("smb", [128, 32, 32], F32)
        m8 = sb("m8", [128, 32, 8], F32)
        NM = pre["NM"]; ones_b = pre["ones_b"]
        PTg = [sb(f"PTg{i}", [128, 1024], BF16) for i in range(3)]
        obs = [sb(f"obs{i}", [65, 512], F32) for i in range(2)]
        rrow = [sb(f"rrowm{i}", [96, 512], F32) for i in range(2)]
        bcm = [sb(f"bcm{i}", [64, 512], F32) for i in range(2)]
        t1m = [sb(f"t1m{i}", [64, 512], F32) for i in range(2)]
        mstm = [sb(f"mstm{i}", [64, 512], BF16) for i in range(2)]
        gmm = [sb(f"gmm{i}", [64, 512], BF16) for i in range(2)]
        NSB = 2
        SBg = [ps(f"m_SB{i}", [128, 1024]) for i in range(NSB)]
        SELb = [ps(f"m_SEL{i}", [128, 512]) for i in range(2)]
        OB = [ps(f"m_OB{i}", [128, 512]) for i in range(2)]
        TPn = [SELb[0][:, :].bitcast(BF16), SELb[1][:, :].bitcast(BF16)]

        s_c = pre["s_c"]; s_hl = [pre["s_hl0"], sem("p3_hl1")]; s_bml = [pre["s_bml0"], sem("p3_bml1")]
        s_sc = sem("p3_sc"); s_m8 = sem("p3_m8"); s_nm = sem("p3_nm"); s_bmf = sem("p3_bmf")
        s_selt = sem("p3_selt"); s_selcp = sem("p3_selcp"); s_qk = sem("p3_qk")
        s_exp = sem("p3_exp"); s_pv = sem("p3_pv"); s_obc = sem("p3_obc"); s_rr = sem("p3_rr"); s_bcs = sem("p3_bcs")
        s_t1 = sem("p3_t1"); s_bc = sem("p3_bc"); s_mx = sem("p3_mx"); s_mo = [sem("p3_mo0"), sem("p3_mo1")]
        s_gl = [sem("p3_gl0"), sem("p3_gl1")]; s_ms = sem("p3_ms"); s_gq = sem("p3_gq")

        heads = []
        gidx = 0
        kidx = 0
        near_g = []
        for h in range(8):
            jl = []
            pairs = []
            for spi, sp in enumerate([7, 0, 6, 1, 5, 2, 4, 3]):
                gp = 8 * h + spi
                i0 = 2 * sp
                pr = dict(gp=gp, sp=sp, h=h, first=gidx)
                nkb = 4 * sp + 4
                for vb in range(nkb):
                    e0 = 2 * i0 + 1 - vb
                    jb = dict(g=gidx, h=h, sp=sp, gp=gp, vb=vb, e0=e0, near=(e0 <= 5),
                              firstjob=(vb == 0), lastjob=(vb == nkb - 1))
                    if jb["near"]:
                        jb["k"] = kidx; kidx += 1; near_g.append(gidx)
                    jl.append(jb)
                    gidx += 1
                pr["last"] = gidx - 1
                pairs.append(pr)
            heads.append(dict(h=h, jobs=jl, pairs=pairs))
        NHL = 7

        with nc.Block() as block:
            @block.sync
            def _(sync):
                sync.dma_start(out=KP[1][64:96, :], in_=onehot_d[:, :]).then_inc(s_c, 16)

                def loads(h):
                    hb = h % 2
                    sync.dma_start(out=KP[hb][0:64, :], in_=BKT[h * 64:(h + 1) * 64, :]).then_inc(s_hl[hb], 16)
                    sync.dma_start(out=QP[hb][0:64, :], in_=BQT[h * 64:(h + 1) * 64, :]).then_inc(s_hl[hb], 16)
                    for g in range(4):
                        sync.dma_start(out=VP[hb][:, 16 * g:16 * (g + 1), 0:64],
                                       in_=BV[2048 * g:2048 * (g + 1), h * 64:(h + 1) * 64].rearrange("(t p) d -> p t d", p=128)
                                       ).then_inc(s_hl[hb], 16)
                    sync.dma_start(out=KM[hb][:], in_=KMT[h * 64:(h + 1) * 64, :]).then_inc(s_hl[hb], 16)

                loads(1)
                p4 = env["pre4"]
                sync.dma_start(out=p4["WO"][:], in_=env["WOd"].rearrange("(c p) n -> p c n", p=128)).then_inc(p4["s_w4"], 16)
                sync.dma_start(out=p4["WG"][:], in_=env["WGd"].rearrange("(c p) n -> p c n", p=128)).then_inc(p4["s_w4g"], 16)
                sync.dma_start(out=p4["WP"][:], in_=env["WPd"].rearrange("(c p) n -> p c n", p=128)).then_inc(p4["s_w4g"], 16)
                sync.dma_start(out=p4["GF"][:], in_=env["gfin"][:, :]).then_inc(p4["s_w4f"], 16)
                for hd in heads:
                    h = hd["h"]
                    for pr in hd["pairs"]:
                        gp = pr["gp"]
                        sync.wait_ge(s_mx, gp + 1)
                        sync.dma_start(out=MT[512 + h * 64:512 + (h + 1) * 64, pr["sp"] * 512:(pr["sp"] + 1) * 512],
                                       in_=mstm[gp % 2][:]).then_inc(s_mo[gp % 2], 16)
                    if h + 2 < 8:
                        loads(h + 2)
                sync.wait_ge(s_mo[0], 16 * 32)
                sync.wait_ge(s_mo[1], 16 * 32)
                sync.wait_ge(p4["s_w4"], 16)
                sync.wait_ge(p4["s_w4g"], 32)
                sync.wait_ge(p4["s_w4f"], 16)

            @block.gpsimd
            def _(g):
                gq = [0]

                def bmload(h):
                    g.dma_start(out=BM[h % 2][:], in_=bmoba_d[h], max_dma_last_dim=4096).then_inc(s_bml[h % 2], 16)

                g.memset(rrow[0][:], 1.0)
                g.memset(rrow[1][:], 1.0)
                g.memset(VP[1][:, :, 64:65], 1.0).then_inc(s_ms, 1)
                bmload(1)
                for hd in heads:
                    h = hd["h"]
                    for pr in hd["pairs"]:
                        gp = pr["gp"]
                        if gp >= 2:
                            g.wait_ge(s_t1, gp - 1)
                        g.dma_start(out=gmm[gp % 2][:],
                                    in_=GT[512 + h * 64:512 + (h + 1) * 64, pr["sp"] * 512:(pr["sp"] + 1) * 512]
                                    ).then_inc(s_gl[gp % 2], 16)
                        g.wait_ge(s_obc, gp + 1)
                        g.wait_ge(s_gl[gp % 2], 16 * (gp // 2 + 1))
                        g.tensor_tensor(out=t1m[gp % 2][:], in0=obs[gp % 2][0:64, :], in1=gmm[gp % 2][:], op=ALU.mult
                                        ).then_inc(s_t1, 1)
                        g.wait_ge(s_bcs, gp + 1)
                        g.wait_ge(s_t1, gp + 1)
                        if gp >= 2:
                            g.wait_ge(s_mo[gp % 2], 16 * (gp // 2))
                        g.tensor_tensor(out=mstm[gp % 2][:], in0=t1m[gp % 2][:], in1=bcm[gp % 2][:], op=ALU.mult
                                        ).then_inc(s_mx, 1)
                    if h + 2 < 8:
                        g.wait_ge(s_qk, hd["jobs"][-1]["g"] + 1)
                        bmload(h + 2)

            @block.tensor
            def _(t):
                t.wait_ge(s_c, 16 * 5)
                t.wait_ge(s_ms, 1)

                def QK(jb):
                    g_, hb, sp, vb = jb["g"], jb["h"] % 2, jb["sp"], jb["vb"]
                    e0 = jb["e0"]
                    for ks in range(2):
                        kt = 2 * vb + ks
                        ins = t.matmul(SBg[g_ % NSB][:, ks * 512:(ks + 1) * 512], KP[hb][0:96, kt * 128:(kt + 1) * 128],
                                       QP[hb][0:96, sp * 512:(sp + 1) * 512], start=True, stop=not jb["near"])
                        if ks == 0 and g_ >= NSB:
                            ins._wait_ge(s_exp, g_ - NSB + 1)
                        if jb["near"]:
                            if e0 < 0:
                                ins = t.matmul(SBg[g_ % NSB][:, ks * 512 + 256:(ks + 1) * 512], identb[:, :],
                                               BM[hb][:, e0 + 4, ks, :], start=False, stop=True)
                            elif e0 <= 3:
                                ins = t.matmul(SBg[g_ % NSB][:, ks * 512:(ks + 1) * 512], identb[:, :],
                                               BM[hb][:, e0 + 2:e0 + 5:2, ks, :], start=False, stop=True)
                            else:
                                ins = t.matmul(SBg[g_ % NSB][:, ks * 512:ks * 512 + 256], identb[:, :],
                                               BM[hb][:, e0 + 2, ks, :], start=False, stop=True)
                    ins.then_inc(s_qk, 1)

                def PV(jb):
                    g_, hb, vb, gp = jb["g"], jb["h"] % 2, jb["vb"], jb["gp"]
                    if jb["firstjob"] and gp >= 2:
                        t.wait_ge(s_obc, gp - 1)
                    for ks in range(2):
                        kt = 2 * vb + ks
                        ins = t.matmul(OB[gp % 2][0:65, :], VP[hb][:, kt, :], PTg[g_ % 3][:, ks * 512:(ks + 1) * 512],
                                       start=(jb["firstjob"] and ks == 0), stop=(jb["lastjob"] and ks == 1))
                        if ks == 0:
                            ins._wait_ge(s_exp, g_ + 1)
                    ins.then_inc(s_pv, 1)

                def sel_scores(h, half):
                    hb = h % 2
                    if half == 0:
                        t.wait_ge(s_hl[hb], 16 * NHL * (h // 2 + 1))
                        if h >= 1:
                            t.wait_ge(s_selcp, 8 * h)
                    else:
                        t.wait_ge(s_m8, 2 * h + 1)
                    for T in range(16 * half, 16 * half + 16):
                        ins = t.matmul(SELb[half][:, (T % 16) * 32:(T % 16 + 1) * 32], QP[hb][0:64, T * 128:(T + 1) * 128],
                                       KM[hb][0:64, :], start=True, stop=True)
                    ins.then_inc(s_sc, 1)

                def sel_tr(h, gq_):
                    if gq_ == 0:
                        t.wait_ge(s_nm, h + 1)
                        t.wait_ge(s_m8, 2 * h + 2)
                    if gq_ >= 2:
                        t.wait_ge(s_selcp, 8 * h + gq_ - 1)
                    for tq in range(4):
                        T = 4 * gq_ + tq
                        ins = t.transpose(out=TPn[gq_ % 2][0:96, tq * 128:(tq + 1) * 128], in_=NM[:, T, :],
                                          identity=identb[:])
                    ins.then_inc(s_selt, 1)

                sel_scores(0, 0)
                sel_scores(0, 1)
                for gq_ in range(8):
                    sel_tr(0, gq_)
                for hd in heads:
                    h = hd["h"]; hb = h % 2
                    t.wait_ge(s_selcp, 8 * (h + 1))
                    t.wait_ge(s_bmf, h + 1)
                    jl = hd["jobs"]
                    QK(jl[0])
                    if NSB >= 3:
                        QK(jl[1])
                    for idx, jb in enumerate(jl):
                        if NSB >= 3:
                            PV(jb)
                            if idx + 2 < len(jl):
                                QK(jl[idx + 2])
                        else:
                            if idx + 1 < len(jl):
                                QK(jl[idx + 1])
                            PV(jb)
                        if h + 1 < 8:
                            if idx == 36:
                                sel_scores(h + 1, 0)
                            if idx == 46:
                                sel_scores(h + 1, 1)
                            if idx >= 64 and (idx - 64) % 8 == 0 and (idx - 64) // 8 < 8:
                                sel_tr(h + 1, (idx - 64) // 8)

            @block.vector
            def _(v):
                v.wait_ge(s_c, 16 * 5)

                def post(pr):
                    gp = pr["gp"]
                    v.wait_ge(s_pv, pr["last"] + 1)
                    if gp >= 2:
                        v.wait_ge(s_t1, gp - 1)
                    v.tensor_copy(out=obs[gp % 2][:], in_=OB[gp % 2][0:65, :]).then_inc(s_obc, 1)
                    for c in range(4):
                        rpend.append((gp, c))

                rpend = []

                def rchunk():
                    if not rpend:
                        return
                    gp, c = rpend.pop(0)
                    if c == 0:
                        v.wait_ge(s_obc, gp + 1)
                        if gp >= 2:
                            v.wait_ge(s_bcs, gp - 1)
                    ins = v.reciprocal(out=rrow[gp % 2][64:65, c * 128:(c + 1) * 128],
                                       in_=obs[gp % 2][64:65, c * 128:(c + 1) * 128])
                    if c == 3:
                        ins.then_inc(s_rr, 1)
                        v.wait_ge(s_rr, gp + 1)
                        if gp >= 2:
                            v.wait_ge(s_mx, gp - 1)
                        v.stream_shuffle(out=bcm[gp % 2][0:32, :], in_=rrow[gp % 2][64:96, :], mask=[0] * 32)
                        v.stream_shuffle(out=bcm[gp % 2][32:64, :], in_=rrow[gp % 2][64:96, :], mask=[0] * 32
                                         ).then_inc(s_bcs, 1)

                def bmfold(h):
                    hb = h % 2
                    v.wait_ge(s_bml[hb], 16 * (h // 2 + 1))
                    v.tensor_scalar(out=BM[hb][:], in0=BM[hb][:], scalar1=cfar_s[:, h:h + 1], scalar2=None,
                                    op0=ALU.subtract).then_inc(s_bmf, 1)

                def sel_dve(h):
                    for half in range(2):
                        v.wait_ge(s_sc, 2 * h + half + 1)
                        v.tensor_tensor(out=smb[:, 16 * half:16 * (half + 1), :],
                                        in0=SELb[half][:, :].rearrange("p (t n) -> p t n", n=32),
                                        in1=pm_s[:, 16 * half:16 * (half + 1), :], op=ALU.add)
                        for T in range(16 * half, 16 * half + 16):
                            ins = v.max(out=m8[:, T, :], in_=smb[:, T, :])
                        ins.then_inc(s_m8, 1)
                    v.wait_ge(s_m8, 2 * h + 2)
                    v.tensor_tensor(out=smb[:], in0=smb[:], in1=m8[:, :, 2:3].broadcast_to([128, 32, 32]), op=ALU.is_ge)
                    v.tensor_scalar(out=smb[:], in0=smb[:], scalar1=-1.0, scalar2=-NEG, op0=ALU.add, op1=ALU.mult)
                    v.tensor_tensor(out=smb[:], in0=smb[:], in1=pm_s[:], op=ALU.add)
                    v.tensor_tensor(out=smb[:], in0=smb[:], in1=oz_s[:], op=ALU.mult)
                    v.tensor_scalar(out=NM[:, :, 64:96], in0=smb[:], scalar1=cfar_s[:, h:h + 1], scalar2=None,
                                    op0=ALU.add).then_inc(s_nm, 1)

                def sel_cp(h, gq_):
                    hb = h % 2
                    v.wait_ge(s_selt, 8 * h + gq_ + 1)
                    v.tensor_copy(out=QP[hb][64:96, gq_ * 512:(gq_ + 1) * 512], in_=TPn[gq_ % 2][64:96, 0:512]
                                  ).then_inc(s_selcp, 1)

                bmfold(0)
                sel_dve(0)
                for gq_ in range(8):
                    sel_cp(0, gq_)
                for hd in heads:
                    h = hd["h"]; hb = h % 2
                    for pi, pr in enumerate(hd["pairs"]):
                        post(pr)
                        while rpend:
                            rchunk()
                        if h + 1 < 8:
                            if pi == 1:
                                sel_dve(h + 1)
                            if pi == 3:
                                for gq_ in range(0, 4):
                                    sel_cp(h + 1, gq_)
                            if pi == 4:
                                for gq_ in range(4, 6):
                                    sel_cp(h + 1, gq_)
                            if pi == 5:
                                for gq_ in range(6, 8):
                                    sel_cp(h + 1, gq_)
                                bmfold(h + 1)

            @block.scalar
            def _(a):
                a.wait_ge(s_c, 16 * 5)
                for hd in heads:
                    h = hd["h"]; hb = h % 2
                    for jb in hd["jobs"]:
                        g_ = jb["g"]
                        if g_ >= 3:
                            a.wait_ge(s_pv, g_ - 2)
                        a.wait_ge(s_qk, g_ + 1)
                        a.activation(out=PTg[g_ % 3][:], in_=SBg[g_ % NSB][:, :], func=AF.Exp).then_inc(s_exp, 1)


def phase4(nc, env):
    sem = env["sem"]; identb = env["identb"]
    WOd = env["WOd"]; WGd = env["WGd"]; WPd = env["WPd"]; gfin = env["gfin"]
    xv = env["xv"]; pown = env["pown"]; MT = env["MT"]; out_d = env["out_d"]
    NT = 32
    with ExitStack() as e4:
        sb, ps = _mk(nc, e4)
        p4 = env["pre4"]
        WO = p4["WO"]; WG = p4["WG"]; WP = p4["WP"]; GF = p4["GF"]
        s_xs1d = sem("p4_xs1d")
        xo = [sb(f"f_xo{i}", [128, D], F32) for i in range(2)]
        pt = [sb(f"f_pt{i}", [128, 256], F32) for i in range(2)]
        mt = [sb(f"f_mt{i}", [128, 8, 128], BF16) for i in range(2)]
        x1 = [sb(f"f_x1{i}", [128, D], F32) for i in range(2)]
        xs1 = sb("f_xs1", [128, D], BF16)
        pb = sb("f_pb", [128, 256], BF16)
        xs1T = sb("f_xs1T", [128, 8, 128], BF16)
        pT = sb("f_pT", [128, 2, 128], BF16)
        sg = sb("f_sg", [128, D], F32)
        tq = sb("f_tq", [128, D], F32)
        x2 = sb("f_x2", [128, D], F32)
        ob = [sb(f"f_ob{i}", [128, D], F32) for i in range(2)]
        junk = sb("f_junk", [128, D], BF16)
        ssq1 = sb("f_ssq1", [128, 1], F32); sq1 = sb("f_sq1", [128, 1], F32); r1 = sb("f_r1", [128, 1], F32)
        ssq2 = sb("f_ssq2", [128, 1], F32); sq2 = sb("f_sq2", [128, 1], F32); r2 = sb("f_r2", [128, 1], F32)
        Y = [ps(f"f_Y{i}", [128, 512]) for i in range(2)]
        G = [ps(f"f_G{i}", [128, 512]) for i in range(2)]
        PP = [ps(f"f_PP{i}", [128, 512]) for i in range(2)]
        TPx = ps("f_TPx", [128, 8, 128], BF16)
        TPp = ps("f_TPp", [128, 2, 128], BF16)
        s_ld = [sem("p4_ld0"), sem("p4_ld1")]; s_A = sem("p4_A"); s_x1 = sem("p4_x1"); s_acc1 = sem("p4_acc1")
        s_xs1 = sem("p4_xs1"); s_tr = sem("p4_tr"); s_cp = sem("p4_cp"); s_G = sem("p4_G")
        s_PP = sem("p4_PP"); s_sg = sem("p4_sg"); s_t = sem("p4_t"); s_x2 = sem("p4_x2"); s_acc2 = sem("p4_acc2")
        s_o = sem("p4_o"); s_od = [sem("p4_od0"), sem("p4_od1")]

        s_p1 = sem("p4_p1"); s_p2 = sem("p4_p2"); s_gq = sem("p4_gq")
        mhalf = env["mhalf"]
        ms1 = sb("f_ms1", [128, 1], F32); ms2 = sb("f_ms2", [128, 1], F32)

        with nc.Block() as block:
            @block.gpsimd
            def _(g):
                def loads(T):
                    if T >= 2:
                        g.wait_ge(s_x1, T - 1)
                        g.wait_ge(s_xs1, T - 1)
                        g.wait_ge(s_A, T - 1)
                    i, u = T // 2, T % 2
                    r0 = 512 * i + 128 * u
                    g.dma_start(out=xo[T % 2][:], in_=xv[r0:r0 + 128, :]).then_inc(s_ld[T % 2], 16)
                    g.dma_start(out=pt[T % 2][:], in_=pown[T * 128:(T + 1) * 128, :]).then_inc(s_ld[T % 2], 16)
                    g.dma_start(out=mt[T % 2][:], in_=MT[:, T * 128:(T + 1) * 128].rearrange("(c p) t -> p c t", p=128)
                                ).then_inc(s_ld[T % 2], 16)

                gq = [0]

                def pow1(T):
                    g.wait_ge(s_acc1, T + 1)
                    g.tensor_scalar(out=ms1[:], in0=ssq1[:], scalar1=1.0 / D, scalar2=1e-6, op0=ALU.mult, op1=ALU.add
                                    ).then_inc(s_gq, 1)
                    gq[0] += 1
                    g.wait_ge(s_gq, gq[0])
                    g.tensor_tensor(out=r1[:], in0=ms1[:], in1=mhalf[:], op=ALU.pow).then_inc(s_p1, 1)

                def pow2(T):
                    g.wait_ge(s_acc2, T + 1)
                    g.tensor_scalar(out=ms2[:], in0=ssq2[:], scalar1=1.0 / D, scalar2=1e-6, op0=ALU.mult, op1=ALU.add
                                    ).then_inc(s_gq, 1)
                    gq[0] += 1
                    g.wait_ge(s_gq, gq[0])
                    g.tensor_tensor(out=r2[:], in0=ms2[:], in1=mhalf[:], op=ALU.pow).then_inc(s_p2, 1)

                loads(0); loads(1)
                pow1(0)
                for T in range(NT):
                    if T + 2 < NT:
                        loads(T + 2)
                    if T >= 1:
                        pow2(T - 1)
                    if T + 1 < NT:
                        pow1(T + 1)
                pow2(NT - 1)

            @block.tensor
            def _(t):
                def A(T):
                    t.wait_ge(s_ld[T % 2], 48 * (T // 2 + 1))
                    if T >= 1:
                        t.wait_ge(s_x1, T)
                    for half in range(2):
                        for c in range(8):
                            ins = t.matmul(Y[half][:, :], mt[T % 2][:, c, :], WO[:, c, half * 512:(half + 1) * 512],
                                           start=(c == 0), stop=(c == 7))
                    ins.then_inc(s_A, 1)

                def TR(T):
                    t.wait_ge(s_xs1, T + 1)
                    t.wait_ge(s_xs1d, T + 1)
                    if T >= 1:
                        t.wait_ge(s_cp, T)
                    for c in range(8):
                        t.transpose(out=TPx[:, c, :], in_=xs1[:, c * 128:(c + 1) * 128], identity=identb[:])
                    for c in range(2):
                        ins = t.transpose(out=TPp[:, c, :], in_=pb[:, c * 128:(c + 1) * 128], identity=identb[:])
                    ins.then_inc(s_tr, 1)

                def GM(T):
                    t.wait_ge(s_cp, T + 1)
                    if T >= 1:
                        t.wait_ge(s_t, T)
                    for half in range(2):
                        for c in range(8):
                            ins = t.matmul(G[half][:, :], xs1T[:, c, :], WG[:, c, half * 512:(half + 1) * 512],
                                           start=(c == 0), stop=(c == 7))
                    ins.then_inc(s_G, 1)
                    for half in range(2):
                        for c in range(2):
                            ins = t.matmul(PP[half][:, :], pT[:, c, :], WP[:, c, half * 512:(half + 1) * 512],
                                           start=(c == 0), stop=(c == 1))
                    ins.then_inc(s_PP, 1)

                A(0)
                for T in range(NT):
                    if T + 1 < NT:
                        A(T + 1)
                    TR(T)
                    GM(T)

            @block.vector
            def _(v):
                def x1f(T):
                    v.wait_ge(s_A, T + 1)
                    v.wait_ge(s_ld[T % 2], 48 * (T // 2 + 1))
                    for half in range(2):
                        ins = v.tensor_tensor(out=x1[T % 2][:, half * 512:(half + 1) * 512], in0=Y[half][:, :],
                                              in1=xo[T % 2][:, half * 512:(half + 1) * 512], op=ALU.add)
                    ins.then_inc(s_x1, 1)

                def xs1half(T):
                    v.wait_ge(s_p1, T + 1)
                    if T >= 1:
                        v.wait_ge(s_tr, T)
                    v.tensor_scalar(out=xs1[:, 512:1024], in0=x1[T % 2][:, 512:1024], scalar1=r1[:, 0:1], scalar2=None,
                                    op0=ALU.mult).then_inc(s_xs1d, 1)

                def copies(T):
                    v.wait_ge(s_tr, T + 1)
                    v.tensor_copy(out=xs1T[:], in_=TPx[:, :, :])
                    v.tensor_copy(out=pT[:], in_=TPp[:, :, :]).then_inc(s_cp, 1)

                def tail(T):
                    v.wait_ge(s_PP, T + 1)
                    v.wait_ge(s_sg, T + 1)
                    for half in range(2):
                        ins = v.tensor_tensor(out=tq[:, half * 512:(half + 1) * 512], in0=PP[half][:, :],
                                              in1=sg[:, half * 512:(half + 1) * 512], op=ALU.mult)
                    ins.then_inc(s_t, 1)
                    v.tensor_tensor(out=x2[:], in0=tq[:], in1=x1[T % 2][:], op=ALU.add).then_inc(s_x2, 1)

                def outf(T):
                    v.wait_ge(s_p2, T + 1)
                    if T >= 2:
                        v.wait_ge(s_od[T % 2], 16 * (T // 2))
                    v.scalar_tensor_tensor(out=ob[T % 2][:], in0=x2[:], scalar=r2[:, 0:1], in1=GF[:],
                                           op0=ALU.mult, op1=ALU.mult).then_inc(s_o, 1)

                x1f(0)
                xs1half(0)
                for T in range(NT):
                    if T >= 1:
                        tail(T - 1)
                    if T + 1 < NT:
                        x1f(T + 1)
                    copies(T)
                    if T >= 1:
                        outf(T - 1)
                    if T + 1 < NT:
                        xs1half(T + 1)
                tail(NT - 1)
                outf(NT - 1)

            @block.scalar
            def _(a):
                def n1(T):
                    a.wait_ge(s_x1, T + 1)
                    a.activation(out=junk[:], in_=x1[T % 2][:], func=AF.Square, accum_out=ssq1[:, 0:1]).then_inc(s_acc1, 1)
                    if T >= 1:
                        a.wait_ge(s_tr, T)
                    a.activation(out=pb[:], in_=pt[T % 2][:], func=AF.Copy)
                    a.wait_ge(s_p1, T + 1)
                    a.activation(out=xs1[:, 0:512], in_=x1[T % 2][:, 0:512], func=AF.Copy, scale=r1[:, 0:1]
                                 ).then_inc(s_xs1, 1)

                def sig(T):
                    a.wait_ge(s_G, T + 1)
                    if T >= 1:
                        a.wait_ge(s_t, T)
                    for half in range(2):
                        ins = a.activation(out=sg[:, half * 512:(half + 1) * 512], in_=G[half][:, :], func=AF.Sigmoid)
                    ins.then_inc(s_sg, 1)

                def n2(T):
                    a.wait_ge(s_x2, T + 1)
                    a.activation(out=junk[:], in_=x2[:], func=AF.Square, accum_out=ssq2[:, 0:1]).then_inc(s_acc2, 1)

                n1(0)
                for T in range(NT):
                    if T >= 1:
                        sig(T - 1)
                        n2(T - 1)
                    if T + 1 < NT:
                        n1(T + 1)
                sig(NT - 1)
                n2(NT - 1)

            @block.sync
            def _(sync):
                for T in range(NT):
                    sync.wait_ge(s_o, T + 1)
                    sync.dma_start(out=out_d[T * 128:(T + 1) * 128, :], in_=ob[T % 2][:]).then_inc(s_od[T % 2], 16)
                sync.wait_ge(s_od[0], 16 * (NT // 2))
                sync.wait_ge(s_od[1], 16 * (NT // 2))


def _rel_bucket(dist):
    n = np.maximum(dist, 0)
    max_exact = 16
    nf = np.maximum(n, 1).astype(np.float32)
    val = (np.log(nf / np.float32(max_exact)) / np.float32(np.log(1024.0 / 16.0))) * np.float32(16)
    large = max_exact + val.astype(np.int32)
    large = np.minimum(large, 31)
    return np.where(n < max_exact, n, large).astype(np.int64)


def _core_tables(r, rel_bias):
    tab = np.asarray(rel_bias, np.float32)
    k = np.arange(128)[:, None]
    q = np.arange(128)[None, :]
    d0 = q - k
    diag = np.where((d0 >= 0)[:, None, :], tab[_rel_bucket(d0)][:, :, :8].transpose(0, 2, 1), np.float32(NEG))
    d1 = 128 + q - k
    prev = np.where((d1 < 128)[:, None, :], tab[_rel_bucket(d1)][:, :, :8].transpose(0, 2, 1), np.float32(NEG))
    masked = np.full_like(prev, NEG)
    bswa = np.stack([diag, prev, prev if r == 0 else masked, masked if r == 0 else prev], axis=1)
    bm = np.full((8, 128, 10, 2, 256), NEG, np.float32)
    qq = np.arange(256)[None, :]
    for e in range(8):
        delta = (e + 2 * r - 1) if e % 2 == 0 else (e - 1)
        for ks in range(2):
            dist = delta * 256 + qq - (ks * 128 + k)
            vals = tab[_rel_bucket(dist)][:, :, 8:]
            vals = np.where((dist >= 0)[:, :, None], vals, np.float32(NEG))
            bm[:, :, e + 2, ks, :] = vals.transpose(2, 0, 1)
    cfar = np.broadcast_to(tab[31, 8:][None, :], (128, 8))
    pm = np.full((32, 32), NEGBIG, np.float32)
    oz = np.ones((32, 32), np.float32)
    for T in range(32):
        i = T // 2
        for vb in range(32):
            past = (vb <= 2 * i - 1) or (r == 1 and vb == 2 * i + 1)
            if past:
                pm[T, vb] = 0.0
        oz[T, 2 * i] = 0.0
    pm = np.broadcast_to(pm[None], (128, 32, 32))
    oz = np.broadcast_to(oz[None], (128, 32, 32))
    return dict(bswa=np.ascontiguousarray(bswa, np.float32), bmoba=np.ascontiguousarray(bm),
                cfar=np.ascontiguousarray(cfar, np.float32), pm=np.ascontiguousarray(pm),
                oz=np.ascontiguousarray(oz))


def _virt_perm(r):
    return np.array([2 * (v // 2) + ((v % 2) ^ r) for v in range(32)])


def make_in_maps(x, p, norm_in, w_in, sinks, rel_bias, w_out, ple_norm, w_ple_gate, w_ple_proj, final_norm):
    f = lambda a: np.ascontiguousarray(np.asarray(a, dtype=np.float32))
    x = f(x); p = f(p); w_in = f(w_in)[0]; w_out = f(w_out)[0]; w_gate = f(w_ple_gate)[0]; w_ple = f(w_ple_proj)[0]
    norm_in = f(norm_in)[0]; ple_norm = f(ple_norm)[0]; final_norm = f(final_norm); sinks = f(sinks)[0]
    onehot = np.zeros((32, S), ml_dtypes.bfloat16)
    for n in range(32):
        onehot[n, n * 256:(n + 1) * 256] = 1.0
    shared = dict(
        w_in=w_in, w_out=w_out, w_gate=w_gate, w_ple=w_ple,
        gin=np.ascontiguousarray(norm_in.reshape(8, 128).T), gple=np.ascontiguousarray(ple_norm.reshape(8, 128).T),
        gfin=np.ascontiguousarray(np.broadcast_to(final_norm[None, :], (128, D))),
        sinkb=np.ascontiguousarray(np.broadcast_to(sinks[None, :], (128, 8))),
        onehot=onehot,
    )
    tabs = [_core_tables(r, rel_bias) for r in range(2)]
    maps = []
    for c in range(8):
        b, r = c // 2, c % 2
        perm = _virt_perm(r)
        xb = x[b].reshape(32, 256, D)
        xvv = np.ascontiguousarray(xb[perm].reshape(S, D))
        own = np.array([2 * i + r for i in range(16)])
        pw = np.ascontiguousarray(p[0, b].reshape(32, 256, 256)[own].reshape(OWN, 256))
        m = dict(shared)
        m.update(tabs[r])
        m["xv"] = xvv
        m["pown"] = pw
        maps.append(m)
    return maps


_NC_CACHE = {}


def kernel(x, p, norm_in, w_in, sinks, rel_bias, w_out, ple_norm, w_ple_gate, w_ple_proj, final_norm):
    maps = make_in_maps(x, p, norm_in, w_in, sinks, rel_bias, w_out, ple_norm, w_ple_gate, w_ple_proj, final_norm)
    if "nc" not in _NC_CACHE:
        _NC_CACHE["nc"] = build_nc()
    nc = _NC_CACHE["nc"]
    res = run_bass_kernel_spmd(nc, maps, core_ids=list(range(8)))
    if DEBUG:
        return res
    out = np.empty((4, 32, 256, D), np.float32)
    for c in range(8):
        b, r = c // 2, c % 2
        o = np.asarray(res.results[c]["out"], np.float32).reshape(16, 256, D)
        for i in range(16):
            out[b, 2 * i + r] = o[i]
    return out.reshape(4, S, D)
```

```python
import numpy as np
import ml_dtypes
from contextlib import ExitStack
import concourse.bass as bass
import concourse.mybir as mybir
from concourse.bass_utils import run_bass_kernel_spmd

F32 = mybir.dt.float32
BF16 = mybir.dt.bfloat16
AF = mybir.ActivationFunctionType
ALU = mybir.AluOpType
AX = mybir.AxisListType

S = 8192
D = 1024
NPAIR = 16
OWN = 4096
INC = 3328
NEG = -30000.0
NEGBIG = -1.0e30
DEBUG = False
STOP_AFTER = 99


def build_nc():
    nc = bass.Bass("TRN2", target_bir_lowering=False)
    dbg_kind = "ExternalOutput" if DEBUG else "Internal"

    def din(name, shape, dt=F32):
        return nc.dram_tensor(name, list(shape), dt, kind="ExternalInput").ap()

    def dscr(name, shape, dt=BF16):
        return nc.dram_tensor(name, list(shape), dt, kind=dbg_kind).ap()

    xv = din("xv", [S, D])
    pown = din("pown", [OWN, 256])
    w_in = din("w_in", [D, INC])
    w_out = din("w_out", [D, D])
    w_gate = din("w_gate", [D, D])
    w_ple = din("w_ple", [256, D])
    gin = din("gin", [128, 8])
    gple = din("gple", [128, 8])
    gfin = din("gfin", [128, D])
    sinkb = din("sinkb", [128, 8])
    cfar_d = din("cfar", [128, 8])
    bswa_d = din("bswa", [128, 4, 8, 128])
    bmoba_d = din("bmoba", [8, 128, 10, 2, 256])
    pm_d = din("pm", [128, 32, 32])
    oz_d = din("oz", [128, 32, 32])
    onehot_d = din("onehot", [32, S], BF16)
    out_d = nc.dram_tensor("out", [OWN, D], F32, kind="ExternalOutput").ap()

    AKT = dscr("AKT", [128, S])
    AV = dscr("AV", [S, 128])
    BKT = dscr("BKT", [512, S])
    BV = dscr("BV", [S, 512])
    AQT = dscr("AQT", [512, OWN])
    BQT = dscr("BQT", [512, OWN])
    GT = dscr("GT", [1024, OWN])
    KMT = dscr("KMT", [512, 32])
    MT = dscr("MT", [1024, OWN])
    WOd = nc.dram_tensor("WOd", [D, D], BF16, kind="Internal").ap()
    WGd = nc.dram_tensor("WGd", [D, D], BF16, kind="Internal").ap()
    WPd = nc.dram_tensor("WPd", [256, D], BF16, kind="Internal").ap()

    with ExitStack() as es:
        def sb(name, shape, dt):
            return es.enter_context(nc.sbuf_tensor(name, list(shape), dt))

        sem_log = []

        def sem(name):
            h = nc.alloc_semaphore(name=name)
            sem_log.append(h)
            return h

        def sems_since(mark, keep=()):
            out = [h for h in sem_log[mark:] if all(h is not k for k in keep)]
            return out

        identb = sb("identb", [128, 128], BF16)
        identf = sb("identf", [128, 128], F32)
        ones_f = sb("ones_f", [128, 64], F32)
        gin_s = sb("gin_s", [128, 8], F32)
        gple_s = sb("gple_s", [128, 8], F32)
        s_wp = sem("s_wp")
        mhalf = sb("mhalf", [128, 1], F32)

        with ExitStack() as es1:
            W1 = es1.enter_context(nc.sbuf_tensor("W1", [128, 8, INC], BF16))

            with ExitStack() as es0:
                wst = [es0.enter_context(nc.sbuf_tensor(f"wst{i}", [128, INC], F32)) for i in range(2)]
                s_wl = [sem("s_wl0"), sem("s_wl1")]
                s_wcA = sem("s_wcA")
                s_wcD = sem("s_wcD")
                s_misc = sem("s_misc")
                s_g0 = sem("s_g0")
                chunks = []
                for kc in range(8):
                    chunks.append((w_in[kc * 128:(kc + 1) * 128, :], INC, W1[:, kc, :], gin_s[:, kc:kc + 1]))
                NCH = len(chunks)
                with nc.Block() as block:
                    @block.sync
                    def _(sync):
                        sync.dma_start(out=gin_s[:], in_=gin[:, :]).then_inc(s_misc, 16)
                        sync.dma_start(out=gple_s[:], in_=gple[:, :]).then_inc(s_misc, 16)
                        for n, (src, wd, dst, sc) in enumerate(chunks):
                            if n >= 2:
                                k = n // 2
                                if n % 2 == 0:
                                    sync.wait_ge(s_wcD, k)
                                else:
                                    sync.wait_ge(s_wcA, k)
                            sync.dma_start(out=wst[n % 2][:, 0:wd], in_=src).then_inc(s_wl[n % 2], 16)
                        sync.wait_ge(s_wl[0], 16 * (NCH // 2))
                        sync.wait_ge(s_wl[1], 16 * (NCH // 2))
                        sync.wait_ge(s_misc, 32)

                    @block.vector
                    def _(v):
                        v.wait_ge(s_misc, 32)
                        for n, (src, wd, dst, sc) in enumerate(chunks):
                            if n % 2 != 0:
                                continue
                            v.wait_ge(s_wl[0], 16 * (n // 2 + 1))
                            if sc is None:
                                v.tensor_copy(out=dst, in_=wst[0][:, 0:wd]).then_inc(s_wcD, 1)
                            else:
                                v.tensor_scalar(out=dst, in0=wst[0][:, 0:wd], scalar1=sc, scalar2=None,
                                                op0=ALU.mult).then_inc(s_wcD, 1)

                    @block.scalar
                    def _(a):
                        a.wait_ge(s_misc, 32)
                        for n, (src, wd, dst, sc) in enumerate(chunks):
                            if n % 2 != 1:
                                continue
                            a.wait_ge(s_wl[1], 16 * (n // 2 + 1))
                            if sc is None:
                                a.activation(out=dst, in_=wst[1][:, 0:wd], func=AF.Copy).then_inc(s_wcA, 1)
                            else:
                                a.activation(out=dst, in_=wst[1][:, 0:wd], func=AF.Copy, scale=sc).then_inc(s_wcA, 1)

                    @block.gpsimd
                    def _(g):
                        g.memset(identf[:], 0.0).then_inc(s_g0, 1)
                        g.wait_ge(s_g0, 1)
                        g.affine_select(out=identf[:], in_=identf[:], pattern=[[-1, 128]],
                                        compare_op=ALU.not_equal, fill=1.0, base=0, channel_multiplier=1).then_inc(s_g0, 1)
                        g.memset(ones_f[:], 1.0)
                        g.memset(mhalf[:], -0.5)
                        g.wait_ge(s_g0, 2)
                        g.tensor_copy(out=identb[:], in_=identf[:])

            free_in_p1 = [s_wl[0], s_wl[1], s_wcA, s_wcD, s_misc, s_g0]
            p1_mark = len(sem_log)
            phase1(nc, es1, locals())
            free_in_p2 = sems_since(p1_mark)

        with ExitStack() as es23:
            def sb23(name, shape, dt):
                return es23.enter_context(nc.sbuf_tensor(name, list(shape), dt))
            pre = dict(
                KP0=sb23("KP0", [96, S], BF16), VP0=sb23("VP0", [128, 64, 65], BF16), QP0=sb23("QP0", [96, OWN], BF16),
                BM0=sb23("BM0", [128, 10, 2, 256], BF16), KM0=sb23("KM0", [64, 32], BF16),
                pm_s=sb23("pm_s", [128, 32, 32], F32), oz_s=sb23("oz_s", [128, 32, 32], F32),
                cfar_s=sb23("cfar_s", [128, 8], F32), NM=sb23("NM", [128, 32, 96], BF16),
                ones_b=sb23("ones_b", [128, 64], BF16),
                s_c=sem("p3_c"), s_hl0=sem("p3_hl0"), s_bml0=sem("p3_bml0"),
            )
            if STOP_AFTER >= 2:
                phase2(nc, locals())
            with ExitStack() as es34:
                def sb34(name, shape, dt):
                    return es34.enter_context(nc.sbuf_tensor(name, list(shape), dt))
                pre4 = dict(WO=sb34("WO", [128, 8, D], BF16), WG=sb34("WG", [128, 8, D], BF16),
                            WP=sb34("WP", [128, 2, D], BF16), GF=sb34("GF", [128, D], F32),
                            s_w4=sem("p4_w"), s_w4g=sem("p4_wg"), s_w4f=sem("p4_wf"))
                if STOP_AFTER >= 3:
                    phase3(nc, locals())
                if STOP_AFTER >= 4:
                    phase4(nc, locals())
    return nc


def phase1(nc, es1, env):
    xv = env["xv"]; W1 = env["W1"]; identb = env["identb"]
    AKT = env["AKT"]; AV = env["AV"]; BKT = env["BKT"]; BV = env["BV"]
    AQT = env["AQT"]; BQT = env["BQT"]; GT = env["GT"]; KMT = env["KMT"]

    def sb(name, shape, dt):
        return es1.enter_context(nc.sbuf_tensor(name, list(shape), dt))

    def ps(name, shape, dt):
        return es1.enter_context(nc.psum_tensor(name, list(shape), dt))

    sem = env["sem"]

    xin = [sb(f"xin{i}", [128, 4, D], F32) for i in range(2)]
    xs = [sb(f"xs{i}", [128, 4, D], BF16) for i in range(2)]
    xT = [sb(f"xT{i}", [128, 8, 512], BF16) for i in range(2)]
    junk = sb("junk", [128, D], BF16)
    ssq = [sb(f"ssq{i}", [128, 4], F32) for i in range(2)]
    sq = [sb(f"sq{i}", [128, 4], F32) for i in range(2)]
    rr = [sb(f"rr{i}", [128, 4], F32) for i in range(2)]
    NS = 3
    stgA = [sb(f"stgA{i}", [128, 512], BF16) for i in range(NS)]
    stgD = [sb(f"stgD{i}", [128, 512], BF16) for i in range(NS)]
    ksum = sb("ksum", [128, 4, 32], F32)
    kmb = sb("kmb", [128, 4, 32], BF16)
    NB = 5
    PB = [ps(f"PB{i}", [128, 512], F32) for i in range(NB)]
    TP = [ps(f"TP{i}", [128, 8, 128], BF16) for i in range(2)]

    s_xl = [sem("p1_xl0"), sem("p1_xl1")]; s_sqd = sem("p1_sqd"); s_rrd = sem("p1_rrd"); s_xsd = sem("p1_xsd")
    s_trd = sem("p1_trd"); s_xtc = sem("p1_xtc"); s_ped = sem("p1_ped")
    s_evA = sem("p1_evA"); s_evD = sem("p1_evD"); s_odA = [sem(f"p1_odA{i}") for i in range(3)]; s_odD = [sem(f"p1_odD{i}") for i in range(3)]
    s_km = sem("p1_km"); s_kmo = sem("p1_kmo"); s_acc = sem("p1_acc")
    s_wp = env["s_wp"]; s_wgl = [sem("p1_wgl0"), sem("p1_wgl1")]; s_wgo = [sem("p1_wgo0"), sem("p1_wgo1")]; s_wgc = sem("p1_wgc")
    w_out = env["w_out"]; w_gate = env["w_gate"]; w_ple = env["w_ple"]; gple_s = env["gple_s"]
    WOd = env["WOd"]; WGd = env["WGd"]; WPd = env["WPd"]
    wgst = [sb(f"wgst{i}", [128, D], F32) for i in range(2)]
    wgo = [sb(f"wgo{i}", [128, D], BF16) for i in range(2)]

    jobs = []
    cnt = {"A": 0, "D": 0}

    def add(j, **kw):
        e = kw["eng"]
        kw["k"] = cnt[e]; cnt[e] += 1
        kw["pair"] = j; kw["n"] = len(jobs)
        jobs.append(kw)

    for j in range(NPAIR):
        t0 = 512 * j
        o0 = 256 * j
        add(j, typ="fm", sec="kv", eng="D", N=512, tok=0, wc=512, dst=AKT[:, t0:t0 + 512], op="copy", ks=None)
        for c in range(4):
            add(j, typ="fm", sec="kv", eng="D", N=512, tok=0, wc=1792 + 128 * c,
                dst=BKT[128 * c:128 * (c + 1), t0:t0 + 512], op="copy", ks=c)
        for tt in range(4):
            add(j, typ="tm", sec="kv", eng="D", N=512, tt=tt, wc=2304,
                dst=BV[t0 + 128 * tt:t0 + 128 * (tt + 1), :], op="copy", ks=None)
            add(j, typ="tm", sec="kv", eng="D", N=128, tt=tt, wc=640,
                dst=AV[t0 + 128 * tt:t0 + 128 * (tt + 1), :], op="copy", ks=None)
        for c in range(4):
            add(j, typ="fm", sec="qg", eng="D", N=256, tok=0, wc=128 * c,
                dst=AQT[128 * c:128 * (c + 1), o0:o0 + 256], op="qscale", ks=None)
            add(j, typ="fm", sec="qg", eng="A", N=256, tok=0, wc=768 + 128 * c,
                dst=GT[128 * c:128 * (c + 1), o0:o0 + 256], op="silu", ks=None)
        for c in range(4):
            add(j, typ="fm", sec="qg", eng="D", N=256, tok=0, wc=1280 + 128 * c,
                dst=BQT[128 * c:128 * (c + 1), o0:o0 + 256], op="qscale", ks=None)
            add(j, typ="fm", sec="qg", eng="A", N=256, tok=0, wc=2816 + 128 * c,
                dst=GT[512 + 128 * c:512 + 128 * (c + 1), o0:o0 + 256], op="silu", ks=None)
    NJ = len(jobs)
    JPP = NJ // NPAIR
    evsem = {"A": s_evA, "D": s_evD}
    odsem = {"A": s_odA, "D": s_odD}
    stg = {"A": stgA, "D": stgD}

    with nc.Block() as block:
        @block.gpsimd
        def _(g):
            nc.clear_and_free_semaphores(env["free_in_p1"])
            for j in range(NPAIR):
                if j >= 2:
                    g.wait_ge(s_xsd, j - 1)
                g.dma_start(out=xin[j % 2][:], in_=xv[512 * j:512 * (j + 1), :].rearrange("(tt p) d -> p tt d", p=128)
                            ).then_inc(s_xl[j % 2], 16)
                if j == 1:
                    g.dma_start(out=WOd[:, :], in_=w_out[:, :], max_dma_last_dim=4096).then_inc(s_wp, 16)
                    g.dma_start(out=WPd[:, :], in_=w_ple[:, :], max_dma_last_dim=4096).then_inc(s_wp, 16)
                if 2 <= j < 10:
                    kc = j - 2
                    g.dma_start(out=wgst[kc % 2][:], in_=w_gate[kc * 128:(kc + 1) * 128, :]).then_inc(s_wgl[kc % 2], 16)
                    g.wait_ge(s_wgl[kc % 2], 16 * (kc // 2 + 1))
                    if kc >= 2:
                        g.wait_ge(s_wgo[kc % 2], 16 * (kc // 2))
                    g.tensor_scalar(out=wgo[kc % 2][:], in0=wgst[kc % 2][:], scalar1=gple_s[:, kc:kc + 1], scalar2=0.0,
                                    op0=ALU.mult, op1=ALU.add).then_inc(s_wgc, 1)
                    g.wait_ge(s_wgc, kc + 1)
                    g.dma_start(out=WGd[kc * 128:(kc + 1) * 128, :], in_=wgo[kc % 2][:]).then_inc(s_wgo[kc % 2], 16)
            g.wait_ge(s_wgo[0], 16 * 4)
            g.wait_ge(s_wgo[1], 16 * 4)
            g.wait_ge(s_wp, 32)

        @block.tensor
        def _(t):
            def transposes(j):
                for tt in range(4):
                    n_t = 4 * j + tt
                    if tt == 0:
                        t.wait_ge(s_xsd, j + 1)
                    if n_t >= 2:
                        t.wait_ge(s_xtc, n_t - 1)
                    for kc in range(8):
                        ins = t.transpose(out=TP[n_t % 2][:, kc, :], in_=xs[j % 2][:, tt, kc * 128:(kc + 1) * 128],
                                          identity=identb[:])
                    ins.then_inc(s_trd, 1)

            def run_job(jb):
                n = jb["n"]; j = jb["pair"]
                if n >= NB:
                    pj = jobs[n - NB]
                    t.wait_ge(evsem[pj["eng"]], pj["k"] + 1)
                bank = PB[n % NB]
                for kc in range(8):
                    if jb["typ"] == "fm":
                        ins = t.matmul(bank[:, 0:jb["N"]], W1[:, kc, jb["wc"]:jb["wc"] + 128],
                                       xT[j % 2][:, kc, jb["tok"]:jb["tok"] + jb["N"]],
                                       start=(kc == 0), stop=(kc == 7))
                    else:
                        tt = jb["tt"]
                        ins = t.matmul(bank[:, 0:jb["N"]], xT[j % 2][:, kc, tt * 128:(tt + 1) * 128],
                                       W1[:, kc, jb["wc"]:jb["wc"] + jb["N"]],
                                       start=(kc == 0), stop=(kc == 7))
                ins.then_inc(s_ped, 1)

            transposes(0)
            for j in range(NPAIR):
                pj = [jb for jb in jobs if jb["pair"] == j]
                t.wait_ge(s_xtc, 4 * (j + 1))
                for jb in pj:
                    if jb["sec"] == "kv":
                        run_job(jb)
                qg = [jb for jb in pj if jb["sec"] == "qg"]
                for jb in qg[:8]:
                    run_job(jb)
                if j + 1 < NPAIR:
                    transposes(j + 1)
                for jb in qg[8:]:
                    run_job(jb)

        def evac(eng, e, jb):
            n = jb["n"]; k = jb["k"]; N = jb["N"]
            eng.wait_ge(s_ped, n + 1)
            if k >= NS:
                eng.wait_ge(odsem[e][k % NS], 16 * (k // NS))
            bank = PB[n % NB]
            dst = stg[e][k % NS][:, 0:N]
            if jb["op"] == "silu":
                eng.activation(out=dst, in_=bank[:, 0:N], func=AF.Silu).then_inc(evsem[e], 1)
            elif jb["op"] == "qscale":
                eng.tensor_scalar(out=dst, in0=bank[:, 0:N], scalar1=0.125, scalar2=None,
                                  op0=ALU.mult).then_inc(evsem[e], 1)
            else:
                if jb["ks"] is not None:
                    c = jb["ks"]; j = jb["pair"]
                    eng.tensor_reduce(out=ksum[:, c, 2 * j:2 * j + 2],
                                      in_=bank[:, 0:512].rearrange("p (b k) -> p b k", k=256),
                                      axis=AX.X, op=ALU.add)
                eng.tensor_copy(out=dst, in_=bank[:, 0:N]).then_inc(evsem[e], 1)

        @block.vector
        def _(v):
            def recip(j):
                v.wait_ge(s_sqd, j + 1)
                v.reciprocal(out=rr[j % 2][:], in_=sq[j % 2][:]).then_inc(s_rrd, 1)

            def xtcopies(j):
                for tt in range(4):
                    n_t = 4 * j + tt
                    v.wait_ge(s_trd, n_t + 1)
                    if tt == 0 and j >= 2:
                        v.wait_ge(s_ped, JPP * (j - 1))
                    v.tensor_copy(out=xT[j % 2][:, :, tt * 128:(tt + 1) * 128], in_=TP[n_t % 2][:, :, :]
                                  ).then_inc(s_xtc, 1)

            recip(0)
            xtcopies(0)
            for j in range(NPAIR):
                pj = [jb for jb in jobs if jb["pair"] == j and jb["eng"] == "D"]
                if j + 1 < NPAIR:
                    recip(j + 1)
                for jb in pj:
                    if jb["sec"] == "kv":
                        evac(v, "D", jb)
                qgd = [jb for jb in pj if jb["sec"] == "qg"]
                for jb in qgd[:4]:
                    evac(v, "D", jb)
                if j + 1 < NPAIR:
                    xtcopies(j + 1)
                for jb in qgd[4:]:
                    evac(v, "D", jb)
            v.tensor_scalar(out=kmb[:], in0=ksum[:], scalar1=1.0 / 256.0, scalar2=None,
                            op0=ALU.mult).then_inc(s_km, 1)

        @block.scalar
        def _(a):
            def norm(j):
                a.wait_ge(s_xl[j % 2], 16 * (j // 2 + 1))
                for tt in range(4):
                    ins = a.activation(out=junk[:], in_=xin[j % 2][:, tt, :], func=AF.Square,
                                       accum_out=ssq[j % 2][:, tt:tt + 1])
                ins.then_inc(s_acc, 1)
                a.wait_ge(s_acc, j + 1)
                a.activation(out=sq[j % 2][:], in_=ssq[j % 2][:], func=AF.Sqrt, scale=1.0 / D, bias=1e-6
                             ).then_inc(s_sqd, 1)
                a.wait_ge(s_rrd, j + 1)
                if j >= 2:
                    a.wait_ge(s_trd, 4 * (j - 1))
                for tt in range(4):
                    ins = a.activation(out=xs[j % 2][:, tt, :], in_=xin[j % 2][:, tt, :], func=AF.Copy,
                                       scale=rr[j % 2][:, tt:tt + 1])
                ins.then_inc(s_xsd, 1)

            norm(0)
            for j in range(NPAIR):
                if j + 1 < NPAIR:
                    norm(j + 1)
                for jb in jobs:
                    if jb["pair"] == j and jb["eng"] == "A":
                        evac(a, "A", jb)

        @block.sync
        def _(sync):
            for jb in jobs:
                e = jb["eng"]; k = jb["k"]
                sync.wait_ge(evsem[e], k + 1)
                sync.dma_start(out=jb["dst"], in_=stg[e][k % NS][:, 0:jb["N"]]).then_inc(odsem[e][k % NS], 16)
            sync.wait_ge(s_km, 1)
            sync.dma_start(out=KMT.rearrange("(c p) n -> p c n", p=128), in_=kmb[:]).then_inc(s_kmo, 16)
            for e_ in ("A", "D"):
                for sl_ in range(NS):
                    sync.wait_ge(odsem[e_][sl_], 16 * len(range(sl_, cnt[e_], NS)))
            sync.wait_ge(s_kmo, 16)


def _mk(nc, est):
    def sb(name, shape, dt):
        return est.enter_context(nc.sbuf_tensor(name, list(shape), dt))

    def ps(name, shape, dt=F32):
        return est.enter_context(nc.psum_tensor(name, list(shape), dt))
    return sb, ps


def phase2(nc, env):
    sem = env["sem"]; ones_f = env["ones_f"]
    AKT = env["AKT"]; AV = env["AV"]; AQT = env["AQT"]; GT = env["GT"]; MT = env["MT"]
    bswa_d = env["bswa_d"]; sinkb = env["sinkb"]
    pre = env["pre"]; BKT = env["BKT"]; BV = env["BV"]; BQT = env["BQT"]; KMT = env["KMT"]
    bmoba_d = env["bmoba_d"]; pm_d = env["pm_d"]; oz_d = env["oz_d"]; onehot_d = env["onehot_d"]; cfar_d = env["cfar_d"]
    with ExitStack() as e2:
        sb, ps = _mk(nc, e2)
        AQs = sb("AQs", [128, 4, OWN], BF16)
        AKs = sb("AKs", [128, S], BF16)
        AVs = sb("AVs", [128, 64, 2, 65], BF16)
        bsw = sb("bsw", [128, 4, 8, 128], F32)
        esk = sb("esk", [128, 8], F32)
        esk_rep = sb("esk_rep", [1, 8, 128], F32)
        esk_hi = sb("esk_hi", [1, 8, 128], BF16)
        esk_lo = sb("esk_lo", [1, 8, 128], BF16)
        sel65 = sb("sel65", [1, 65], BF16)
        bcs = [sb(f"s_bcs{i}", [64, 512], F32) for i in range(4)]
        lnr = sb("s_lnr", [128, 512], F32)
        tmp = [sb(f"s_tmp{i}", [128, 4, 128], F32) for i in range(2)]
        PTb = [sb(f"s_pt{i}", [128, 512], BF16) for i in range(3)]
        rrow = [sb(f"s_rrow{i}", [128, 512], F32) for i in range(4)]
        t1b = [sb(f"s_t1{i}", [64, 512], F32) for i in range(4)]
        mst = [sb(f"s_mst{i}", [64, 4, 128], BF16) for i in range(4)]
        gsw = [sb(f"s_g{i}", [64, 4, 128], BF16) for i in range(4)]
        SP = [ps(f"s_SP{i}", [128, 512]) for i in range(4)]
        OP = [ps(f"s_OP{i}", [128, 512]) for i in range(2)]
        s_ld = sem("p2_ld"); s_ldc = [sem(f"p2_ldc{i}") for i in range(4)]; s_qk = sem("p2_qk"); s_add = sem("p2_add"); s_exp = sem("p2_exp"); s_pv = sem("p2_pv")
        s_rr = sem("p2_rr"); s_t1 = sem("p2_t1"); s_bc = sem("p2_bc"); s_mx = sem("p2_mx"); s_mo = [sem(f"p2_mo{i}") for i in range(4)]
        s_gl = [sem(f"p2_gl{i}") for i in range(4)]; s_es = sem("p2_es"); s_ms = sem("p2_ms"); s_g2 = sem("p2_g2"); s_er = sem("p2_er"); s_bcs = sem("p2_bcs")

        NU = 64
        units = []
        q = 0
        for m in range(NU):
            T, kvh = m // 2, m % 2
            i, u = T // 2, T % 2
            if u == 0:
                cl = ([(4 * (i - 1) + 3, 2)] if i > 0 else []) + [(4 * i + 3, 3), (4 * i, 0)]
            else:
                cl = [(4 * i, 1), (4 * i + 1, 0)]
            cands = []
            for (kt, kind) in cl:
                cands.append(dict(q=q, kt=kt, kind=kind))
                q += 1
            units.append(dict(m=m, T=T, kvh=kvh, cands=cands))
        NLD = 2
        NLDC = 2 + 1 + 2

        with nc.Block() as block:
            @block.sync
            def _(sync):
                sync.dma_start(out=esk[:], in_=sinkb[:, :]).then_inc(s_ld, 16)
                sync.dma_start(out=bsw[:], in_=bswa_d[:, :, :, :]).then_inc(s_ld, 16)
                for g in range(4):
                    for kvh in range(2):
                        sync.dma_start(out=AQs[kvh * 64:(kvh + 1) * 64, :, 1024 * g:1024 * (g + 1)],
                                       in_=AQT[kvh * 256:(kvh + 1) * 256, 1024 * g:1024 * (g + 1)].rearrange("(hh d) t -> d hh t", d=64)
                                       ).then_inc(s_ldc[g], 16)
                    sync.dma_start(out=AKs[:, 2048 * g:2048 * (g + 1)], in_=AKT[:, 2048 * g:2048 * (g + 1)]).then_inc(s_ldc[g], 16)
                    for k in range(2):
                        sync.dma_start(out=AVs[:, 16 * g:16 * (g + 1), k, 0:64],
                                       in_=AV[2048 * g:2048 * (g + 1), k * 64:(k + 1) * 64].rearrange("(t p) d -> p t d", p=128)
                                       ).then_inc(s_ldc[g], 16)
                sync.dma_start(out=pre["KP0"][64:96, :], in_=onehot_d[:, :]).then_inc(pre["s_c"], 16)
                sync.dma_start(out=pre["pm_s"][:], in_=pm_d[:, :, :]).then_inc(pre["s_c"], 16)
                sync.dma_start(out=pre["oz_s"][:], in_=oz_d[:, :, :]).then_inc(pre["s_c"], 16)
                sync.dma_start(out=pre["cfar_s"][:], in_=cfar_d[:, :]).then_inc(pre["s_c"], 16)
                sync.dma_start(out=pre["KP0"][0:64, :], in_=BKT[0:64, :]).then_inc(pre["s_hl0"], 16)
                sync.dma_start(out=pre["QP0"][0:64, :], in_=BQT[0:64, :]).then_inc(pre["s_hl0"], 16)
                for g in range(4):
                    sync.dma_start(out=pre["VP0"][:, 16 * g:16 * (g + 1), 0:64],
                                   in_=BV[2048 * g:2048 * (g + 1), 0:64].rearrange("(t p) d -> p t d", p=128)
                                   ).then_inc(pre["s_hl0"], 16)
                sync.dma_start(out=pre["KM0"][:], in_=KMT[0:64, :]).then_inc(pre["s_hl0"], 16)
                for un in units:
                    m, T, kvh = un["m"], un["T"], un["kvh"]
                    sync.wait_ge(s_mx, m + 1)
                    sync.dma_start(out=MT[kvh * 256:(kvh + 1) * 256, T * 128:(T + 1) * 128].rearrange("(hh d) t -> d hh t", d=64),
                                   in_=mst[m % 4][:]).then_inc(s_mo[m % 4], 16)
                for k_ in range(4):
                    sync.wait_ge(s_mo[k_], 16 * (NU // 4))
                sync.wait_ge(pre["s_c"], 16 * 4)
                sync.wait_ge(pre["s_hl0"], 16 * 7)

            @block.gpsimd
            def _(g):
                nc.clear_and_free_semaphores(env["free_in_p2"])
                g.memset(AVs[:, :, :, 64:65], 1.0)
                g.memset(sel65[:], 0.0).then_inc(s_g2, 1)
                g.wait_ge(s_g2, 1)
                g.memset(sel65[0:1, 64:65], 1.0)
                g.memset(rrow[0][:], 1.0)
                g.memset(rrow[2][:], 1.0)
                g.memset(rrow[3][:], 1.0)
                g.memset(rrow[1][:], 1.0).then_inc(s_ms, 1)
                g.memset(pre["VP0"][:, :, 64:65], 1.0)
                g.memset(pre["NM"][:], 0.0)
                g.memset(pre["ones_b"][:], 1.0)
                g.dma_start(out=pre["BM0"][:], in_=bmoba_d[0], max_dma_last_dim=4096).then_inc(pre["s_bml0"], 16)

                def gload(un):
                    m, T, kvh = un["m"], un["T"], un["kvh"]
                    if m >= 4:
                        g.wait_ge(s_t1, m - 3)
                    g.dma_start(out=gsw[m % 4][:],
                                in_=GT[kvh * 256:(kvh + 1) * 256, T * 128:(T + 1) * 128].rearrange("(hh d) t -> d hh t", d=64)
                                ).then_inc(s_gl[m % 4], 16)

                for k_ in range(3):
                    gload(units[k_])
                for un in units:
                    m = un["m"]
                    if m + 3 < NU:
                        gload(units[m + 3])
                    g.wait_ge(s_bcs, m + 1)
                    g.wait_ge(s_t1, m + 1)
                    if m >= 4:
                        g.wait_ge(s_mo[m % 4], 16 * (m // 4))
                    g.tensor_tensor(out=mst[m % 4][:].rearrange("p h q -> p (h q)"), in0=t1b[m % 4][:],
                                    in1=bcs[m % 4][:], op=ALU.mult).then_inc(s_mx, 1)

            @block.tensor
            def _(t):
                def QK(un):
                    T, kvh = un["T"], un["kvh"]
                    if un["m"] % 16 == 0:
                        t.wait_ge(s_ldc[un["m"] // 16], 16 * NLDC)
                    for cd in un["cands"]:
                        qq = cd["q"]
                        if qq >= 4:
                            t.wait_ge(s_add, qq - 3)
                        t.matmul(SP[qq % 4][:, :], AKs[kvh * 64:(kvh + 1) * 64, cd["kt"] * 128:(cd["kt"] + 1) * 128],
                                 AQs[kvh * 64:(kvh + 1) * 64, :, T * 128:(T + 1) * 128], start=True, stop=True
                                 ).then_inc(s_qk, 1)

                def PV(un):
                    m, kvh = un["m"], un["kvh"]
                    if m >= 2:
                        t.wait_ge(s_t1, m - 1)
                    first = True
                    for cd in un["cands"]:
                        qq = cd["q"]
                        t.wait_ge(s_exp, qq + 1)
                        ins = t.matmul(OP[m % 2][0:65, :], AVs[:, cd["kt"], kvh, :], PTb[qq % 3][:, :],
                                       start=first, stop=False)
                        first = False
                        if cd is un["cands"][-1]:
                            ins = t.matmul(OP[m % 2][0:65, :], sel65[0:1, 0:65], esk_hi[0:1, kvh * 4:(kvh + 1) * 4, :],
                                           start=False, stop=True)
                        ins.then_inc(s_pv, 1)

                t.wait_ge(s_ld, 16 * NLD)
                t.wait_ge(s_ms, 1)
                t.wait_ge(s_er, 1)
                QK(units[0]); QK(units[1])
                for m in range(NU):
                    PV(units[m])
                    if m + 2 < NU:
                        QK(units[m + 2])

            @block.vector
            def _(v):
                def adds(un):
                    kvh = un["kvh"]
                    for cd in un["cands"]:
                        qq = cd["q"]
                        v.wait_ge(s_qk, qq + 1)
                        if qq >= 2:
                            v.wait_ge(s_exp, qq - 1)
                        v.tensor_tensor(out=tmp[qq % 2][:], in0=SP[qq % 4][:, :].rearrange("p (h q) -> p h q", q=128),
                                        in1=bsw[:, cd["kind"], kvh * 4:(kvh + 1) * 4, :], op=ALU.add).then_inc(s_add, 1)

                def post(un):
                    m, kvh = un["m"], un["kvh"]
                    lastq = un["cands"][-1]["q"]
                    v.wait_ge(s_pv, lastq + 1)
                    v.wait_ge(s_gl[m % 4], 16 * (m // 4 + 1))
                    if m >= 4:
                        v.wait_ge(s_mx, m - 3)
                    v.tensor_tensor(out=t1b[m % 4][:], in0=OP[m % 2][0:64, :],
                                    in1=gsw[m % 4][:].rearrange("p h q -> p (h q)"), op=ALU.mult).then_inc(s_t1, 1)

                def fin(un):
                    m = un["m"]
                    v.wait_ge(s_rr, m + 1)
                    if m >= 4:
                        v.wait_ge(s_mx, m - 3)
                    v.stream_shuffle(out=bcs[m % 4][0:64, :], in_=rrow[m % 4][64:128, :], mask=[0] * 32
                                     ).then_inc(s_bcs, 1)

                v.wait_ge(s_ld, 16 * NLD)
                v.wait_ge(s_es, 1)
                v.tensor_copy(out=esk_rep[0:1, :, :], in_=esk[0:1, :].unsqueeze(2).broadcast_to([1, 8, 128]))
                v.tensor_copy(out=esk_hi[0:1, :, :], in_=esk_rep[0:1, :, :])
                v.tensor_tensor(out=esk_lo[0:1, :, :], in0=esk_rep[0:1, :, :], in1=esk_hi[0:1, :, :], op=ALU.subtract
                                ).then_inc(s_er, 1)
                adds(units[0]); adds(units[1])
                for m in range(NU):
                    post(units[m])
                    if m + 2 < NU:
                        adds(units[m + 2])
                    if m >= 1:
                        fin(units[m - 1])
                fin(units[NU - 1])

            @block.scalar
            def _(a):
                a.wait_ge(s_ld, 16 * NLD)
                a.activation(out=esk[:], in_=esk[:], func=AF.Exp).then_inc(s_es, 1)
                def recip_act(un):
                    m = un["m"]
                    a.wait_ge(s_pv, un["cands"][-1]["q"] + 1)
                    if m >= 4:
                        a.wait_ge(s_bcs, m - 3)
                    a.activation(out=lnr[64:65, :], in_=OP[m % 2][64:65, :], func=AF.Ln)
                    a.activation(out=rrow[m % 4][64:65, :], in_=lnr[64:65, :], func=AF.Exp, scale=-1.0)
                    a.activation(out=rrow[m % 4][96:97, :], in_=lnr[64:65, :], func=AF.Exp, scale=-1.0).then_inc(s_rr, 1)

                for un in units:
                    for cd in un["cands"]:
                        qq = cd["q"]
                        a.wait_ge(s_add, qq + 1)
                        if qq >= 3:
                            a.wait_ge(s_pv, qq - 2)
                        a.activation(out=PTb[qq % 3][:], in_=tmp[qq % 2][:].rearrange("p h q -> p (h q)"), func=AF.Exp
                                     ).then_inc(s_exp, 1)
                    if un["m"] >= 1:
                        recip_act(units[un["m"] - 1])
                recip_act(units[NU - 1])


def phase3(nc, env):
    sem = env["sem"]; identb = env["identb"]
    BKT = env["BKT"]; BV = env["BV"]; BQT = env["BQT"]; GT = env["GT"]; MT = env["MT"]; KMT = env["KMT"]
    bmoba_d = env["bmoba_d"]; pm_d = env["pm_d"]; oz_d = env["oz_d"]; onehot_d = env["onehot_d"]; cfar_d = env["cfar_d"]
    with ExitStack() as e3:
        sb, ps = _mk(nc, e3)
        pre = env["pre"]
        KP = [pre["KP0"], sb("KP1", [96, S], BF16)]
        VP = [pre["VP0"], sb("VP1", [128, 64, 65], BF16)]
        QP = [pre["QP0"], sb("QP1", [96, OWN], BF16)]
        BM = [pre["BM0"], sb("BM1", [128, 10, 2, 256], BF16)]
        KM = [pre["KM0"], sb("KM1", [64, 32], BF16)]
        pm_s = pre["pm_s"]; oz_s = pre["oz_s"]; cfar_s = pre["cfar_s"]
        smb = sb("smb", [128, 32, 32], F32)
        m8 = sb("m8", [128, 32, 8], F32)
        NM = pre["NM"]; ones_b = pre["ones_b"]
        PTg = [sb(f"PTg{i}", [128, 1024], BF16) for i in range(3)]
        obs = [sb(f"obs{i}", [65, 512], F32) for i in range(2)]
        rrow = [sb(f"rrowm{i}", [96, 512], F32) for i in range(2)]
        bcm = [sb(f"bcm{i}", [64, 512], F32) for i in range(2)]
        t1m = [sb(f"t1m{i}", [64, 512], F32) for i in range(2)]
        mstm = [sb(f"mstm{i}", [64, 512], BF16) for i in range(2)]
        gmm = [sb(f"gmm{i}", [64, 512], BF16) for i in range(2)]
        NSB = 2
        SBg = [ps(f"m_SB{i}", [128, 1024]) for i in range(NSB)]
        SELb = [ps(f"m_SEL{i}", [128, 512]) for i in range(2)]
        OB = [ps(f"m_OB{i}", [128, 512]) for i in range(2)]
        TPn = [SELb[0][:, :].bitcast(BF16), SELb[1][:, :].bitcast(BF16)]

        s_c = pre["s_c"]; s_hl = [pre["s_hl0"], sem("p3_hl1")]; s_bml = [pre["s_bml0"], sem("p3_bml1")]
        s_sc = sem("p3_sc"); s_m8 = sem("p3_m8"); s_nm = sem("p3_nm"); s_bmf = sem("p3_bmf")
        s_selt = sem("p3_selt"); s_selcp = sem("p3_selcp"); s_qk = sem("p3_qk")
        s_exp = sem("p3_exp"); s_pv = sem("p3_pv"); s_obc = sem("p3_obc"); s_rr = sem("p3_rr"); s_bcs = sem("p3_bcs")
        s_t1 = sem("p3_t1"); s_bc = sem("p3_bc"); s_mx = sem("p3_mx"); s_mo = [sem("p3_mo0"), sem("p3_mo1")]
        s_gl = [sem("p3_gl0"), sem("p3_gl1")]; s_ms = sem("p3_ms"); s_gq = sem("p3_gq")

        heads = []
        gidx = 0
        kidx = 0
        near_g = []
        for h in range(8):
            jl = []
            pairs = []
            for spi, sp in enumerate([7, 0, 6, 1, 5, 2, 4, 3]):
                gp = 8 * h + spi
                i0 = 2 * sp
                pr = dict(gp=gp, sp=sp, h=h, first=gidx)
                nkb = 4 * sp + 4
                for vb in range(nkb):
                    e0 = 2 * i0 + 1 - vb
                    jb = dict(g=gidx, h=h, sp=sp, gp=gp, vb=vb, e0=e0, near=(e0 <= 5),
                              firstjob=(vb == 0), lastjob=(vb == nkb - 1))
                    if jb["near"]:
                        jb["k"] = kidx; kidx += 1; near_g.append(gidx)
                    jl.append(jb)
                    gidx += 1
                pr["last"] = gidx - 1
                pairs.append(pr)
            heads.append(dict(h=h, jobs=jl, pairs=pairs))
        NHL = 7

        with nc.Block() as block:
            @block.sync
            def _(sync):
                sync.dma_start(out=KP[1][64:96, :], in_=onehot_d[:, :]).then_inc(s_c, 16)

                def loads(h):
                    hb = h % 2
                    sync.dma_start(out=KP[hb][0:64, :], in_=BKT[h * 64:(h + 1) * 64, :]).then_inc(s_hl[hb], 16)
                    sync.dma_start(out=QP[hb][0:64, :], in_=BQT[h * 64:(h + 1) * 64, :]).then_inc(s_hl[hb], 16)
                    for g in range(4):
                        sync.dma_start(out=VP[hb][:, 16 * g:16 * (g + 1), 0:64],
                                       in_=BV[2048 * g:2048 * (g + 1), h * 64:(h + 1) * 64].rearrange("(t p) d -> p t d", p=128)
                                       ).then_inc(s_hl[hb], 16)
                    sync.dma_start(out=KM[hb][:], in_=KMT[h * 64:(h + 1) * 64, :]).then_inc(s_hl[hb], 16)

                loads(1)
                p4 = env["pre4"]
                sync.dma_start(out=p4["WO"][:], in_=env["WOd"].rearrange("(c p) n -> p c n", p=128)).then_inc(p4["s_w4"], 16)
                sync.dma_start(out=p4["WG"][:], in_=env["WGd"].rearrange("(c p) n -> p c n", p=128)).then_inc(p4["s_w4g"], 16)
                sync.dma_start(out=p4["WP"][:], in_=env["WPd"].rearrange("(c p) n -> p c n", p=128)).then_inc(p4["s_w4g"], 16)
                sync.dma_start(out=p4["GF"][:], in_=env["gfin"][:, :]).then_inc(p4["s_w4f"], 16)
                for hd in heads:
                    h = hd["h"]
                    for pr in hd["pairs"]:
                        gp = pr["gp"]
                        sync.wait_ge(s_mx, gp + 1)
                        sync.dma_start(out=MT[512 + h * 64:512 + (h + 1) * 64, pr["sp"] * 512:(pr["sp"] + 1) * 512],
                                       in_=mstm[gp % 2][:]).then_inc(s_mo[gp % 2], 16)
                    if h + 2 < 8:
                        loads(h + 2)
                sync.wait_ge(s_mo[0], 16 * 32)
                sync.wait_ge(s_mo[1], 16 * 32)
                sync.wait_ge(p4["s_w4"], 16)
                sync.wait_ge(p4["s_w4g"], 32)
                sync.wait_ge(p4["s_w4f"], 16)

            @block.gpsimd
            def _(g):
                gq = [0]

                def bmload(h):
                    g.dma_start(out=BM[h % 2][:], in_=bmoba_d[h], max_dma_last_dim=4096).then_inc(s_bml[h % 2], 16)

                g.memset(rrow[0][:], 1.0)
                g.memset(rrow[1][:], 1.0)
                g.memset(VP[1][:, :, 64:65], 1.0).then_inc(s_ms, 1)
                bmload(1)
                for hd in heads:
                    h = hd["h"]
                    for pr in hd["pairs"]:
                        gp = pr["gp"]
                        if gp >= 2:
                            g.wait_ge(s_t1, gp - 1)
                        g.dma_start(out=gmm[gp % 2][:],
                                    in_=GT[512 + h * 64:512 + (h + 1) * 64, pr["sp"] * 512:(pr["sp"] + 1) * 512]
                                    ).then_inc(s_gl[gp % 2], 16)
                        g.wait_ge(s_obc, gp + 1)
                        g.wait_ge(s_gl[gp % 2], 16 * (gp // 2 + 1))
                        g.tensor_tensor(out=t1m[gp % 2][:], in0=obs[gp % 2][0:64, :], in1=gmm[gp % 2][:], op=ALU.mult
                                        ).then_inc(s_t1, 1)
                        g.wait_ge(s_bcs, gp + 1)
                        g.wait_ge(s_t1, gp + 1)
                        if gp >= 2:
                            g.wait_ge(s_mo[gp % 2], 16 * (gp // 2))
                        g.tensor_tensor(out=mstm[gp % 2][:], in0=t1m[gp % 2][:], in1=bcm[gp % 2][:], op=ALU.mult
                                        ).then_inc(s_mx, 1)
                    if h + 2 < 8:
                        g.wait_ge(s_qk, hd["jobs"][-1]["g"] + 1)
                        bmload(h + 2)

            @block.tensor
            def _(t):
                t.wait_ge(s_c, 16 * 5)
                t.wait_ge(s_ms, 1)

                def QK(jb):
                    g_, hb, sp, vb = jb["g"], jb["h"] % 2, jb["sp"], jb["vb"]
                    e0 = jb["e0"]
                    for ks in range(2):
                        kt = 2 * vb + ks
                        if e0 < 0:
                            ins = t.matmul(SBg[g_ % NSB][:, ks * 512 + 256:(ks + 1) * 512],
                                           KP[hb][0:96, kt * 128:(kt + 1) * 128],
                                           QP[hb][0:96, sp * 512 + 256:(sp + 1) * 512], start=True, stop=False)
                        else:
                            ins = t.matmul(SBg[g_ % NSB][:, ks * 512:(ks + 1) * 512], KP[hb][0:96, kt * 128:(kt + 1) * 128],
                                           QP[hb][0:96, sp * 512:(sp + 1) * 512], start=True, stop=not jb["near"])
                        if ks == 0 and g_ >= NSB:
                            ins._wait_ge(s_exp, g_ - NSB + 1)
                        if jb["near"]:
                            if e0 < 0:
                                ins = t.matmul(SBg[g_ % NSB][:, ks * 512 + 256:(ks + 1) * 512], identb[:, :],
                                               BM[hb][:, e0 + 4, ks, :], start=False, stop=True)
                            elif e0 <= 3:
                                ins = t.matmul(SBg[g_ % NSB][:, ks * 512:(ks + 1) * 512], identb[:, :],
                                               BM[hb][:, e0 + 2:e0 + 5:2, ks, :], start=False, stop=True)
                            else:
                                ins = t.matmul(SBg[g_ % NSB][:, ks * 512:ks * 512 + 256], identb[:, :],
                                               BM[hb][:, e0 + 2, ks, :], start=False, stop=True)
                    ins.then_inc(s_qk, 1)

                def PV(jb):
                    g_, hb, vb, gp = jb["g"], jb["h"] % 2, jb["vb"], jb["gp"]
                    if jb["firstjob"] and gp >= 2:
                        t.wait_ge(s_obc, gp - 1)
                    for ks in range(2):
                        kt = 2 * vb + ks
                        if jb["e0"] < 0:
                            ins = t.matmul(OB[gp % 2][0:65, 256:512], VP[hb][:, kt, :],
                                           PTg[g_ % 3][:, ks * 512 + 256:(ks + 1) * 512],
                                           start=False, stop=(jb["lastjob"] and ks == 1))
                        else:
                            ins = t.matmul(OB[gp % 2][0:65, :], VP[hb][:, kt, :], PTg[g_ % 3][:, ks * 512:(ks + 1) * 512],
                                           start=(jb["firstjob"] and ks == 0), stop=(jb["lastjob"] and ks == 1))
                        if ks == 0:
                            ins._wait_ge(s_exp, g_ + 1)
                    ins.then_inc(s_pv, 1)

                def sel_scores(h, half):
                    hb = h % 2
                    if half == 0:
                        t.wait_ge(s_hl[hb], 16 * NHL * (h // 2 + 1))
                        if h >= 1:
                            t.wait_ge(s_selcp, 8 * h)
                    else:
                        t.wait_ge(s_m8, 2 * h + 1)
                    for T in range(16 * half, 16 * half + 16):
                        ins = t.matmul(SELb[half][:, (T % 16) * 32:(T % 16 + 1) * 32], QP[hb][0:64, T * 128:(T + 1) * 128],
                                       KM[hb][0:64, :], start=True, stop=True)
                    ins.then_inc(s_sc, 1)

                def sel_tr(h, gq_):
                    if gq_ == 0:
                        t.wait_ge(s_nm, h + 1)
                        t.wait_ge(s_m8, 2 * h + 2)
                    if gq_ >= 2:
                        t.wait_ge(s_selcp, 8 * h + gq_ - 1)
                    for tq in range(4):
                        T = 4 * gq_ + tq
                        ins = t.transpose(out=TPn[gq_ % 2][0:96, tq * 128:(tq + 1) * 128], in_=NM[:, T, :],
                                          identity=identb[:])
                    ins.then_inc(s_selt, 1)

                sel_scores(0, 0)
                sel_scores(0, 1)
                for gq_ in range(8):
                    sel_tr(0, gq_)
                for hd in heads:
                    h = hd["h"]; hb = h % 2
                    t.wait_ge(s_selcp, 8 * (h + 1))
                    t.wait_ge(s_bmf, h + 1)
                    jl = hd["jobs"]
                    QK(jl[0])
                    if NSB >= 3:
                        QK(jl[1])
                    for idx, jb in enumerate(jl):
                        if NSB >= 3:
                            PV(jb)
                            if idx + 2 < len(jl):
                                QK(jl[idx + 2])
                        else:
                            if idx + 1 < len(jl):
                                QK(jl[idx + 1])
                            PV(jb)
                        if h + 1 < 8:
                            if idx == 36:
                                sel_scores(h + 1, 0)
                            if idx == 46:
                                sel_scores(h + 1, 1)
                            if idx >= 64 and (idx - 64) % 8 == 0 and (idx - 64) // 8 < 8:
                                sel_tr(h + 1, (idx - 64) // 8)

            @block.vector
            def _(v):
                v.wait_ge(s_c, 16 * 5)

                def post(pr):
                    gp = pr["gp"]
                    v.wait_ge(s_pv, pr["last"] + 1)
                    if gp >= 2:
                        v.wait_ge(s_t1, gp - 1)
                    v.tensor_copy(out=obs[gp % 2][:], in_=OB[gp % 2][0:65, :]).then_inc(s_obc, 1)
                    for c in range(4):
                        rpend.append((gp, c))

                rpend = []

                def rchunk():
                    if not rpend:
                        return
                    gp, c = rpend.pop(0)
                    if c == 0:
                        v.wait_ge(s_obc, gp + 1)
                        if gp >= 2:
                            v.wait_ge(s_bcs, gp - 1)
                    ins = v.reciprocal(out=rrow[gp % 2][64:65, c * 128:(c + 1) * 128],
                                       in_=obs[gp % 2][64:65, c * 128:(c + 1) * 128])
                    if c == 3:
                        ins.then_inc(s_rr, 1)
                        v.wait_ge(s_rr, gp + 1)
                        if gp >= 2:
                            v.wait_ge(s_mx, gp - 1)
                        v.stream_shuffle(out=bcm[gp % 2][0:32, :], in_=rrow[gp % 2][64:96, :], mask=[0] * 32)
                        v.stream_shuffle(out=bcm[gp % 2][32:64, :], in_=rrow[gp % 2][64:96, :], mask=[0] * 32
                                         ).then_inc(s_bcs, 1)

                def bmfold(h):
                    hb = h % 2
                    v.wait_ge(s_bml[hb], 16 * (h // 2 + 1))
                    v.tensor_scalar(out=BM[hb][:], in0=BM[hb][:], scalar1=cfar_s[:, h:h + 1], scalar2=None,
                                    op0=ALU.subtract).then_inc(s_bmf, 1)

                def sel_dve(h):
                    for half in range(2):
                        v.wait_ge(s_sc, 2 * h + half + 1)
                        v.tensor_tensor(out=smb[:, 16 * half:16 * (half + 1), :],
                                        in0=SELb[half][:, :].rearrange("p (t n) -> p t n", n=32),
                                        in1=pm_s[:, 16 * half:16 * (half + 1), :], op=ALU.add)
                        for T in range(16 * half, 16 * half + 16):
                            ins = v.max(out=m8[:, T, :], in_=smb[:, T, :])
                        ins.then_inc(s_m8, 1)
                    v.wait_ge(s_m8, 2 * h + 2)
                    v.tensor_tensor(out=smb[:], in0=smb[:], in1=m8[:, :, 2:3].broadcast_to([128, 32, 32]), op=ALU.is_ge)
                    v.tensor_scalar(out=smb[:], in0=smb[:], scalar1=-1.0, scalar2=-NEG, op0=ALU.add, op1=ALU.mult)
                    v.tensor_tensor(out=smb[:], in0=smb[:], in1=pm_s[:], op=ALU.add)
                    v.tensor_tensor(out=smb[:], in0=smb[:], in1=oz_s[:], op=ALU.mult)
                    v.tensor_scalar(out=NM[:, :, 64:96], in0=smb[:], scalar1=cfar_s[:, h:h + 1], scalar2=None,
                                    op0=ALU.add).then_inc(s_nm, 1)

                def sel_cp(h, gq_):
                    hb = h % 2
                    v.wait_ge(s_selt, 8 * h + gq_ + 1)
                    v.tensor_copy(out=QP[hb][64:96, gq_ * 512:(gq_ + 1) * 512], in_=TPn[gq_ % 2][64:96, 0:512]
                                  ).then_inc(s_selcp, 1)

                bmfold(0)
                sel_dve(0)
                for gq_ in range(8):
                    sel_cp(0, gq_)
                for hd in heads:
                    h = hd["h"]; hb = h % 2
                    for pi, pr in enumerate(hd["pairs"]):
                        post(pr)
                        while rpend:
                            rchunk()
                        if h + 1 < 8:
                            if pi == 1:
                                sel_dve(h + 1)
                            if pi == 3:
                                for gq_ in range(0, 4):
                                    sel_cp(h + 1, gq_)
                            if pi == 4:
                                for gq_ in range(4, 6):
                                    sel_cp(h + 1, gq_)
                            if pi == 5:
                                for gq_ in range(6, 8):
                                    sel_cp(h + 1, gq_)
                                bmfold(h + 1)

            @block.scalar
            def _(a):
                a.wait_ge(s_c, 16 * 5)
                for hd in heads:
                    h = hd["h"]; hb = h % 2
                    for jb in hd["jobs"]:
                        g_ = jb["g"]
                        if g_ >= 3:
                            a.wait_ge(s_pv, g_ - 2)
                        a.wait_ge(s_qk, g_ + 1)
                        if jb["e0"] < 0:
                            a.activation(out=PTg[g_ % 3][:].rearrange("p (k c) -> p k c", k=2)[:, :, 256:512],
                                         in_=SBg[g_ % NSB][:, :].rearrange("p (k c) -> p k c", k=2)[:, :, 256:512],
                                         func=AF.Exp).then_inc(s_exp, 1)
                        else:
                            a.activation(out=PTg[g_ % 3][:], in_=SBg[g_ % NSB][:, :], func=AF.Exp).then_inc(s_exp, 1)


def phase4(nc, env):
    sem = env["sem"]; identb = env["identb"]
    WOd = env["WOd"]; WGd = env["WGd"]; WPd = env["WPd"]; gfin = env["gfin"]
    xv = env["xv"]; pown = env["pown"]; MT = env["MT"]; out_d = env["out_d"]
    NT = 32
    with ExitStack() as e4:
        sb, ps = _mk(nc, e4)
        p4 = env["pre4"]
        WO = p4["WO"]; WG = p4["WG"]; WP = p4["WP"]; GF = p4["GF"]
        s_xs1d = sem("p4_xs1d")
        xo = [sb(f"f_xo{i}", [128, D], F32) for i in range(2)]
        pt = [sb(f"f_pt{i}", [128, 256], F32) for i in range(2)]
        mt = [sb(f"f_mt{i}", [128, 8, 128], BF16) for i in range(2)]
        x1 = [sb(f"f_x1{i}", [128, D], F32) for i in range(2)]
        xs1 = sb("f_xs1", [128, D], BF16)
        pb = sb("f_pb", [128, 256], BF16)
        xs1T = sb("f_xs1T", [128, 8, 128], BF16)
        pT = sb("f_pT", [128, 2, 128], BF16)
        sg = sb("f_sg", [128, D], F32)
        tq = sb("f_tq", [128, D], F32)
        x2 = sb("f_x2", [128, D], F32)
        ob = [sb(f"f_ob{i}", [128, D], F32) for i in range(2)]
        junk = sb("f_junk", [128, D], BF16)
        ssq1 = sb("f_ssq1", [128, 1], F32); sq1 = sb("f_sq1", [128, 1], F32); r1 = sb("f_r1", [128, 1], F32)
        ssq2 = sb("f_ssq2", [128, 1], F32); sq2 = sb("f_sq2", [128, 1], F32); r2 = sb("f_r2", [128, 1], F32)
        Y = [ps(f"f_Y{i}", [128, 512]) for i in range(2)]
        G = [ps(f"f_G{i}", [128, 512]) for i in range(2)]
        PP = [ps(f"f_PP{i}", [128, 512]) for i in range(2)]
        TPx = ps("f_TPx", [128, 8, 128], BF16)
        TPp = ps("f_TPp", [128, 2, 128], BF16)
        s_ld = [sem("p4_ld0"), sem("p4_ld1")]; s_A = sem("p4_A"); s_x1 = sem("p4_x1"); s_acc1 = sem("p4_acc1")
        s_xs1 = sem("p4_xs1"); s_tr = sem("p4_tr"); s_cp = sem("p4_cp"); s_G = sem("p4_G")
        s_PP = sem("p4_PP"); s_sg = sem("p4_sg"); s_t = sem("p4_t"); s_x2 = sem("p4_x2"); s_acc2 = sem("p4_acc2")
        s_o = sem("p4_o"); s_od = [sem("p4_od0"), sem("p4_od1")]

        s_p1 = sem("p4_p1"); s_p2 = sem("p4_p2"); s_gq = sem("p4_gq")
        mhalf = env["mhalf"]
        ms1 = sb("f_ms1", [128, 1], F32); ms2 = sb("f_ms2", [128, 1], F32)

        with nc.Block() as block:
            @block.gpsimd
            def _(g):
                def loads(T):
                    if T >= 2:
                        g.wait_ge(s_x1, T - 1)
                        g.wait_ge(s_xs1, T - 1)
                        g.wait_ge(s_A, T - 1)
                    i, u = T // 2, T % 2
                    r0 = 512 * i + 128 * u
                    g.dma_start(out=xo[T % 2][:], in_=xv[r0:r0 + 128, :]).then_inc(s_ld[T % 2], 16)
                    g.dma_start(out=pt[T % 2][:], in_=pown[T * 128:(T + 1) * 128, :]).then_inc(s_ld[T % 2], 16)
                    g.dma_start(out=mt[T % 2][:], in_=MT[:, T * 128:(T + 1) * 128].rearrange("(c p) t -> p c t", p=128)
                                ).then_inc(s_ld[T % 2], 16)

                gq = [0]

                def pow1(T):
                    g.wait_ge(s_acc1, T + 1)
                    g.tensor_scalar(out=ms1[:], in0=ssq1[:], scalar1=1.0 / D, scalar2=1e-6, op0=ALU.mult, op1=ALU.add
                                    ).then_inc(s_gq, 1)
                    gq[0] += 1
                    g.wait_ge(s_gq, gq[0])
                    g.tensor_tensor(out=r1[:], in0=ms1[:], in1=mhalf[:], op=ALU.pow).then_inc(s_p1, 1)

                def pow2(T):
                    g.wait_ge(s_acc2, T + 1)
                    g.tensor_scalar(out=ms2[:], in0=ssq2[:], scalar1=1.0 / D, scalar2=1e-6, op0=ALU.mult, op1=ALU.add
                                    ).then_inc(s_gq, 1)
                    gq[0] += 1
                    g.wait_ge(s_gq, gq[0])
                    g.tensor_tensor(out=r2[:], in0=ms2[:], in1=mhalf[:], op=ALU.pow).then_inc(s_p2, 1)

                loads(0); loads(1)
                pow1(0)
                for T in range(NT):
                    if T + 2 < NT:
                        loads(T + 2)
                    if T >= 1:
                        pow2(T - 1)
                    if T + 1 < NT:
                        pow1(T + 1)
                pow2(NT - 1)

            @block.tensor
            def _(t):
                def A(T):
                    t.wait_ge(s_ld[T % 2], 48 * (T // 2 + 1))
                    if T >= 1:
                        t.wait_ge(s_x1, T)
                    for half in range(2):
                        for c in range(8):
                            ins = t.matmul(Y[half][:, :], mt[T % 2][:, c, :], WO[:, c, half * 512:(half + 1) * 512],
                                           start=(c == 0), stop=(c == 7))
                    ins.then_inc(s_A, 1)

                def TR(T):
                    t.wait_ge(s_xs1, T + 1)
                    t.wait_ge(s_xs1d, T + 1)
                    if T >= 1:
                        t.wait_ge(s_cp, T)
                    for c in range(8):
                        t.transpose(out=TPx[:, c, :], in_=xs1[:, c * 128:(c + 1) * 128], identity=identb[:])
                    for c in range(2):
                        ins = t.transpose(out=TPp[:, c, :], in_=pb[:, c * 128:(c + 1) * 128], identity=identb[:])
                    ins.then_inc(s_tr, 1)

                def GM(T):
                    t.wait_ge(s_cp, T + 1)
                    if T >= 1:
                        t.wait_ge(s_t, T)
                    for half in range(2):
                        for c in range(8):
                            ins = t.matmul(G[half][:, :], xs1T[:, c, :], WG[:, c, half * 512:(half + 1) * 512],
                                           start=(c == 0), stop=(c == 7))
                    ins.then_inc(s_G, 1)
                    for half in range(2):
                        for c in range(2):
                            ins = t.matmul(PP[half][:, :], pT[:, c, :], WP[:, c, half * 512:(half + 1) * 512],
                                           start=(c == 0), stop=(c == 1))
                    ins.then_inc(s_PP, 1)

                A(0)
                for T in range(NT):
                    if T + 1 < NT:
                        A(T + 1)
                    TR(T)
                    GM(T)

            @block.vector
            def _(v):
                def x1f(T):
                    v.wait_ge(s_A, T + 1)
                    v.wait_ge(s_ld[T % 2], 48 * (T // 2 + 1))
                    for half in range(2):
                        ins = v.tensor_tensor(out=x1[T % 2][:, half * 512:(half + 1) * 512], in0=Y[half][:, :],
                                              in1=xo[T % 2][:, half * 512:(half + 1) * 512], op=ALU.add)
                    ins.then_inc(s_x1, 1)

                def xs1half(T):
                    v.wait_ge(s_p1, T + 1)
                    if T >= 1:
                        v.wait_ge(s_tr, T)
                    v.tensor_scalar(out=xs1[:, 512:1024], in0=x1[T % 2][:, 512:1024], scalar1=r1[:, 0:1], scalar2=None,
                                    op0=ALU.mult).then_inc(s_xs1d, 1)

                def copies(T):
                    v.wait_ge(s_tr, T + 1)
                    v.tensor_copy(out=xs1T[:], in_=TPx[:, :, :])
                    v.tensor_copy(out=pT[:], in_=TPp[:, :, :]).then_inc(s_cp, 1)

                def tail(T):
                    v.wait_ge(s_PP, T + 1)
                    v.wait_ge(s_sg, T + 1)
                    for half in range(2):
                        ins = v.tensor_tensor(out=tq[:, half * 512:(half + 1) * 512], in0=PP[half][:, :],
                                              in1=sg[:, half * 512:(half + 1) * 512], op=ALU.mult)
                    ins.then_inc(s_t, 1)
                    v.tensor_tensor(out=x2[:], in0=tq[:], in1=x1[T % 2][:], op=ALU.add).then_inc(s_x2, 1)

                def outf(T):
                    v.wait_ge(s_p2, T + 1)
                    if T >= 2:
                        v.wait_ge(s_od[T % 2], 16 * (T // 2))
                    v.scalar_tensor_tensor(out=ob[T % 2][:], in0=x2[:], scalar=r2[:, 0:1], in1=GF[:],
                                           op0=ALU.mult, op1=ALU.mult).then_inc(s_o, 1)

                x1f(0)
                xs1half(0)
                for T in range(NT):
                    if T >= 1:
                        tail(T - 1)
                    if T + 1 < NT:
                        x1f(T + 1)
                    copies(T)
                    if T >= 1:
                        outf(T - 1)
                    if T + 1 < NT:
                        xs1half(T + 1)
                tail(NT - 1)
                outf(NT - 1)

            @block.scalar
            def _(a):
                def n1(T):
                    a.wait_ge(s_x1, T + 1)
                    a.activation(out=junk[:], in_=x1[T % 2][:], func=AF.Square, accum_out=ssq1[:, 0:1]).then_inc(s_acc1, 1)
                    if T >= 1:
                        a.wait_ge(s_tr, T)
                    a.activation(out=pb[:], in_=pt[T % 2][:], func=AF.Copy)
                    a.wait_ge(s_p1, T + 1)
                    a.activation(out=xs1[:, 0:512], in_=x1[T % 2][:, 0:512], func=AF.Copy, scale=r1[:, 0:1]
                                 ).then_inc(s_xs1, 1)

                def sig(T):
                    a.wait_ge(s_G, T + 1)
                    if T >= 1:
                        a.wait_ge(s_t, T)
                    for half in range(2):
                        ins = a.activation(out=sg[:, half * 512:(half + 1) * 512], in_=G[half][:, :], func=AF.Sigmoid)
                    ins.then_inc(s_sg, 1)

                def n2(T):
                    a.wait_ge(s_x2, T + 1)
                    a.activation(out=junk[:], in_=x2[:], func=AF.Square, accum_out=ssq2[:, 0:1]).then_inc(s_acc2, 1)

                n1(0)
                for T in range(NT):
                    if T >= 1:
                        sig(T - 1)
                        n2(T - 1)
                    if T + 1 < NT:
                        n1(T + 1)
                sig(NT - 1)
                n2(NT - 1)

            @block.sync
            def _(sync):
                for T in range(NT):
                    sync.wait_ge(s_o, T + 1)
                    sync.dma_start(out=out_d[T * 128:(T + 1) * 128, :], in_=ob[T % 2][:]).then_inc(s_od[T % 2], 16)
                sync.wait_ge(s_od[0], 16 * (NT // 2))
                sync.wait_ge(s_od[1], 16 * (NT // 2))


def _rel_bucket(dist):
    n = np.maximum(dist, 0)
    max_exact = 16
    nf = np.maximum(n, 1).astype(np.float32)
    val = (np.log(nf / np.float32(max_exact)) / np.float32(np.log(1024.0 / 16.0))) * np.float32(16)
    large = max_exact + val.astype(np.int32)
    large = np.minimum(large, 31)
    return np.where(n < max_exact, n, large).astype(np.int64)


def _core_tables(r, rel_bias):
    tab = np.asarray(rel_bias, np.float32)
    k = np.arange(128)[:, None]
    q = np.arange(128)[None, :]
    d0 = q - k
    diag = np.where((d0 >= 0)[:, None, :], tab[_rel_bucket(d0)][:, :, :8].transpose(0, 2, 1), np.float32(NEG))
    d1 = 128 + q - k
    prev = np.where((d1 < 128)[:, None, :], tab[_rel_bucket(d1)][:, :, :8].transpose(0, 2, 1), np.float32(NEG))
    masked = np.full_like(prev, NEG)
    bswa = np.stack([diag, prev, prev if r == 0 else masked, masked if r == 0 else prev], axis=1)
    bm = np.full((8, 128, 10, 2, 256), NEG, np.float32)
    qq = np.arange(256)[None, :]
    for e in range(8):
        delta = (e + 2 * r - 1) if e % 2 == 0 else (e - 1)
        for ks in range(2):
            dist = delta * 256 + qq - (ks * 128 + k)
            vals = tab[_rel_bucket(dist)][:, :, 8:]
            vals = np.where((dist >= 0)[:, :, None], vals, np.float32(NEG))
            bm[:, :, e + 2, ks, :] = vals.transpose(2, 0, 1)
    cfar = np.broadcast_to(tab[31, 8:][None, :], (128, 8))
    pm = np.full((32, 32), NEGBIG, np.float32)
    oz = np.ones((32, 32), np.float32)
    for T in range(32):
        i = T // 2
        for vb in range(32):
            past = (vb <= 2 * i - 1) or (r == 1 and vb == 2 * i + 1)
            if past:
                pm[T, vb] = 0.0
        oz[T, 2 * i] = 0.0
    pm = np.broadcast_to(pm[None], (128, 32, 32))
    oz = np.broadcast_to(oz[None], (128, 32, 32))
    return dict(bswa=np.ascontiguousarray(bswa, np.float32), bmoba=np.ascontiguousarray(bm),
                cfar=np.ascontiguousarray(cfar, np.float32), pm=np.ascontiguousarray(pm),
                oz=np.ascontiguousarray(oz))


def _virt_perm(r):
    return np.array([2 * (v // 2) + ((v % 2) ^ r) for v in range(32)])


def make_in_maps(x, p, norm_in, w_in, sinks, rel_bias, w_out, ple_norm, w_ple_gate, w_ple_proj, final_norm):
    f = lambda a: np.ascontiguousarray(np.asarray(a, dtype=np.float32))
    x = f(x); p = f(p); w_in = f(w_in)[0]; w_out = f(w_out)[0]; w_gate = f(w_ple_gate)[0]; w_ple = f(w_ple_proj)[0]
    norm_in = f(norm_in)[0]; ple_norm = f(ple_norm)[0]; final_norm = f(final_norm); sinks = f(sinks)[0]
    onehot = np.zeros((32, S), ml_dtypes.bfloat16)
    for n in range(32):
        onehot[n, n * 256:(n + 1) * 256] = 1.0
    shared = dict(
        w_in=w_in, w_out=w_out, w_gate=w_gate, w_ple=w_ple,
        gin=np.ascontiguousarray(norm_in.reshape(8, 128).T), gple=np.ascontiguousarray(ple_norm.reshape(8, 128).T),
        gfin=np.ascontiguousarray(np.broadcast_to(final_norm[None, :], (128, D))),
        sinkb=np.ascontiguousarray(np.broadcast_to(sinks[None, :], (128, 8))),
        onehot=onehot,
    )
    tabs = [_core_tables(r, rel_bias) for r in range(2)]
    maps = []
    for c in range(8):
        b, r = c // 2, c % 2
        perm = _virt_perm(r)
        xb = x[b].reshape(32, 256, D)
        xvv = np.ascontiguousarray(xb[perm].reshape(S, D))
        own = np.array([2 * i + r for i in range(16)])
        pw = np.ascontiguousarray(p[0, b].reshape(32, 256, 256)[own].reshape(OWN, 256))
        m = dict(shared)
        m.update(tabs[r])
        m["xv"] = xvv
        m["pown"] = pw
        maps.append(m)
    return maps


_NC_CACHE = {}


def kernel(x, p, norm_in, w_in, sinks, rel_bias, w_out, ple_norm, w_ple_gate, w_ple_proj, final_norm):
    maps = make_in_maps(x, p, norm_in, w_in, sinks, rel_bias, w_out, ple_norm, w_ple_gate, w_ple_proj, final_norm)
    if "nc" not in _NC_CACHE:
        _NC_CACHE["nc"] = build_nc()
    nc = _NC_CACHE["nc"]
    res = run_bass_kernel_spmd(nc, maps, core_ids=list(range(8)))
    if DEBUG:
        return res
    out = np.empty((4, 32, 256, D), np.float32)
    for c in range(8):
        b, r = c // 2, c % 2
        o = np.asarray(res.results[c]["out"], np.float32).reshape(16, 256, D)
        for i in range(16):
            out[b, 2 * i + r] = o[i]
    return out.reshape(4, S, D)
```
